# Optimizing a Trainium2 kernel written in Bass

```python
import jax, jax.numpy as jnp
from jax import lax
import numpy as np


D_MODEL = 1024
BATCH = 16
SEQ = 2048
DEPTH = 1

CONV_CH = D_MODEL
CONV_WIDTH = 31
N_HEADS = 16
HEAD_DIM = 64
ATTN_DIM = N_HEADS * HEAD_DIM
DILATION_GROUPS = ((128, 1), (512, 4), (2048, 16))
ATTN_BLOCK = 128
D_FF = 2816
FFN_CONV_WIDTH = 3
EPS = 1e-6
SPLITS = (2 * CONV_CH, 2 * CONV_CH + ATTN_DIM, 2 * CONV_CH + 2 * ATTN_DIM, 2 * CONV_CH + 3 * ATTN_DIM)
IN_COLS = 2 * CONV_CH + 3 * ATTN_DIM + 2 * D_MODEL

kernel_name = "hybrid_conformer_conv_dilated_attn_block"


def rms_norm(x, g):
    xf = x.astype(jnp.float32)
    xf = xf * lax.rsqrt(jnp.mean(xf * xf, axis=-1, keepdims=True) + EPS)
    return (xf * g.astype(jnp.float32)).astype(x.dtype)


def causal_depthwise_conv(u, w, b):
    width, ch = w.shape
    out = lax.conv_general_dilated(
        u, w[:, None, :].astype(u.dtype), window_strides=(1,), padding=[(width - 1, 0)],
        dimension_numbers=('NWC', 'WIO', 'NWC'), feature_group_count=ch)
    return out + b.astype(u.dtype)


def alibi_slopes(n_heads):
    return 2.0 ** (-8.0 * jnp.arange(1, n_heads + 1, dtype=jnp.float32) / n_heads)


def dilated_window_attention(q, k, v, slopes, window, dilation):
    B, S, H, Dh = q.shape
    n_back = window // dilation
    L = S // dilation
    nb = -(-L // ATTN_BLOCK)
    Lp = nb * ATTN_BLOCK

    def to_sub(t):
        t = t.reshape(B, L, dilation, H, Dh).transpose(0, 2, 1, 3, 4).reshape(B * dilation, L, H, Dh)
        t = jnp.pad(t, ((0, 0), (0, Lp - L), (0, 0), (0, 0)))
        return t.reshape(B * dilation, nb, ATTN_BLOCK, H, Dh)

    def with_prev(t):
        prev = jnp.pad(t, ((0, 0), (1, 0), (0, 0), (0, 0), (0, 0)))[:, :-1]
        return jnp.concatenate([prev, t], axis=2)

    qb = to_sub(q)
    kc = with_prev(to_sub(k))
    vc = with_prev(to_sub(v))

    scores = jnp.einsum('nbqhd,nbkhd->nbhqk', qb, kc)
    steps = (jnp.arange(ATTN_BLOCK)[:, None] + ATTN_BLOCK) - jnp.arange(2 * ATTN_BLOCK)[None, :]
    valid = (steps >= 0) & (steps <= n_back)
    first_block = (jnp.arange(nb) == 0)[:, None, None]
    prev_cols = (jnp.arange(2 * ATTN_BLOCK) < ATTN_BLOCK)[None, None, :]
    valid = valid[None] & ~(first_block & prev_cols)
    dist = (steps * dilation).astype(jnp.float32)
    scores = scores - slopes[:, None, None] * dist
    scores = jnp.where(valid[None, :, None], scores, -jnp.inf)
    m = jnp.max(scores, axis=-1, keepdims=True)
    p = jnp.exp(scores - m)
    den = jnp.sum(p, axis=-1)
    o = jnp.einsum('nbhqk,nbkhd->nbqhd', p, vc) / jnp.swapaxes(den, 2, 3)[..., None]
    lse = jnp.swapaxes(m[..., 0] + jnp.log(den), 2, 3)

    def from_sub(t):
        rest = t.shape[3:]
        t = t.reshape(B, dilation, Lp, *rest)[:, :, :L]
        return jnp.swapaxes(t, 1, 2).reshape(B, S, *rest)

    return from_sub(o), from_sub(lse)


def setup_inputs(seed: int = 0) -> dict:
    key = jax.random.key(seed)
    ks = jax.random.split(key, 18)
    f32 = jnp.float32

    def nrm(k, shape, scale):
        return jax.random.normal(k, shape, f32) * scale

    return {
        'x': nrm(ks[0], (BATCH, SEQ, D_MODEL), 1.0),
        'norm1_g': 1.0 + nrm(ks[1], (DEPTH, D_MODEL), 0.1),
        'w_in': nrm(ks[2], (DEPTH, D_MODEL, IN_COLS), D_MODEL ** -0.5),
        'gate_b': nrm(ks[3], (DEPTH, 2 * D_MODEL), 0.02),
        'conv_w': nrm(ks[4], (DEPTH, CONV_WIDTH, CONV_CH), CONV_WIDTH ** -0.5),
        'conv_b': nrm(ks[5], (DEPTH, CONV_CH), 0.02),
        'conv_norm_g': 1.0 + nrm(ks[6], (DEPTH, CONV_CH), 0.1),
        'w_conv_out': nrm(ks[7], (DEPTH, CONV_CH, D_MODEL), CONV_CH ** -0.5),
        'q_norm_g': 1.0 + nrm(ks[8], (DEPTH, HEAD_DIM), 0.1),
        'k_norm_g': 1.0 + nrm(ks[9], (DEPTH, HEAD_DIM), 0.1),
        'w_attn_out': nrm(ks[10], (DEPTH, ATTN_DIM, D_MODEL), ATTN_DIM ** -0.5),
        'w_out': nrm(ks[11], (DEPTH, D_MODEL, D_MODEL), D_MODEL ** -0.5),
        'norm2_g': 1.0 + nrm(ks[12], (DEPTH, D_MODEL), 0.1),
        'w_up': nrm(ks[13], (DEPTH, D_MODEL, 2 * D_FF), D_MODEL ** -0.5),
        'ffn_conv_w': nrm(ks[14], (DEPTH, FFN_CONV_WIDTH, 2 * D_FF), FFN_CONV_WIDTH ** -0.5),
        'ffn_conv_b': nrm(ks[15], (DEPTH, 2 * D_FF), 0.02),
        'w_down': nrm(ks[16], (DEPTH, D_FF, D_MODEL), D_FF ** -0.5),
    }


def reference(x, norm1_g, w_in, gate_b, conv_w, conv_b, conv_norm_g, w_conv_out,
              q_norm_g, k_norm_g, w_attn_out, w_out, norm2_g, w_up, ffn_conv_w,
              ffn_conv_b, w_down):
    B, S, _ = x.shape
    slopes = alibi_slopes(N_HEADS)
    for l in range(DEPTH):
        h = rms_norm(x, norm1_g[l])
        z = h @ w_in[l].astype(h.dtype)
        glu_in, q, k, v, gate_logits = jnp.split(z, SPLITS, axis=-1)

        a_val, a_gate = jnp.split(glu_in, 2, axis=-1)
        a = a_val * jax.nn.sigmoid(a_gate)
        a = causal_depthwise_conv(a, conv_w[l], conv_b[l])
        a = jax.nn.silu(rms_norm(a, conv_norm_g[l]))
        y_a = a @ w_conv_out[l].astype(a.dtype)

        q = rms_norm(q.reshape(B, S, N_HEADS, HEAD_DIM), q_norm_g[l]).astype(jnp.float32) * HEAD_DIM ** -0.5
        k = rms_norm(k.reshape(B, S, N_HEADS, HEAD_DIM), k_norm_g[l]).astype(jnp.float32)
        v = v.reshape(B, S, N_HEADS, HEAD_DIM).astype(jnp.float32)
        outs, lses = [], []
        for window, dilation in DILATION_GROUPS:
            o_g, lse_g = dilated_window_attention(q, k, v, slopes, window, dilation)
            outs.append(o_g)
            lses.append(lse_g)
        mix = jax.nn.softmax(jnp.stack(lses), axis=0)
        o = jnp.einsum('gbsh,gbshd->bshd', mix, jnp.stack(outs))
        o = o.astype(x.dtype).reshape(B, S, ATTN_DIM)
        y_b = o @ w_attn_out[l].astype(o.dtype)

        g = jax.nn.sigmoid(gate_logits + gate_b[l].astype(gate_logits.dtype))
        g_a, g_b = jnp.split(g, 2, axis=-1)
        x = x + (g_a * y_a + g_b * y_b) @ w_out[l].astype(x.dtype)

        h = rms_norm(x, norm2_g[l])
        u = causal_depthwise_conv(h @ w_up[l].astype(h.dtype), ffn_conv_w[l], ffn_conv_b[l])
        u_val, u_gate = jnp.split(u, 2, axis=-1)
        x = x + (jax.nn.silu(u_gate) * u_val) @ w_down[l].astype(x.dtype)
    return x
```

```python
import numpy as np
import concourse.bass as bass
import concourse.mybir as mybir
from concourse.bass_utils import run_bass_kernel_spmd
from contextlib import ExitStack

F32 = mybir.dt.float32
BF16 = mybir.dt.bfloat16
ALU = mybir.AluOpType
AF = mybir.ActivationFunctionType

S = 2048
D = 1024
NH = 16
DFF = 2816
IN_COLS = 7168
EPS = 1e-6
NCORES = 8
SEQ_PER_CORE = 2
NFC = 2 * DFF // 128
ND = 10
KB = 1024

V_GATEB = 0
V_CONVB = 16
V_CNG = 24
V_FCB = 32
V_CW = 76
V_FCW = V_CW + 8 * 31
V_QG = V_FCW + NFC * 3
V_KG = V_QG + 1
NV = V_KG + 1

C_ID, C_BLK, C_ST, C_MK, NCST = 0, 128, 256, 512, 768


class Buf:
    __slots__ = ("ap", "w", "r")

    def __init__(self, ap):
        self.ap = ap
        self.w = None
        self.r = {}


class Sched:
    def __init__(self, nc, stack):
        self.nc = nc
        self.eng = dict(pe=nc.tensor, act=nc.scalar, dve=nc.vector, pool=nc.gpsimd, sp=nc.sync)
        self.msem = {e: stack.enter_context(nc.semaphore("ms_" + e)) for e in ("pe", "act", "dve", "pool")}
        self.mcnt = {e: 0 for e in self.msem}
        self.dsem = {q: [stack.enter_context(nc.semaphore("d_%s_%d" % (q, i))) for i in range(ND)] for q in ("sp", "pool")}
        self.dcnt = {q: [0] * ND for q in self.dsem}
        self.dnext = {q: 0 for q in self.dsem}
        self.seen = {e: {} for e in self.eng}
        self.pending = {e: False for e in self.msem}

    def _wait(self, e, ev):
        sem, val, key, prod = ev
        if self.seen[e].get(key, 0) >= val:
            return
        self.eng[e].wait_ge(sem, val)
        self.seen[e][key] = val

    def _deps(self, e, reads, writes):
        for b in reads:
            if b.w is not None:
                if b.w[3] == e and e == "pe":
                    continue
                self._wait(e, b.w)
        for b in writes:
            if b.w is not None and not (b.w[3] == e and e == "pe"):
                self._wait(e, b.w)
            for ev in b.r.values():
                if not (ev[3] == e and e == "pe"):
                    self._wait(e, ev)

    def _record(self, ev, reads, writes):
        for b in reads:
            b.r[ev[2]] = ev
        for b in writes:
            b.w = ev
            b.r = {}

    def op(self, e, fn, reads=(), writes=(), signal=True):
        self._deps(e, reads, writes)
        ins = fn(self.eng[e])
        if signal:
            self.mcnt[e] += 1
            ins.then_inc(self.msem[e], 1)
            ev = (self.msem[e], self.mcnt[e], e, e)
            self.pending[e] = False
        else:
            ev = (self.msem[e], self.mcnt[e] + 1, e, e)
            self.pending[e] = True
        self._record(ev, reads, writes)
        return ins

    def dma(self, q, out_ap, in_ap, reads=(), writes=()):
        i = self.dnext[q]
        self.dnext[q] = (i + 1) % ND
        sem = self.dsem[q][i]
        key = "d_%s_%d" % (q, i)
        if self.dcnt[q][i] > 0:
            self._wait(q, (sem, self.dcnt[q][i], key, "dma"))
        self._deps(q, reads, writes)
        self.eng[q].dma_start(out=out_ap, in_=in_ap).then_inc(sem, 16)
        self.dcnt[q][i] += 16
        ev = (sem, self.dcnt[q][i], key, "dma")
        self._record(ev, reads, writes)

    def all_events(self):
        evs = []
        for e in self.msem:
            assert not self.pending[e], e
            if self.mcnt[e] > 0:
                evs.append((self.msem[e], self.mcnt[e], e, e))
        for q in self.dsem:
            for i in range(ND):
                if self.dcnt[q][i] > 0:
                    evs.append((self.dsem[q][i], self.dcnt[q][i], "d_%s_%d" % (q, i), "dma"))
        return evs

    def barrier(self, engines=("pe", "act", "dve", "pool", "sp")):
        evs = self.all_events()
        for e in engines:
            for ev in evs:
                self._wait(e, ev)


def build_nc(nseq=SEQ_PER_CORE, dbg=False, stop_after=None):
    nc = bass.Bass("TRN2", target_bir_lowering=False)
    dt = nc.dram_tensor
    x_d = dt("x", [nseq * S, D], F32, kind="ExternalInput").ap()
    w_in = dt("w_in", [D, IN_COLS], F32, kind="ExternalInput").ap()
    w_co = dt("w_conv_out", [D, D], F32, kind="ExternalInput").ap()
    w_ao = dt("w_attn_out", [D, D], F32, kind="ExternalInput").ap()
    w_out = dt("w_out", [D, D], F32, kind="ExternalInput").ap()
    w_up = dt("w_up", [D, 2 * DFF], F32, kind="ExternalInput").ap()
    w_dn = dt("w_down", [DFF, D], F32, kind="ExternalInput").ap()
    vecs_d = dt("vecs", [128, NV], F32, kind="ExternalInput").ap()
    rows_d = dt("rows", [128, 2 * D], F32, kind="ExternalInput").ap()
    cst_d = dt("consts", [128, NCST], F32, kind="ExternalInput").ap()
    out_d = dt("out", [nseq * S, D], F32, kind="ExternalOutput").ap()
    vscr = dt("vscr", [S, 8, 256], BF16, kind="Internal").ap()
    x1scr = dt("x1scr", [S, D], F32, kind="Internal").ap()
    dbg_d = {}
    if dbg:
        for nm, shp, ty in (("d_h1T", [128, 8, S], BF16), ("d_qT", [128, S], BF16), ("d_kT", [128, S], BF16),
                            ("d_oT", [128, 8, S], BF16), ("d_saT", [128, 8, S], BF16), ("d_mixT", [128, 8, S], BF16),
                            ("d_h2T", [128, 8, S], BF16), ("d_gT", [128, 22, 1024], BF16)):
            dbg_d[nm] = dt(nm, shp, ty, kind="ExternalOutput").ap()

    with ExitStack() as top:
        sc = Sched(nc, top)
        op, dma = sc.op, sc.dma
        base0 = nc.sbuf_base
        base = ((base0 + 63) // 64) * 64
        ARENA = 204 * KB
        top.enter_context(nc.sbuf_tensor("arena", [128, (ARENA + base - base0 + 64) // 2], BF16))
        uid = [0]

        def at(off, shape, dtype):
            uid[0] += 1
            n = 1
            for d_ in shape[1:]:
                n *= d_
            nbytes = n * (4 if dtype == F32 else 2)
            assert off % 32 == 0 and off + nbytes <= ARENA, (off, nbytes)
            h = nc.alloc_sbuf_tensor_at("t%d" % uid[0], list(shape), dtype, offset=base + off)
            return Buf(h.ap()), off + ((nbytes + 31) // 32) * 32

        class Region:
            def __init__(self, off):
                self.off = off

            def get(self, shape, dtype):
                b, self.off = at(self.off, shape, dtype)
                return b

        G = Region(0)
        cst = G.get([128, NCST], F32)
        vecs = G.get([128, NV], F32)
        rows = G.get([128, 2 * D], F32)
        ident = G.get([128, 128], BF16)
        blk = G.get([128, 128], BF16)
        ones = G.get([128, 128], BF16)
        NRING = 3
        ring = [G.get([128, 8, 512], BF16) for i in range(NRING)]
        ring_i = [0]
        R0 = G.off
        psb = [Buf(top.enter_context(nc.psum_tensor("ps%d" % i, [128, 512], F32)).ap()) for i in range(7)]
        pst = Buf(top.enter_context(nc.psum_tensor("pst", [128, 1024], BF16)).ap())

        def V(col, n=1):
            return vecs.ap[:, col:col + n]

        dma("sp", cst.ap, cst_d, writes=[cst])
        dma("sp", vecs.ap, vecs_d, writes=[vecs])
        dma("sp", rows.ap, rows_d, writes=[rows])
        op("dve", lambda e: e.tensor_copy(out=ident.ap, in_=cst.ap[:, C_ID:C_ID + 128]), [cst], [ident])
        op("dve", lambda e: e.tensor_copy(out=blk.ap, in_=cst.ap[:, C_BLK:C_BLK + 128]), [cst], [blk])
        op("dve", lambda e: e.memset(ones.ap, 1.0), [], [ones])
        op("dve", lambda e: e.tensor_scalar(out=V(V_KG), in0=V(V_KG), scalar1=8.0, scalar2=None, op0=ALU.mult), [vecs], [vecs])

        def eidx(h, gi):
            return 16 - (h + 1) + 4 * gi

        def load_w(src_ap, ncols=512, kchunks=8, slot=None, col0=0):
            if slot is None:
                slot = ring[ring_i[0] % NRING]
                ring_i[0] += 1
            dma("pool", slot.ap[:, 0:kchunks, col0:col0 + ncols], src_ap.rearrange("(k p) c -> p k c", p=128), writes=[slot])
            return slot

        def mm_group(out_ap, out_buf, pairs, reads, start=True, sig=True):
            n = len(pairs)
            for i, (l, r) in enumerate(pairs):
                op("pe", lambda e: e.matmul(out_ap, lhsT=l, rhs=r, start=(start and i == 0), stop=(i == n - 1), skip_group_check=True),
                   reads, [out_buf], signal=(sig and i == n - 1))

        rrc = {"p4": 0, "st": 0, "pb": 0, "o": 0, "pq": 0, "b1": 0}

        def nxt(key, n):
            v = rrc[key] % n
            rrc[key] += 1
            return v

        def rms_stage1(src_dram_rows, xb, junk, sq):
            if src_dram_rows is not None:
                dma("sp", xb.ap, src_dram_rows, writes=[xb])
            op("act", lambda e: e.activation(out=junk.ap, in_=xb.ap, func=AF.Square, accum_out=sq.ap[:, 0:1]), [xb], [junk, sq])
            op("dve", lambda e: e.tensor_scalar(out=sq.ap[:, 1:2], in0=sq.ap[:, 0:1], scalar1=1.0 / D, scalar2=EPS, op0=ALU.mult, op1=ALU.add), [sq], [sq])
            op("act", lambda e: e.activation(out=sq.ap[:, 1:2], in_=sq.ap[:, 1:2], func=AF.Ln), [sq], [sq])
            op("act", lambda e: e.activation(out=sq.ap[:, 1:2], in_=sq.ap[:, 1:2], func=AF.Exp, scale=-0.5), [sq], [sq])

        def rms_stage2(xb, sq, xnb, grow_ap, dstT, t):
            op("dve", lambda e: e.scalar_tensor_tensor(out=xnb.ap, in0=xb.ap, scalar=sq.ap[:, 1:2], in1=grow_ap, op0=ALU.mult, op1=ALU.mult), [xb, sq, rows], [xnb])
            for k in range(8):
                op("pe", lambda e: e.transpose(out=pst.ap[:, 128 * k:128 * k + 128], in_=xnb.ap[:, 128 * k:128 * k + 128], identity=ident.ap),
                   [xnb, ident], [pst], signal=(k == 7))
            op("act", lambda e: e.activation(out=dstT.ap[:, :, 128 * t:128 * t + 128], in_=pst.ap.rearrange("p (k t) -> p k t", k=8), func=AF.Copy), [pst], [dstT])

        pre = {}
        for s in range(nseq):
            xs = x_d[s * S:(s + 1) * S, :]
            outs = out_d[s * S:(s + 1) * S, :]
            stop_now = False
            skip_rest = False
            for _once in ([] if stop_after == "conly" else [0]):
                L = Region(R0)
                h1T = L.get([128, 8, S], BF16)
                oT = L.get([128, 8, S], BF16)
                R1 = L.off
                T = Region(R1)
                xt = [T.get([128, D], F32) for i in range(3)]
                junk = T.get([128, D], BF16)
                xn = [T.get([128, D], BF16) for i in range(2)]
                ssq = [T.get([128, 8], F32) for i in range(3)]
                vst = [T.get([128, 8, 256], BF16) for i in range(2)]
                for i in range(2):
                    op("dve", lambda e: e.memset(vst[i].ap, 1.0), [], [vst[i]])
                wv = pre.pop("wv", None) or [load_w(w_in[:, 4096 + 512 * cb:4096 + 512 * cb + 512]) for cb in range(2)]
                for tq in range(17):
                    if tq < 16:
                        rms_stage1(xs[128 * tq:128 * tq + 128, :], xt[tq % 3], junk, ssq[tq % 3])
                    if tq < 1:
                        continue
                    t = tq - 1
                    rms_stage2(xt[t % 3], ssq[t % 3], xn[t % 2], rows.ap[:, 0:D], h1T, t)
                    vb = vst[t % 2]
                    for cb in range(2):
                        pb_ = psb[nxt("p4", 4)]
                        mm_group(pb_.ap, pb_, [(h1T.ap[:, k, 128 * t:128 * t + 128], wv[cb].ap[:, k, :]) for k in range(8)], [h1T, wv[cb]])
                        pv4 = pb_.ap.rearrange("p (h c) -> p h c", h=4)
                        op("act", lambda e: e.activation(out=vb.ap[:, 4 * cb:4 * cb + 4, 0:64], in_=pv4[:, :, 0:64], func=AF.Copy), [pb_], [vb])
                        op("dve", lambda e: e.tensor_copy(out=vb.ap[:, 4 * cb:4 * cb + 4, 192:256], in_=pv4[:, :, 64:128]), [pb_], [vb])
                    dma("sp", vscr[128 * t:128 * t + 128, :, :], vb.ap, reads=[vb])
                if dbg:
                    dma("sp", dbg_d["d_h1T"], h1T.ap, reads=[h1T])
                if stop_after not in ("v", "v+"):
                    pre["wq"] = load_w(w_in[:, 2048:2048 + 512])
                    pre["wk"] = load_w(w_in[:, 3072:3072 + 512])
                sc.barrier()
                if stop_after == "v":
                    stop_now = True
                    break
                if stop_after == "v+":
                    skip_rest = True
                    break
                T = Region(R1)
                etab = T.get([128, 24, 256], BF16)
                etmp = T.get([128, 256], F32)
                qT = [T.get([128, S], BF16) for i in range(2)]
                kT = [T.get([128, S], BF16) for i in range(2)]
                sqb = [T.get([128, 512], BF16) for i in range(2)]
                Rb = [T.get([128, 512], F32) for i in range(2)]
                varr = [[T.get([128, 16, 256], BF16) for g in range(3)] for i in range(2)]
                P3s = [T.get([128, S], BF16) for i in range(2)]
                Pb = [T.get([128, 512], BF16) for i in range(8)]
                rden = T.get([128, 512], F32)
                for ei in range(24):
                    c = 2.0 ** (ei / 2.0 - 8.0)
                    op("act", lambda e: e.activation(out=etmp.ap, in_=cst.ap[:, C_ST:C_ST + 256], func=AF.Exp, scale=-c), [cst], [etmp])
                    op("dve", lambda e: e.tensor_tensor(out=etab.ap[:, ei, :], in0=etmp.ap, in1=cst.ap[:, C_MK:C_MK + 256], op=ALU.mult), [etmp, cst], [etab])
                wq = wk = None
                pending = []

                def tick():
                    for p_ in pending:
                        p_[0] -= 1
                    while pending and pending[0][0] <= 0:
                        pending.pop(0)[1]()

                def flush():
                    while pending:
                        pending.pop(0)[1]()

                for hp in range(8):
                    if hp == 0:
                        wq, wk = pre.pop("wq"), pre.pop("wk")
                    elif hp % 4 == 0:
                        wq = load_w(w_in[:, 2048 + 128 * hp:2048 + 128 * hp + 512])
                        wk = load_w(w_in[:, 3072 + 128 * hp:3072 + 128 * hp + 512])
                    va = varr[hp % 2]
                    for g, r in enumerate((1, 4, 16)):
                        if r == 1:
                            dma("sp", va[g].ap, vscr.rearrange("(b i) h c -> i b h c", i=128)[:, :, hp, :], writes=[va[g]])
                        else:
                            for rho in range(r):
                                nb = 16 // r
                                src = vscr.rearrange("(b i r) h c -> r i b h c", r=r, i=128)[rho, :, :, hp, :]
                                dma("sp", va[g].ap[:, rho * nb:(rho + 1) * nb, :], src, writes=[va[g]])
                    qb, kb = qT[hp % 2], kT[hp % 2]
                    coff = 128 * (hp % 4)
                    for (wt_, dstb, gcol) in ((wq, qb, V_QG), (wk, kb, V_KG)):
                        for tt in range(4):
                            pa_ = psb[4 + nxt("pq", 2)]
                            sq_, R_ = sqb[tt % 2], Rb[tt % 2]
                            mm_group(pa_.ap, pa_, [(wt_.ap[:, k, coff:coff + 128], h1T.ap[:, k, 512 * tt:512 * tt + 512]) for k in range(8)], [wt_, h1T])
                            op("act", lambda e: e.activation(out=sq_.ap, in_=pa_.ap, func=AF.Square), [pa_], [sq_])
                            p6 = psb[6]
                            mm_group(p6.ap, p6, [(blk.ap, sq_.ap)], [blk, sq_])
                            op("dve", lambda e: e.tensor_scalar(out=R_.ap, in0=p6.ap, scalar1=64.0 * EPS, scalar2=None, op0=ALU.add), [p6], [R_])
                            op("act", lambda e: e.activation(out=R_.ap, in_=R_.ap, func=AF.Ln), [R_], [R_])
                            op("act", lambda e: e.activation(out=R_.ap, in_=R_.ap, func=AF.Exp, scale=-0.5), [R_], [R_])
                            op("dve", lambda e: e.scalar_tensor_tensor(out=dstb.ap[:, 512 * tt:512 * tt + 512], in0=pa_.ap, scalar=V(gcol), in1=R_.ap, op0=ALU.mult, op1=ALU.mult),
                               [pa_, vecs, R_], [dstb])
                    if dbg and hp == 0:
                        dma("sp", dbg_d["d_qT"], qb.ap, reads=[qb])
                        dma("sp", dbg_d["d_kT"], kb.ap, reads=[kb])
                    for e_ in range(2):
                        h = 2 * hp + e_
                        p0 = 64 * e_
                        qh = qb.ap[p0:p0 + 64, :]
                        kh = kb.ap[p0:p0 + 64, :]
                        P3 = P3s[h % 2]

                        def stile(dst_ap, dst_buf, kcols, qcols, sig, kh=kh, qh=qh, kb=kb, qb=qb):
                            op("pe", lambda e: e.matmul(dst_ap, lhsT=kh[:, kcols], rhs=qh[:, qcols], start=True, stop=True, skip_group_check=True),
                               [kb, qb], [dst_buf], signal=sig)

                        def make_pv(units, c, e_=e_, p0=p0, hp=hp, va=va):
                            def emit():
                                ob = psb[2 + nxt("o", 2)]
                                nu = len(units)
                                for ui, (pbuf, p_ap, gi, tile_i, ocols) in enumerate(units):
                                    op("pe", lambda e: e.matmul(ob.ap[:, ocols], lhsT=va[gi].ap[:, tile_i, 128 * e_:128 * e_ + 128], rhs=p_ap,
                                                                start=(ui == 0), stop=(ui == nu - 1), skip_group_check=True),
                                       [va[gi], pbuf], [ob], signal=(ui == nu - 1))
                                dq = 64 * (1 - e_)
                                op("act", lambda e: e.activation(out=rden.ap[p0:p0 + 64, :], in_=ob.ap[dq:dq + 64, :], func=AF.Ln), [ob], [rden])
                                op("act", lambda e: e.activation(out=rden.ap[p0:p0 + 64, :], in_=rden.ap[p0:p0 + 64, :], func=AF.Exp, scale=-1.0), [rden], [rden])
                                op("dve", lambda e: e.tensor_tensor(out=oT.ap[p0:p0 + 64, hp, 512 * c:512 * c + 512], in0=ob.ap[p0:p0 + 64, :], in1=rden.ap[p0:p0 + 64, :], op=ALU.mult),
                                   [ob, rden], [oT])
                            return emit

                        e3 = eidx(h, 2)
                        for ch in range(4):
                            stb = psb[(0, 1, 6)[nxt("st", 3)]]
                            for rr in range(4):
                                rho = 4 * ch + rr
                                stile(stb.ap[:, 128 * rr:128 * rr + 128], stb, slice(rho, S, 16), slice(rho, S, 16), rr == 3)
                            pv = P3.ap[:, 512 * ch:512 * ch + 512]
                            pv3 = pv.rearrange("p (a b) -> p a b", a=4)
                            e_ap = etab.ap[:, e3:e3 + 1, 128:256].to_broadcast([128, 4, 128])
                            op("act", lambda e: e.activation(out=pv, in_=stb.ap, func=AF.Exp), [stb], [P3])
                            op("dve", lambda e: e.tensor_tensor(out=pv3, in0=pv3, in1=e_ap, op=ALU.mult), [P3, etab], [P3])
                            tick()
                        for c in range(4):
                            units = []
                            for gi, r in ((0, 1), (1, 4)):
                                ei = eidx(h, gi)
                                for half in range(2):
                                    stb = psb[(0, 1, 6)[nxt("st", 3)]]
                                    pbuf = Pb[nxt("pb", 8)]
                                    for jj in range(2):
                                        if gi == 0:
                                            b = 4 * c + 2 * half + jj
                                            has_prev = b > 0
                                            kprev = slice(128 * (b - 1), 128 * b)
                                            kown = slice(128 * b, 128 * b + 128)
                                            ocols = slice(128 * (b - 4 * c), 128 * (b - 4 * c) + 128)
                                            tprev, town = b - 1, b
                                        else:
                                            rho = 2 * half + jj
                                            has_prev = c > 0
                                            kprev = slice(rho + 512 * (c - 1), 512 * c, 4)
                                            kown = slice(rho + 512 * c, 512 * c + 512, 4)
                                            ocols = slice(rho, 512, 4)
                                            tprev, town = rho * 4 + c - 1, rho * 4 + c
                                        o0 = 256 * jj
                                        if has_prev:
                                            stile(stb.ap[:, o0:o0 + 128], stb, kprev, kown, False)
                                            units.append((pbuf, pbuf.ap[:, o0:o0 + 128], gi, tprev, ocols))
                                        stile(stb.ap[:, o0 + 128:o0 + 256], stb, kown, kown, jj == 1)
                                        units.append((pbuf, pbuf.ap[:, o0 + 128:o0 + 256], gi, town, ocols))
                                    e_ap = etab.ap[:, ei:ei + 1, :].to_broadcast([128, 2, 256])
                                    pb3 = pbuf.ap.rearrange("p (a b) -> p a b", a=2)
                                    op("act", lambda e: e.activation(out=pbuf.ap, in_=stb.ap, func=AF.Exp), [stb], [pbuf])
                                    op("dve", lambda e: e.tensor_tensor(out=pb3, in0=pb3, in1=e_ap, op=ALU.mult), [pbuf, etab], [pbuf])
                                    tick()
                            for rho in range(16):
                                units.append((P3, P3.ap[:, 128 * rho + 32 * c:128 * rho + 32 * c + 32], 2, rho, slice(rho, 512, 16)))
                            pending.append([2, make_pv(units, c)])
                    flush()
                if dbg:
                    dma("sp", dbg_d["d_oT"], oT.ap, reads=[oT])
                if stop_after not in ("attn", "attn+"):
                    pre["wa"] = load_w(w_in[:, 0:512])
                    pre["wg"] = load_w(w_in[:, 1024:1024 + 512])
                sc.barrier()
                if stop_after == "attn":
                    stop_now = True
                    break
                if stop_after == "attn+":
                    skip_rest = True
                    break
                T = Region(R1)
                aT = T.get([128, 8, 32 + S], BF16)
                R2 = T.off
                cvT = T.get([128, 8, S], BF16)
                dgb = T.get([128, 31, 128], BF16)
                sg = [T.get([128, 512], F32) for i in range(2)]
                sgm = [T.get([128, 512], F32) for i in range(2)]
                sqv = [T.get([128, 512], BF16) for i in range(2)]
                rr_ = T.get([128, S], F32)
                gsb = [T.get([128, 512], F32) for i in range(2)]
                t1 = [T.get([128, 512], F32) for i in range(2)]
                AO = 32
                op("dve", lambda e: e.memset(aT.ap[:, :, 0:AO], 0.0), [], [aT])
                for cc in range(8):
                    if cc == 0:
                        wa, wg = pre.pop("wa"), pre.pop("wg")
                    elif cc % 4 == 0:
                        wa = load_w(w_in[:, 128 * cc:128 * cc + 512])
                        wg = load_w(w_in[:, 1024 + 128 * cc:1024 + 128 * cc + 512])
                    coff = 128 * (cc % 4)
                    for j in range(31):
                        op("dve", lambda e: e.tensor_scalar(out=dgb.ap[:, j, :], in0=ident.ap, scalar1=V(V_CW + cc * 31 + j), scalar2=None, op0=ALU.mult),
                           [ident, vecs], [dgb])
                    for tt in range(4):
                        pg_ = psb[4 + nxt("b1", 3)]
                        pv_ = psb[4 + nxt("b1", 3)]
                        mm_group(pg_.ap, pg_, [(wg.ap[:, k, coff:coff + 128], h1T.ap[:, k, 512 * tt:512 * tt + 512]) for k in range(8)], [wg, h1T])
                        mm_group(pv_.ap, pv_, [(wa.ap[:, k, coff:coff + 128], h1T.ap[:, k, 512 * tt:512 * tt + 512]) for k in range(8)], [wa, h1T])
                        sg_ = sg[tt % 2]
                        op("act", lambda e: e.activation(out=sg_.ap, in_=pg_.ap, func=AF.Sigmoid), [pg_], [sg_])
                        op("dve", lambda e: e.tensor_tensor(out=aT.ap[:, cc, AO + 512 * tt:AO + 512 * tt + 512], in0=pv_.ap, in1=sg_.ap, op=ALU.mult), [pv_, sg_], [aT])
                    for tt in range(4):
                        pc_ = psb[4 + nxt("b1", 3)]
                        mm_group(pc_.ap, pc_, [(dgb.ap[:, j, :], aT.ap[:, cc, AO - 30 + 512 * tt + j:AO - 30 + 512 * tt + j + 512]) for j in range(31)], [dgb, aT])
                        op("act", lambda e: e.activation(out=cvT.ap[:, cc, 512 * tt:512 * tt + 512], in_=pc_.ap, func=AF.Identity, bias=V(V_CONVB + cc)), [pc_, vecs], [cvT])
                        sv = sqv[tt % 2]
                        op("act", lambda e: e.activation(out=sv.ap, in_=pc_.ap, func=AF.Square, bias=V(V_CONVB + cc)), [pc_, vecs], [sv])
                        op("pe", lambda e: e.matmul(psb[tt].ap, lhsT=ones.ap, rhs=sv.ap, start=(cc == 0), stop=(cc == 7), skip_group_check=True),
                           [ones, sv], [psb[tt]], signal=True)
                for tt in range(4):
                    op("dve", lambda e: e.tensor_scalar(out=rr_.ap[:, 512 * tt:512 * tt + 512], in0=psb[tt].ap, scalar1=1.0 / D, scalar2=EPS, op0=ALU.mult, op1=ALU.add), [psb[tt]], [rr_])
                op("act", lambda e: e.activation(out=rr_.ap, in_=rr_.ap, func=AF.Ln), [rr_], [rr_])
                op("act", lambda e: e.activation(out=rr_.ap, in_=rr_.ap, func=AF.Exp, scale=-0.5), [rr_], [rr_])
                def silu_unit(cc, tt):
                    sg_, sm_ = sg[tt % 2], sgm[tt % 2]
                    cs = slice(512 * tt, 512 * tt + 512)
                    op("dve", lambda e: e.scalar_tensor_tensor(out=sg_.ap, in0=cvT.ap[:, cc, cs], scalar=V(V_CNG + cc), in1=rr_.ap[:, cs], op0=ALU.mult, op1=ALU.mult),
                       [cvT, vecs, rr_], [sg_])
                    op("act", lambda e: e.activation(out=sm_.ap, in_=sg_.ap, func=AF.Sigmoid), [sg_], [sm_])
                    op("dve", lambda e: e.tensor_tensor(out=cvT.ap[:, cc, cs], in0=sg_.ap, in1=sm_.ap, op=ALU.mult), [sg_, sm_], [cvT])

                silu_units = [(cc, tt) for cc in range(8) for tt in range(4)]
                mixT = aT
                for dc in range(8):
                    wsl = ring[ring_i[0] % NRING]
                    ring_i[0] += 1
                    load_w(w_in[:, 6144 + 128 * dc:6144 + 128 * dc + 128], ncols=128, slot=wsl, col0=0)
                    load_w(w_ao[:, 128 * dc:128 * dc + 128], ncols=128, slot=wsl, col0=128)
                    for tt in range(4):
                        cs = slice(512 * tt, 512 * tt + 512)
                        pg_ = psb[nxt("p4", 4)]
                        py_ = psb[nxt("p4", 4)]
                        mm_group(pg_.ap, pg_, [(wsl.ap[:, k, 0:128], h1T.ap[:, k, cs]) for k in range(8)], [wsl, h1T])
                        mm_group(py_.ap, py_, [(wsl.ap[:, k, 128:256], oT.ap[:, k, cs]) for k in range(8)], [wsl, oT])
                        g_ = gsb[0]
                        op("act", lambda e: e.activation(out=g_.ap, in_=pg_.ap, func=AF.Sigmoid, bias=V(V_GATEB + 8 + dc)), [pg_, vecs], [g_])
                        op("dve", lambda e: e.tensor_tensor(out=mixT.ap[:, dc, AO + 512 * tt:AO + 512 * tt + 512], in0=py_.ap, in1=g_.ap, op=ALU.mult), [py_, g_], [mixT])
                        if silu_units:
                            silu_unit(*silu_units.pop(0))
                while silu_units:
                    silu_unit(*silu_units.pop(0))
                if dbg:
                    dma("sp", dbg_d["d_saT"], cvT.ap, reads=[cvT])
                for dc in range(8):
                    wsl = ring[ring_i[0] % NRING]
                    ring_i[0] += 1
                    load_w(w_in[:, 5120 + 128 * dc:5120 + 128 * dc + 128], ncols=128, slot=wsl, col0=0)
                    load_w(w_co[:, 128 * dc:128 * dc + 128], ncols=128, slot=wsl, col0=128)
                    for tt in range(4):
                        cs = slice(512 * tt, 512 * tt + 512)
                        mcs = slice(AO + 512 * tt, AO + 512 * tt + 512)
                        pg_ = psb[nxt("p4", 4)]
                        py_ = psb[nxt("p4", 4)]
                        mm_group(pg_.ap, pg_, [(wsl.ap[:, k, 0:128], h1T.ap[:, k, cs]) for k in range(8)], [wsl, h1T])
                        mm_group(py_.ap, py_, [(wsl.ap[:, k, 128:256], cvT.ap[:, k, cs]) for k in range(8)], [wsl, cvT])
                        g_ = gsb[1]
                        t_ = t1[tt % 2]
                        op("act", lambda e: e.activation(out=g_.ap, in_=pg_.ap, func=AF.Sigmoid, bias=V(V_GATEB + dc)), [pg_, vecs], [g_])
                        op("dve", lambda e: e.tensor_tensor(out=t_.ap, in0=py_.ap, in1=g_.ap, op=ALU.mult), [py_, g_], [t_])
                        op("dve", lambda e: e.tensor_tensor(out=mixT.ap[:, dc, mcs], in0=mixT.ap[:, dc, mcs], in1=t_.ap, op=ALU.add), [mixT, t_], [mixT])
                if dbg:
                    dma("sp", dbg_d["d_mixT"], mixT.ap[:, :, AO:AO + S], reads=[mixT])
                if stop_after not in ("mix", "mix+"):
                    pre["wo"] = [load_w(w_out[:, 512 * cb:512 * cb + 512]) for cb in range(2)]
                sc.barrier()
                if stop_after == "mix":
                    stop_now = True
                    break
                if stop_after == "mix+":
                    skip_rest = True
                    break
                L = Region(R0)
                h2T = L.get([128, 8, S], BF16)
                T = Region(R2)
                xt = [T.get([128, D], F32) for i in range(3)]
                x1t = [T.get([128, D], F32) for i in range(3)]
                junk = T.get([128, D], BF16)
                xn = [T.get([128, D], BF16) for i in range(2)]
                ssq = [T.get([128, 8], F32) for i in range(3)]
                wo = pre.pop("wo")
                for t in range(17):
                    if t < 16:
                        xb, x1b = xt[t % 3], x1t[t % 3]
                        dma("sp", xb.ap, xs[128 * t:128 * t + 128, :], writes=[xb])
                        for cb in range(2):
                            p_ = psb[nxt("p4", 4)]
                            mm_group(p_.ap, p_, [(mixT.ap[:, k, AO + 128 * t:AO + 128 * t + 128], wo[cb].ap[:, k, :]) for k in range(8)], [mixT, wo[cb]])
                            op("dve", lambda e: e.tensor_tensor(out=x1b.ap[:, 512 * cb:512 * cb + 512], in0=p_.ap, in1=xb.ap[:, 512 * cb:512 * cb + 512], op=ALU.add), [p_, xb], [x1b])
                        dma("sp", x1scr[128 * t:128 * t + 128, :], x1b.ap, reads=[x1b])
                        rms_stage1(None, x1b, junk, ssq[t % 3])
                    if t >= 1:
                        rms_stage2(x1t[(t - 1) % 3], ssq[(t - 1) % 3], xn[(t - 1) % 2], rows.ap[:, D:2 * D], h2T, t - 1)
                if dbg:
                    dma("sp", dbg_d["d_h2T"], h2T.ap, reads=[h2T])
                sc.barrier()
                if stop_after == "x1":
                    stop_now = True
                    break
                if stop_after == "x1+":
                    skip_rest = True
                    break
            if stop_now:
                break
            if skip_rest:
                continue
            if stop_after == "conly":
                L = Region(R0)
                h2T = L.get([128, 8, S], BF16)
            T = Region(L.off)
            gT = T.get([128, 22, 1024], BF16)
            wdn = [T.get([128, D], BF16) for kk in range(22)]
            uT = [[T.get([128, 8 + 1024], BF16) for vg in range(2)] for i in range(2)]
            halo = T.get([128, NFC, 2], BF16)
            dgf = [T.get([128, 6, 128], BF16) for i in range(2)]
            sgf = [T.get([128, 512], F32) for i in range(2)]
            x1r = [T.get([128, D], F32) for i in range(2)]
            ot = [T.get([128, D], F32) for i in range(2)]
            UO = 8
            for i in range(2):
                for vg in range(2):
                    op("dve", lambda e: e.memset(uT[i][vg].ap[:, 0:UO], 0.0), [], [uT[i][vg]])
            for hf in range(2):
                for j in range(22):
                    wsl = ring[ring_i[0] % NRING]
                    ring_i[0] += 1
                    load_w(w_up[:, 128 * j:128 * j + 128], ncols=128, slot=wsl, col0=0)
                    load_w(w_up[:, DFF + 128 * j:DFF + 128 * j + 128], ncols=128, slot=wsl, col0=128)
                    if hf == 0:
                        dma("pool", wdn[j].ap, w_dn[128 * j:128 * j + 128, :], writes=[wdn[j]])
                    dg_ = dgf[j % 2]
                    for vg in range(2):
                        ch = j + 22 * vg
                        for jj in range(3):
                            op("dve", lambda e: e.tensor_scalar(out=dg_.ap[:, 3 * vg + jj, :], in0=ident.ap, scalar1=V(V_FCW + ch * 3 + jj), scalar2=None, op0=ALU.mult),
                               [ident, vecs], [dg_])
                    ub = uT[j % 2]
                    for vg in range(2):
                        ch = j + 22 * vg
                        if hf == 1:
                            op("dve", lambda e: e.tensor_copy(out=ub[vg].ap[:, UO - 2:UO], in_=halo.ap[:, ch, :]), [halo], [ub[vg]])
                        for tt in range(2):
                            tok = slice(1024 * hf + 512 * tt, 1024 * hf + 512 * tt + 512)
                            p_ = psb[nxt("p4", 4)]
                            mm_group(p_.ap, p_, [(wsl.ap[:, k, 128 * vg:128 * vg + 128], h2T.ap[:, k, tok]) for k in range(8)], [wsl, h2T])
                            op("act", lambda e: e.activation(out=ub[vg].ap[:, UO + 512 * tt:UO + 512 * tt + 512], in_=p_.ap, func=AF.Copy), [p_], [ub[vg]])
                        if hf == 0:
                            op("dve", lambda e: e.tensor_copy(out=halo.ap[:, ch, :], in_=ub[vg].ap[:, UO + 1022:UO + 1024]), [ub[vg]], [halo])
                    for tt in range(2):
                        pcv = psb[4 + nxt("b1", 3)]
                        pcg = psb[4 + nxt("b1", 3)]
                        for vg, pc_ in ((1, pcg), (0, pcv)):
                            mm_group(pc_.ap, pc_, [(dg_.ap[:, 3 * vg + jj, :], ub[vg].ap[:, UO - 2 + 512 * tt + jj:UO - 2 + 512 * tt + jj + 512]) for jj in range(3)], [dg_, ub[vg]])
                        s_ = sgf[tt % 2]
                        op("act", lambda e: e.activation(out=s_.ap, in_=pcg.ap, func=AF.Silu, bias=V(V_FCB + 22 + j)), [pcg, vecs], [s_])
                        op("dve", lambda e: e.scalar_tensor_tensor(out=gT.ap[:, j, 512 * tt:512 * tt + 512], in0=pcv.ap, scalar=V(V_FCB + j), in1=s_.ap, op0=ALU.add, op1=ALU.mult),
                           [pcv, vecs, s_], [gT])
                if dbg and hf == 0:
                    dma("sp", dbg_d["d_gT"], gT.ap, reads=[gT])
                for t8 in range(8):
                    t = 8 * hf + t8
                    xr, ob_ = x1r[t8 % 2], ot[t8 % 2]
                    dma("sp", xr.ap, x1scr[128 * t:128 * t + 128, :], writes=[xr])
                    for cb in range(2):
                        p_ = psb[nxt("p4", 4)]
                        mm_group(p_.ap, p_, [(gT.ap[:, j, 128 * t8:128 * t8 + 128], wdn[j].ap[:, 512 * cb:512 * cb + 512]) for j in range(22)], [gT] + wdn)
                        op("dve", lambda e: e.tensor_tensor(out=ob_.ap[:, 512 * cb:512 * cb + 512], in0=p_.ap, in1=xr.ap[:, 512 * cb:512 * cb + 512], op=ALU.add), [p_, xr], [ob_])
                    dma("sp", outs[128 * t:128 * t + 128, :], ob_.ap, reads=[ob_])
            if s + 1 < nseq:
                pre["wv"] = [load_w(w_in[:, 4096 + 512 * cb:4096 + 512 * cb + 512]) for cb in range(2)]
            sc.barrier()
        sc.barrier(engines=("sp",))
    return nc


def _host_layout(inputs):
    f = lambda a: np.ascontiguousarray(np.asarray(a, dtype=np.float32))
    vecs = np.zeros((128, NV), np.float32)
    vecs[:, V_GATEB:V_GATEB + 16] = f(inputs["gate_b"][0]).reshape(16, 128).T
    vecs[:, V_CONVB:V_CONVB + 8] = f(inputs["conv_b"][0]).reshape(8, 128).T
    vecs[:, V_CNG:V_CNG + 8] = f(inputs["conv_norm_g"][0]).reshape(8, 128).T
    vecs[:, V_FCB:V_FCB + NFC] = f(inputs["ffn_conv_b"][0]).reshape(NFC, 128).T
    vecs[:, V_CW:V_CW + 248] = f(inputs["conv_w"][0]).reshape(31, 8, 128).transpose(2, 1, 0).reshape(128, 248)
    vecs[:, V_FCW:V_FCW + NFC * 3] = f(inputs["ffn_conv_w"][0]).reshape(3, NFC, 128).transpose(2, 1, 0).reshape(128, NFC * 3)
    vecs[:, V_QG] = np.tile(f(inputs["q_norm_g"][0]), 2)
    vecs[:, V_KG] = np.tile(f(inputs["k_norm_g"][0]), 2)
    rows = np.zeros((128, 2 * D), np.float32)
    rows[:, 0:D] = np.broadcast_to(f(inputs["norm1_g"][0])[None, :], (128, D))
    rows[:, D:] = np.broadcast_to(f(inputs["norm2_g"][0])[None, :], (128, D))
    cst = np.zeros((128, NCST), np.float32)
    cst[:, C_ID:C_ID + 128] = np.eye(128, dtype=np.float32)
    cst[0:64, C_BLK:C_BLK + 64] = 1.0
    cst[64:128, C_BLK + 64:C_BLK + 128] = 1.0
    k_i = np.arange(128)[:, None]
    q_i = np.arange(128)[None, :]
    cst[:, C_ST:C_ST + 128] = q_i + 128 - k_i
    cst[:, C_ST + 128:C_ST + 256] = np.maximum(q_i - k_i, 0)
    cst[:, C_MK:C_MK + 128] = (k_i >= q_i)
    cst[:, C_MK + 128:C_MK + 256] = (q_i >= k_i)
    shared = {
        "w_in": f(inputs["w_in"][0]), "w_conv_out": f(inputs["w_conv_out"][0]), "w_attn_out": f(inputs["w_attn_out"][0]),
        "w_out": f(inputs["w_out"][0]), "w_up": f(inputs["w_up"][0]), "w_down": f(inputs["w_down"][0]),
        "vecs": vecs, "rows": rows, "consts": cst,
    }
    return shared


def kernel(**inputs):
    x = np.asarray(inputs["x"], dtype=np.float32)
    B = x.shape[0]
    shared = _host_layout(inputs)
    nc = build_nc(SEQ_PER_CORE)
    in_maps = []
    for c in range(NCORES):
        m = dict(shared)
        m["x"] = np.ascontiguousarray(x[c * SEQ_PER_CORE:(c + 1) * SEQ_PER_CORE].reshape(SEQ_PER_CORE * S, D))
        in_maps.append(m)
    res = run_bass_kernel_spmd(nc, in_maps, core_ids=list(range(NCORES)))
    out = np.concatenate([np.asarray(r["out"], dtype=np.float32).reshape(SEQ_PER_CORE, S, D) for r in res.results], axis=0)
    return out
```

```python
import numpy as np
import concourse.bass as bass
import concourse.mybir as mybir
from concourse.bass_utils import run_bass_kernel_spmd
from contextlib import ExitStack

F32 = mybir.dt.float32
BF16 = mybir.dt.bfloat16
ALU = mybir.AluOpType
AF = mybir.ActivationFunctionType

S = 2048
D = 1024
NH = 16
DFF = 2816
IN_COLS = 7168
EPS = 1e-6
NCORES = 8
SEQ_PER_CORE = 2
NFC = 2 * DFF // 128
ND = 10
KB = 1024

V_GATEB = 0
V_CONVB = 16
V_CNG = 24
V_FCB = 32
V_CW = 76
V_FCW = V_CW + 8 * 31
V_QG = V_FCW + NFC * 3
V_KG = V_QG + 1
NV = V_KG + 1

C_ID, C_BLK, C_ST, C_MK, NCST = 0, 128, 256, 512, 768


class Buf:
    __slots__ = ("ap", "w", "r")

    def __init__(self, ap):
        self.ap = ap
        self.w = None
        self.r = {}


class Sched:
    def __init__(self, nc, stack):
        self.nc = nc
        self.eng = dict(pe=nc.tensor, act=nc.scalar, dve=nc.vector, pool=nc.gpsimd, sp=nc.sync)
        self.msem = {e: stack.enter_context(nc.semaphore("ms_" + e)) for e in ("pe", "act", "dve", "pool")}
        self.mcnt = {e: 0 for e in self.msem}
        self.dsem = {q: [stack.enter_context(nc.semaphore("d_%s_%d" % (q, i))) for i in range(ND)] for q in ("sp", "pool")}
        self.dcnt = {q: [0] * ND for q in self.dsem}
        self.dnext = {q: 0 for q in self.dsem}
        self.seen = {e: {} for e in self.eng}
        self.pending = {e: False for e in self.msem}

    def _wait(self, e, ev):
        sem, val, key, prod = ev
        if self.seen[e].get(key, 0) >= val:
            return
        self.eng[e].wait_ge(sem, val)
        self.seen[e][key] = val

    def _deps(self, e, reads, writes):
        for b in reads:
            if b.w is not None:
                if b.w[3] == e and e == "pe":
                    continue
                self._wait(e, b.w)
        for b in writes:
            if b.w is not None and not (b.w[3] == e and e == "pe"):
                self._wait(e, b.w)
            for ev in b.r.values():
                if not (ev[3] == e and e == "pe"):
                    self._wait(e, ev)

    def _record(self, ev, reads, writes):
        for b in reads:
            b.r[ev[2]] = ev
        for b in writes:
            b.w = ev
            b.r = {}

    def op(self, e, fn, reads=(), writes=(), signal=True):
        self._deps(e, reads, writes)
        ins = fn(self.eng[e])
        if signal:
            self.mcnt[e] += 1
            ins.then_inc(self.msem[e], 1)
            ev = (self.msem[e], self.mcnt[e], e, e)
            self.pending[e] = False
        else:
            ev = (self.msem[e], self.mcnt[e] + 1, e, e)
            self.pending[e] = True
        self._record(ev, reads, writes)
        return ins

    def dma(self, q, out_ap, in_ap, reads=(), writes=()):
        i = self.dnext[q]
        self.dnext[q] = (i + 1) % ND
        sem = self.dsem[q][i]
        key = "d_%s_%d" % (q, i)
        if self.dcnt[q][i] > 0:
            self._wait(q, (sem, self.dcnt[q][i], key, "dma"))
        self._deps(q, reads, writes)
        self.eng[q].dma_start(out=out_ap, in_=in_ap).then_inc(sem, 16)
        self.dcnt[q][i] += 16
        ev = (sem, self.dcnt[q][i], key, "dma")
        self._record(ev, reads, writes)

    def all_events(self):
        evs = []
        for e in self.msem:
            assert not self.pending[e], e
            if self.mcnt[e] > 0:
                evs.append((self.msem[e], self.mcnt[e], e, e))
        for q in self.dsem:
            for i in range(ND):
                if self.dcnt[q][i] > 0:
                    evs.append((self.dsem[q][i], self.dcnt[q][i], "d_%s_%d" % (q, i), "dma"))
        return evs

    def barrier(self, engines=("pe", "act", "dve", "pool", "sp")):
        evs = self.all_events()
        for e in engines:
            for ev in evs:
                self._wait(e, ev)


def build_nc(nseq=SEQ_PER_CORE, dbg=False, stop_after=None):
    nc = bass.Bass("TRN2", target_bir_lowering=False)
    dt = nc.dram_tensor
    x_d = dt("x", [nseq * S, D], F32, kind="ExternalInput").ap()
    w_in = dt("w_in", [D, IN_COLS], F32, kind="ExternalInput").ap()
    w_co = dt("w_conv_out", [D, D], F32, kind="ExternalInput").ap()
    w_ao = dt("w_attn_out", [D, D], F32, kind="ExternalInput").ap()
    w_out = dt("w_out", [D, D], F32, kind="ExternalInput").ap()
    w_up = dt("w_up", [D, 2 * DFF], F32, kind="ExternalInput").ap()
    w_dn = dt("w_down", [DFF, D], F32, kind="ExternalInput").ap()
    vecs_d = dt("vecs", [128, NV], F32, kind="ExternalInput").ap()
    rows_d = dt("rows", [128, 2 * D], F32, kind="ExternalInput").ap()
    cst_d = dt("consts", [128, NCST], F32, kind="ExternalInput").ap()
    out_d = dt("out", [nseq * S, D], F32, kind="ExternalOutput").ap()
    vscr = dt("vscr", [S, 8, 256], BF16, kind="Internal").ap()
    x1scr = dt("x1scr", [S, D], F32, kind="Internal").ap()
    dbg_d = {}
    if dbg:
        for nm, shp, ty in (("d_h1T", [128, 8, S], BF16), ("d_qT", [128, S], BF16), ("d_kT", [128, S], BF16),
                            ("d_oT", [128, 8, S], BF16), ("d_saT", [128, 8, S], BF16), ("d_mixT", [128, 8, S], BF16),
                            ("d_h2T", [128, 8, S], BF16), ("d_gT", [128, 22, 1024], BF16)):
            dbg_d[nm] = dt(nm, shp, ty, kind="ExternalOutput").ap()

    with ExitStack() as top:
        sc = Sched(nc, top)
        op, dma = sc.op, sc.dma
        base0 = nc.sbuf_base
        base = ((base0 + 63) // 64) * 64
        ARENA = 204 * KB
        top.enter_context(nc.sbuf_tensor("arena", [128, (ARENA + base - base0 + 64) // 2], BF16))
        uid = [0]

        def at(off, shape, dtype):
            uid[0] += 1
            n = 1
            for d_ in shape[1:]:
                n *= d_
            nbytes = n * (4 if dtype == F32 else 2)
            assert off % 32 == 0 and off + nbytes <= ARENA, (off, nbytes)
            h = nc.alloc_sbuf_tensor_at("t%d" % uid[0], list(shape), dtype, offset=base + off)
            return Buf(h.ap()), off + ((nbytes + 31) // 32) * 32

        class Region:
            def __init__(self, off):
                self.off = off

            def get(self, shape, dtype):
                b, self.off = at(self.off, shape, dtype)
                return b

        G = Region(0)
        cst = G.get([128, NCST], F32)
        vecs = G.get([128, NV], F32)
        rows = G.get([128, 2 * D], F32)
        ident = G.get([128, 128], BF16)
        blk = G.get([128, 128], BF16)
        ones = G.get([128, 128], BF16)
        NRING = 3
        ring = [G.get([128, 8, 512], BF16) for i in range(NRING)]
        ring_i = [0]
        R0 = G.off
        psb = [Buf(top.enter_context(nc.psum_tensor("ps%d" % i, [128, 512], F32)).ap()) for i in range(7)]
        pst = Buf(top.enter_context(nc.psum_tensor("pst", [128, 1024], BF16)).ap())

        def V(col, n=1):
            return vecs.ap[:, col:col + n]

        dma("sp", cst.ap, cst_d, writes=[cst])
        dma("sp", vecs.ap, vecs_d, writes=[vecs])
        dma("sp", rows.ap, rows_d, writes=[rows])
        op("dve", lambda e: e.tensor_copy(out=ident.ap, in_=cst.ap[:, C_ID:C_ID + 128]), [cst], [ident])
        op("dve", lambda e: e.tensor_copy(out=blk.ap, in_=cst.ap[:, C_BLK:C_BLK + 128]), [cst], [blk])
        op("dve", lambda e: e.memset(ones.ap, 1.0), [], [ones])
        op("dve", lambda e: e.tensor_scalar(out=V(V_KG), in0=V(V_KG), scalar1=8.0, scalar2=None, op0=ALU.mult), [vecs], [vecs])

        def eidx(h, gi):
            return 16 - (h + 1) + 4 * gi

        def load_w(src_ap, ncols=512, kchunks=8, slot=None, col0=0):
            if slot is None:
                slot = ring[ring_i[0] % NRING]
                ring_i[0] += 1
            dma("pool", slot.ap[:, 0:kchunks, col0:col0 + ncols], src_ap.rearrange("(k p) c -> p k c", p=128), writes=[slot])
            return slot

        def mm_group(out_ap, out_buf, pairs, reads, start=True, sig=True):
            n = len(pairs)
            for i, (l, r) in enumerate(pairs):
                op("pe", lambda e: e.matmul(out_ap, lhsT=l, rhs=r, start=(start and i == 0), stop=(i == n - 1), skip_group_check=True),
                   reads, [out_buf], signal=(sig and i == n - 1))

        rrc = {"p4": 0, "st": 0, "pb": 0, "o": 0, "pq": 0, "b1": 0}

        def nxt(key, n):
            v = rrc[key] % n
            rrc[key] += 1
            return v

        def rms_stage1(src_dram_rows, xb, junk, sq):
            if src_dram_rows is not None:
                dma("sp", xb.ap, src_dram_rows, writes=[xb])
            op("act", lambda e: e.activation(out=junk.ap, in_=xb.ap, func=AF.Square, accum_out=sq.ap[:, 0:1]), [xb], [junk, sq])
            op("dve", lambda e: e.tensor_scalar(out=sq.ap[:, 1:2], in0=sq.ap[:, 0:1], scalar1=1.0 / D, scalar2=EPS, op0=ALU.mult, op1=ALU.add), [sq], [sq])
            op("act", lambda e: e.activation(out=sq.ap[:, 1:2], in_=sq.ap[:, 1:2], func=AF.Ln), [sq], [sq])
            op("act", lambda e: e.activation(out=sq.ap[:, 1:2], in_=sq.ap[:, 1:2], func=AF.Exp, scale=-0.5), [sq], [sq])

        def rms_stage2(xb, sq, xnb, grow_ap, dstT, t):
            op("dve", lambda e: e.scalar_tensor_tensor(out=xnb.ap, in0=xb.ap, scalar=sq.ap[:, 1:2], in1=grow_ap, op0=ALU.mult, op1=ALU.mult), [xb, sq, rows], [xnb])
            for k in range(8):
                op("pe", lambda e: e.transpose(out=pst.ap[:, 128 * k:128 * k + 128], in_=xnb.ap[:, 128 * k:128 * k + 128], identity=ident.ap),
                   [xnb, ident], [pst], signal=(k == 7))
            op("act", lambda e: e.activation(out=dstT.ap[:, :, 128 * t:128 * t + 128], in_=pst.ap.rearrange("p (k t) -> p k t", k=8), func=AF.Copy), [pst], [dstT])

        pre = {}
        for s in range(nseq):
            xs = x_d[s * S:(s + 1) * S, :]
            outs = out_d[s * S:(s + 1) * S, :]
            stop_now = False
            skip_rest = False
            for _once in ([] if stop_after == "conly" else [0]):
                L = Region(R0)
                h1T = L.get([128, 8, S], BF16)
                oT = L.get([128, 8, S], BF16)
                R1 = L.off
                T = Region(R1)
                xt = [T.get([128, D], F32) for i in range(3)]
                junk = T.get([128, D], BF16)
                xn = [T.get([128, D], BF16) for i in range(2)]
                ssq = [T.get([128, 8], F32) for i in range(3)]
                vst = [T.get([128, 8, 256], BF16) for i in range(2)]
                for i in range(2):
                    op("dve", lambda e: e.memset(vst[i].ap, 1.0), [], [vst[i]])
                wv = pre.pop("wv", None) or [load_w(w_in[:, 4096 + 512 * cb:4096 + 512 * cb + 512]) for cb in range(2)]
                for t in range(17):
                    if t < 16:
                        rms_stage1(xs[128 * t:128 * t + 128, :], xt[t % 3], junk, ssq[t % 3])
                    if t >= 1:
                        rms_stage2(xt[(t - 1) % 3], ssq[(t - 1) % 3], xn[(t - 1) % 2], rows.ap[:, 0:D], h1T, t - 1)
                for t in range(16):
                    vb = vst[t % 2]
                    for cb in range(2):
                        pb_ = psb[nxt("p4", 4)]
                        mm_group(pb_.ap, pb_, [(h1T.ap[:, k, 128 * t:128 * t + 128], wv[cb].ap[:, k, :]) for k in range(8)], [h1T, wv[cb]])
                        pv4 = pb_.ap.rearrange("p (h c) -> p h c", h=4)
                        op("act", lambda e: e.activation(out=vb.ap[:, 4 * cb:4 * cb + 4, 0:64], in_=pv4[:, :, 0:64], func=AF.Copy), [pb_], [vb])
                        op("dve", lambda e: e.tensor_copy(out=vb.ap[:, 4 * cb:4 * cb + 4, 192:256], in_=pv4[:, :, 64:128]), [pb_], [vb])
                    dma("sp", vscr[128 * t:128 * t + 128, :, :], vb.ap, reads=[vb])
                if dbg:
                    dma("sp", dbg_d["d_h1T"], h1T.ap, reads=[h1T])
                if stop_after not in ("v", "v+"):
                    pre["wq"] = load_w(w_in[:, 2048:2048 + 512])
                    pre["wk"] = load_w(w_in[:, 3072:3072 + 512])
                sc.barrier()
                if stop_after == "v":
                    stop_now = True
                    break
                if stop_after == "v+":
                    skip_rest = True
                    break
                T = Region(R1)
                etab = T.get([128, 24, 256], BF16)
                etmp = T.get([128, 256], F32)
                qT = [T.get([128, S], BF16) for i in range(2)]
                kT = [T.get([128, S], BF16) for i in range(2)]
                sqb = [T.get([128, 512], BF16) for i in range(2)]
                Rb = [T.get([128, 512], F32) for i in range(2)]
                varr = [[T.get([128, 16, 256], BF16) for g in range(3)] for i in range(2)]
                P3s = [T.get([128, S], BF16) for i in range(2)]
                Pb = [T.get([128, 512], BF16) for i in range(8)]
                rden = T.get([128, 512], F32)
                for ei in range(24):
                    c = 2.0 ** (ei / 2.0 - 8.0)
                    op("act", lambda e: e.activation(out=etmp.ap, in_=cst.ap[:, C_ST:C_ST + 256], func=AF.Exp, scale=-c), [cst], [etmp])
                    op("dve", lambda e: e.tensor_tensor(out=etab.ap[:, ei, :], in0=etmp.ap, in1=cst.ap[:, C_MK:C_MK + 256], op=ALU.mult), [etmp, cst], [etab])
                wq = wk = None
                pending = []

                def tick():
                    for p_ in pending:
                        p_[0] -= 1
                    while pending and pending[0][0] <= 0:
                        pending.pop(0)[1]()

                def flush():
                    while pending:
                        pending.pop(0)[1]()

                for hp in range(8):
                    if hp == 0:
                        wq, wk = pre.pop("wq"), pre.pop("wk")
                    elif hp % 4 == 0:
                        wq = load_w(w_in[:, 2048 + 128 * hp:2048 + 128 * hp + 512])
                        wk = load_w(w_in[:, 3072 + 128 * hp:3072 + 128 * hp + 512])
                    va = varr[hp % 2]
                    for g, r in enumerate((1, 4, 16)):
                        if r == 1:
                            dma("sp", va[g].ap, vscr.rearrange("(b i) h c -> i b h c", i=128)[:, :, hp, :], writes=[va[g]])
                        else:
                            for rho in range(r):
                                nb = 16 // r
                                src = vscr.rearrange("(b i r) h c -> r i b h c", r=r, i=128)[rho, :, :, hp, :]
                                dma("sp", va[g].ap[:, rho * nb:(rho + 1) * nb, :], src, writes=[va[g]])
                    qb, kb = qT[hp % 2], kT[hp % 2]
                    coff = 128 * (hp % 4)
                    for (wt_, dstb, gcol) in ((wq, qb, V_QG), (wk, kb, V_KG)):
                        for tt in range(4):
                            pa_ = psb[4 + nxt("pq", 2)]
                            sq_, R_ = sqb[tt % 2], Rb[tt % 2]
                            mm_group(pa_.ap, pa_, [(wt_.ap[:, k, coff:coff + 128], h1T.ap[:, k, 512 * tt:512 * tt + 512]) for k in range(8)], [wt_, h1T])
                            op("act", lambda e: e.activation(out=sq_.ap, in_=pa_.ap, func=AF.Square), [pa_], [sq_])
                            p6 = psb[6]
                            mm_group(p6.ap, p6, [(blk.ap, sq_.ap)], [blk, sq_])
                            op("dve", lambda e: e.tensor_scalar(out=R_.ap, in0=p6.ap, scalar1=64.0 * EPS, scalar2=None, op0=ALU.add), [p6], [R_])
                            op("act", lambda e: e.activation(out=R_.ap, in_=R_.ap, func=AF.Ln), [R_], [R_])
                            op("act", lambda e: e.activation(out=R_.ap, in_=R_.ap, func=AF.Exp, scale=-0.5), [R_], [R_])
                            op("dve", lambda e: e.scalar_tensor_tensor(out=dstb.ap[:, 512 * tt:512 * tt + 512], in0=pa_.ap, scalar=V(gcol), in1=R_.ap, op0=ALU.mult, op1=ALU.mult),
                               [pa_, vecs, R_], [dstb])
                    if dbg and hp == 0:
                        dma("sp", dbg_d["d_qT"], qb.ap, reads=[qb])
                        dma("sp", dbg_d["d_kT"], kb.ap, reads=[kb])
                    for e_ in range(2):
                        h = 2 * hp + e_
                        p0 = 64 * e_
                        qh = qb.ap[p0:p0 + 64, :]
                        kh = kb.ap[p0:p0 + 64, :]
                        P3 = P3s[h % 2]

                        def stile(dst_ap, dst_buf, kcols, qcols, sig, kh=kh, qh=qh, kb=kb, qb=qb):
                            op("pe", lambda e: e.matmul(dst_ap, lhsT=kh[:, kcols], rhs=qh[:, qcols], start=True, stop=True, skip_group_check=True),
                               [kb, qb], [dst_buf], signal=sig)

                        def make_pv(units, c, e_=e_, p0=p0, hp=hp, va=va):
                            def emit():
                                ob = psb[2 + nxt("o", 2)]
                                nu = len(units)
                                for ui, (pbuf, p_ap, gi, tile_i, ocols) in enumerate(units):
                                    op("pe", lambda e: e.matmul(ob.ap[:, ocols], lhsT=va[gi].ap[:, tile_i, 128 * e_:128 * e_ + 128], rhs=p_ap,
                                                                start=(ui == 0), stop=(ui == nu - 1), skip_group_check=True),
                                       [va[gi], pbuf], [ob], signal=(ui == nu - 1))
                                dq = 64 * (1 - e_)
                                op("act", lambda e: e.activation(out=rden.ap[p0:p0 + 64, :], in_=ob.ap[dq:dq + 64, :], func=AF.Ln), [ob], [rden])
                                op("act", lambda e: e.activation(out=rden.ap[p0:p0 + 64, :], in_=rden.ap[p0:p0 + 64, :], func=AF.Exp, scale=-1.0), [rden], [rden])
                                op("dve", lambda e: e.tensor_tensor(out=oT.ap[p0:p0 + 64, hp, 512 * c:512 * c + 512], in0=ob.ap[p0:p0 + 64, :], in1=rden.ap[p0:p0 + 64, :], op=ALU.mult),
                                   [ob, rden], [oT])
                            return emit

                        e3 = eidx(h, 2)
                        for ch in range(4):
                            stb = psb[(0, 1, 6)[nxt("st", 3)]]
                            for rr in range(4):
                                rho = 4 * ch + rr
                                stile(stb.ap[:, 128 * rr:128 * rr + 128], stb, slice(rho, S, 16), slice(rho, S, 16), rr == 3)
                            pv = P3.ap[:, 512 * ch:512 * ch + 512]
                            pv3 = pv.rearrange("p (a b) -> p a b", a=4)
                            e_ap = etab.ap[:, e3:e3 + 1, 128:256].to_broadcast([128, 4, 128])
                            op("act", lambda e: e.activation(out=pv, in_=stb.ap, func=AF.Exp), [stb], [P3])
                            op("dve", lambda e: e.tensor_tensor(out=pv3, in0=pv3, in1=e_ap, op=ALU.mult), [P3, etab], [P3])
                            tick()
                        for c in range(4):
                            units = []
                            for gi, r in ((0, 1), (1, 4)):
                                ei = eidx(h, gi)
                                for half in range(2):
                                    stb = psb[(0, 1, 6)[nxt("st", 3)]]
                                    pbuf = Pb[nxt("pb", 8)]
                                    for jj in range(2):
                                        if gi == 0:
                                            b = 4 * c + 2 * half + jj
                                            has_prev = b > 0
                                            kprev = slice(128 * (b - 1), 128 * b)
                                            kown = slice(128 * b, 128 * b + 128)
                                            ocols = slice(128 * (b - 4 * c), 128 * (b - 4 * c) + 128)
                                            tprev, town = b - 1, b
                                        else:
                                            rho = 2 * half + jj
                                            has_prev = c > 0
                                            kprev = slice(rho + 512 * (c - 1), 512 * c, 4)
                                            kown = slice(rho + 512 * c, 512 * c + 512, 4)
                                            ocols = slice(rho, 512, 4)
                                            tprev, town = rho * 4 + c - 1, rho * 4 + c
                                        o0 = 256 * jj
                                        if has_prev:
                                            stile(stb.ap[:, o0:o0 + 128], stb, kprev, kown, False)
                                            units.append((pbuf, pbuf.ap[:, o0:o0 + 128], gi, tprev, ocols))
                                        stile(stb.ap[:, o0 + 128:o0 + 256], stb, kown, kown, jj == 1)
                                        units.append((pbuf, pbuf.ap[:, o0 + 128:o0 + 256], gi, town, ocols))
                                    e_ap = etab.ap[:, ei:ei + 1, :].to_broadcast([128, 2, 256])
                                    pb3 = pbuf.ap.rearrange("p (a b) -> p a b", a=2)
                                    op("act", lambda e: e.activation(out=pbuf.ap, in_=stb.ap, func=AF.Exp), [stb], [pbuf])
                                    op("dve", lambda e: e.tensor_tensor(out=pb3, in0=pb3, in1=e_ap, op=ALU.mult), [pbuf, etab], [pbuf])
                                    tick()
                            for rho in range(16):
                                units.append((P3, P3.ap[:, 128 * rho + 32 * c:128 * rho + 32 * c + 32], 2, rho, slice(rho, 512, 16)))
                            pending.append([2, make_pv(units, c)])
                    flush()
                if dbg:
                    dma("sp", dbg_d["d_oT"], oT.ap, reads=[oT])
                if stop_after not in ("attn", "attn+"):
                    pre["wa"] = load_w(w_in[:, 0:512])
                    pre["wg"] = load_w(w_in[:, 1024:1024 + 512])
                sc.barrier()
                if stop_after == "attn":
                    stop_now = True
                    break
                if stop_after == "attn+":
                    skip_rest = True
                    break
                T = Region(R1)
                aT = T.get([128, 8, 32 + S], BF16)
                R2 = T.off
                cvT = T.get([128, 8, S], BF16)
                dgb = T.get([128, 31, 128], BF16)
                sg = [T.get([128, 512], F32) for i in range(2)]
                sgm = [T.get([128, 512], F32) for i in range(2)]
                sqv = [T.get([128, 512], BF16) for i in range(2)]
                rr_ = T.get([128, S], F32)
                gsb = [T.get([128, 512], F32) for i in range(2)]
                t1 = [T.get([128, 512], F32) for i in range(2)]
                AO = 32
                op("dve", lambda e: e.memset(aT.ap[:, :, 0:AO], 0.0), [], [aT])
                for cc in range(8):
                    if cc == 0:
                        wa, wg = pre.pop("wa"), pre.pop("wg")
                    elif cc % 4 == 0:
                        wa = load_w(w_in[:, 128 * cc:128 * cc + 512])
                        wg = load_w(w_in[:, 1024 + 128 * cc:1024 + 128 * cc + 512])
                    coff = 128 * (cc % 4)
                    for j in range(31):
                        op("dve", lambda e: e.tensor_scalar(out=dgb.ap[:, j, :], in0=ident.ap, scalar1=V(V_CW + cc * 31 + j), scalar2=None, op0=ALU.mult),
                           [ident, vecs], [dgb])
                    for tt in range(4):
                        pg_ = psb[4 + nxt("b1", 3)]
                        pv_ = psb[4 + nxt("b1", 3)]
                        mm_group(pg_.ap, pg_, [(wg.ap[:, k, coff:coff + 128], h1T.ap[:, k, 512 * tt:512 * tt + 512]) for k in range(8)], [wg, h1T])
                        mm_group(pv_.ap, pv_, [(wa.ap[:, k, coff:coff + 128], h1T.ap[:, k, 512 * tt:512 * tt + 512]) for k in range(8)], [wa, h1T])
                        sg_ = sg[tt % 2]
                        op("act", lambda e: e.activation(out=sg_.ap, in_=pg_.ap, func=AF.Sigmoid), [pg_], [sg_])
                        op("dve", lambda e: e.tensor_tensor(out=aT.ap[:, cc, AO + 512 * tt:AO + 512 * tt + 512], in0=pv_.ap, in1=sg_.ap, op=ALU.mult), [pv_, sg_], [aT])
                    for tt in range(4):
                        pc_ = psb[4 + nxt("b1", 3)]
                        mm_group(pc_.ap, pc_, [(dgb.ap[:, j, :], aT.ap[:, cc, AO - 30 + 512 * tt + j:AO - 30 + 512 * tt + j + 512]) for j in range(31)], [dgb, aT])
                        op("act", lambda e: e.activation(out=cvT.ap[:, cc, 512 * tt:512 * tt + 512], in_=pc_.ap, func=AF.Identity, bias=V(V_CONVB + cc)), [pc_, vecs], [cvT])
                        sv = sqv[tt % 2]
                        op("act", lambda e: e.activation(out=sv.ap, in_=pc_.ap, func=AF.Square, bias=V(V_CONVB + cc)), [pc_, vecs], [sv])
                        op("pe", lambda e: e.matmul(psb[tt].ap, lhsT=ones.ap, rhs=sv.ap, start=(cc == 0), stop=(cc == 7), skip_group_check=True),
                           [ones, sv], [psb[tt]], signal=True)
                for tt in range(4):
                    op("dve", lambda e: e.tensor_scalar(out=rr_.ap[:, 512 * tt:512 * tt + 512], in0=psb[tt].ap, scalar1=1.0 / D, scalar2=EPS, op0=ALU.mult, op1=ALU.add), [psb[tt]], [rr_])
                op("act", lambda e: e.activation(out=rr_.ap, in_=rr_.ap, func=AF.Ln), [rr_], [rr_])
                op("act", lambda e: e.activation(out=rr_.ap, in_=rr_.ap, func=AF.Exp, scale=-0.5), [rr_], [rr_])
                def silu_unit(cc, tt):
                    sg_, sm_ = sg[tt % 2], sgm[tt % 2]
                    cs = slice(512 * tt, 512 * tt + 512)
                    op("dve", lambda e: e.scalar_tensor_tensor(out=sg_.ap, in0=cvT.ap[:, cc, cs], scalar=V(V_CNG + cc), in1=rr_.ap[:, cs], op0=ALU.mult, op1=ALU.mult),
                       [cvT, vecs, rr_], [sg_])
                    op("act", lambda e: e.activation(out=sm_.ap, in_=sg_.ap, func=AF.Sigmoid), [sg_], [sm_])
                    op("dve", lambda e: e.tensor_tensor(out=cvT.ap[:, cc, cs], in0=sg_.ap, in1=sm_.ap, op=ALU.mult), [sg_, sm_], [cvT])

                silu_units = [(cc, tt) for cc in range(8) for tt in range(4)]
                mixT = aT
                for dc in range(8):
                    wsl = ring[ring_i[0] % NRING]
                    ring_i[0] += 1
                    load_w(w_in[:, 6144 + 128 * dc:6144 + 128 * dc + 128], ncols=128, slot=wsl, col0=0)
                    load_w(w_ao[:, 128 * dc:128 * dc + 128], ncols=128, slot=wsl, col0=128)
                    for tt in range(4):
                        cs = slice(512 * tt, 512 * tt + 512)
                        pg_ = psb[nxt("p4", 4)]
                        py_ = psb[nxt("p4", 4)]
                        mm_group(pg_.ap, pg_, [(wsl.ap[:, k, 0:128], h1T.ap[:, k, cs]) for k in range(8)], [wsl, h1T])
                        mm_group(py_.ap, py_, [(wsl.ap[:, k, 128:256], oT.ap[:, k, cs]) for k in range(8)], [wsl, oT])
                        g_ = gsb[0]
                        op("act", lambda e: e.activation(out=g_.ap, in_=pg_.ap, func=AF.Sigmoid, bias=V(V_GATEB + 8 + dc)), [pg_, vecs], [g_])
                        op("dve", lambda e: e.tensor_tensor(out=mixT.ap[:, dc, AO + 512 * tt:AO + 512 * tt + 512], in0=py_.ap, in1=g_.ap, op=ALU.mult), [py_, g_], [mixT])
                        if silu_units:
                            silu_unit(*silu_units.pop(0))
                while silu_units:
                    silu_unit(*silu_units.pop(0))
                if dbg:
                    dma("sp", dbg_d["d_saT"], cvT.ap, reads=[cvT])
                for dc in range(8):
                    wsl = ring[ring_i[0] % NRING]
                    ring_i[0] += 1
                    load_w(w_in[:, 5120 + 128 * dc:5120 + 128 * dc + 128], ncols=128, slot=wsl, col0=0)
                    load_w(w_co[:, 128 * dc:128 * dc + 128], ncols=128, slot=wsl, col0=128)
                    for tt in range(4):
                        cs = slice(512 * tt, 512 * tt + 512)
                        mcs = slice(AO + 512 * tt, AO + 512 * tt + 512)
                        pg_ = psb[nxt("p4", 4)]
                        py_ = psb[nxt("p4", 4)]
                        mm_group(pg_.ap, pg_, [(wsl.ap[:, k, 0:128], h1T.ap[:, k, cs]) for k in range(8)], [wsl, h1T])
                        mm_group(py_.ap, py_, [(wsl.ap[:, k, 128:256], cvT.ap[:, k, cs]) for k in range(8)], [wsl, cvT])
                        g_ = gsb[1]
                        t_ = t1[tt % 2]
                        op("act", lambda e: e.activation(out=g_.ap, in_=pg_.ap, func=AF.Sigmoid, bias=V(V_GATEB + dc)), [pg_, vecs], [g_])
                        op("dve", lambda e: e.tensor_tensor(out=t_.ap, in0=py_.ap, in1=g_.ap, op=ALU.mult), [py_, g_], [t_])
                        op("dve", lambda e: e.tensor_tensor(out=mixT.ap[:, dc, mcs], in0=mixT.ap[:, dc, mcs], in1=t_.ap, op=ALU.add), [mixT, t_], [mixT])
                if dbg:
                    dma("sp", dbg_d["d_mixT"], mixT.ap[:, :, AO:AO + S], reads=[mixT])
                if stop_after not in ("mix", "mix+"):
                    pre["wo"] = [load_w(w_out[:, 512 * cb:512 * cb + 512]) for cb in range(2)]
                sc.barrier()
                if stop_after == "mix":
                    stop_now = True
                    break
                if stop_after == "mix+":
                    skip_rest = True
                    break
                L = Region(R0)
                h2T = L.get([128, 8, S], BF16)
                T = Region(R2)
                xt = [T.get([128, D], F32) for i in range(3)]
                x1t = [T.get([128, D], F32) for i in range(3)]
                junk = T.get([128, D], BF16)
                xn = [T.get([128, D], BF16) for i in range(2)]
                ssq = [T.get([128, 8], F32) for i in range(3)]
                wo = pre.pop("wo")
                for t in range(17):
                    if t < 16:
                        xb, x1b = xt[t % 3], x1t[t % 3]
                        dma("sp", xb.ap, xs[128 * t:128 * t + 128, :], writes=[xb])
                        for cb in range(2):
                            p_ = psb[nxt("p4", 4)]
                            mm_group(p_.ap, p_, [(mixT.ap[:, k, AO + 128 * t:AO + 128 * t + 128], wo[cb].ap[:, k, :]) for k in range(8)], [mixT, wo[cb]])
                            op("dve", lambda e: e.tensor_tensor(out=x1b.ap[:, 512 * cb:512 * cb + 512], in0=p_.ap, in1=xb.ap[:, 512 * cb:512 * cb + 512], op=ALU.add), [p_, xb], [x1b])
                        dma("sp", x1scr[128 * t:128 * t + 128, :], x1b.ap, reads=[x1b])
                        rms_stage1(None, x1b, junk, ssq[t % 3])
                    if t >= 1:
                        rms_stage2(x1t[(t - 1) % 3], ssq[(t - 1) % 3], xn[(t - 1) % 2], rows.ap[:, D:2 * D], h2T, t - 1)
                if dbg:
                    dma("sp", dbg_d["d_h2T"], h2T.ap, reads=[h2T])
                sc.barrier()
                if stop_after == "x1":
                    stop_now = True
                    break
                if stop_after == "x1+":
                    skip_rest = True
                    break
            if stop_now:
                break
            if skip_rest:
                continue
            if stop_after == "conly":
                L = Region(R0)
                h2T = L.get([128, 8, S], BF16)
            T = Region(L.off)
            gT = T.get([128, 22, 1024], BF16)
            wdn = [T.get([128, D], BF16) for kk in range(22)]
            uT = [[T.get([128, 8 + 1024], BF16) for vg in range(2)] for i in range(2)]
            halo = T.get([128, NFC, 2], BF16)
            dgf = [T.get([128, 6, 128], BF16) for i in range(2)]
            sgf = [T.get([128, 512], F32) for i in range(2)]
            x1r = [T.get([128, D], F32) for i in range(2)]
            ot = [T.get([128, D], F32) for i in range(2)]
            UO = 8
            for i in range(2):
                for vg in range(2):
                    op("dve", lambda e: e.memset(uT[i][vg].ap[:, 0:UO], 0.0), [], [uT[i][vg]])
            for hf in range(2):
                for j in range(22):
                    wsl = ring[ring_i[0] % NRING]
                    ring_i[0] += 1
                    load_w(w_up[:, 128 * j:128 * j + 128], ncols=128, slot=wsl, col0=0)
                    load_w(w_up[:, DFF + 128 * j:DFF + 128 * j + 128], ncols=128, slot=wsl, col0=128)
                    if hf == 0:
                        dma("pool", wdn[j].ap, w_dn[128 * j:128 * j + 128, :], writes=[wdn[j]])
                    dg_ = dgf[j % 2]
                    for vg in range(2):
                        ch = j + 22 * vg
                        for jj in range(3):
                            op("dve", lambda e: e.tensor_scalar(out=dg_.ap[:, 3 * vg + jj, :], in0=ident.ap, scalar1=V(V_FCW + ch * 3 + jj), scalar2=None, op0=ALU.mult),
                               [ident, vecs], [dg_])
                    ub = uT[j % 2]
                    for vg in range(2):
                        ch = j + 22 * vg
                        if hf == 1:
                            op("dve", lambda e: e.tensor_copy(out=ub[vg].ap[:, UO - 2:UO], in_=halo.ap[:, ch, :]), [halo], [ub[vg]])
                        for tt in range(2):
                            tok = slice(1024 * hf + 512 * tt, 1024 * hf + 512 * tt + 512)
                            p_ = psb[nxt("p4", 4)]
                            mm_group(p_.ap, p_, [(wsl.ap[:, k, 128 * vg:128 * vg + 128], h2T.ap[:, k, tok]) for k in range(8)], [wsl, h2T])
                            op("act", lambda e: e.activation(out=ub[vg].ap[:, UO + 512 * tt:UO + 512 * tt + 512], in_=p_.ap, func=AF.Copy), [p_], [ub[vg]])
                        if hf == 0:
                            op("dve", lambda e: e.tensor_copy(out=halo.ap[:, ch, :], in_=ub[vg].ap[:, UO + 1022:UO + 1024]), [ub[vg]], [halo])
                    for tt in range(2):
                        pcv = psb[4 + nxt("b1", 3)]
                        pcg = psb[4 + nxt("b1", 3)]
                        for vg, pc_ in ((1, pcg), (0, pcv)):
                            mm_group(pc_.ap, pc_, [(dg_.ap[:, 3 * vg + jj, :], ub[vg].ap[:, UO - 2 + 512 * tt + jj:UO - 2 + 512 * tt + jj + 512]) for jj in range(3)], [dg_, ub[vg]])
                        s_ = sgf[tt % 2]
                        op("act", lambda e: e.activation(out=s_.ap, in_=pcg.ap, func=AF.Silu, bias=V(V_FCB + 22 + j)), [pcg, vecs], [s_])
                        op("dve", lambda e: e.scalar_tensor_tensor(out=gT.ap[:, j, 512 * tt:512 * tt + 512], in0=pcv.ap, scalar=V(V_FCB + j), in1=s_.ap, op0=ALU.add, op1=ALU.mult),
                           [pcv, vecs, s_], [gT])
                if dbg and hf == 0:
                    dma("sp", dbg_d["d_gT"], gT.ap, reads=[gT])
                for t8 in range(8):
                    t = 8 * hf + t8
                    xr, ob_ = x1r[t8 % 2], ot[t8 % 2]
                    dma("sp", xr.ap, x1scr[128 * t:128 * t + 128, :], writes=[xr])
                    for cb in range(2):
                        p_ = psb[nxt("p4", 4)]
                        mm_group(p_.ap, p_, [(gT.ap[:, j, 128 * t8:128 * t8 + 128], wdn[j].ap[:, 512 * cb:512 * cb + 512]) for j in range(22)], [gT] + wdn)
                        op("dve", lambda e: e.tensor_tensor(out=ob_.ap[:, 512 * cb:512 * cb + 512], in0=p_.ap, in1=xr.ap[:, 512 * cb:512 * cb + 512], op=ALU.add), [p_, xr], [ob_])
                    dma("sp", outs[128 * t:128 * t + 128, :], ob_.ap, reads=[ob_])
            if s + 1 < nseq:
                pre["wv"] = [load_w(w_in[:, 4096 + 512 * cb:4096 + 512 * cb + 512]) for cb in range(2)]
            sc.barrier()
        sc.barrier(engines=("sp",))
    return nc


def _host_layout(inputs):
    f = lambda a: np.ascontiguousarray(np.asarray(a, dtype=np.float32))
    vecs = np.zeros((128, NV), np.float32)
    vecs[:, V_GATEB:V_GATEB + 16] = f(inputs["gate_b"][0]).reshape(16, 128).T
    vecs[:, V_CONVB:V_CONVB + 8] = f(inputs["conv_b"][0]).reshape(8, 128).T
    vecs[:, V_CNG:V_CNG + 8] = f(inputs["conv_norm_g"][0]).reshape(8, 128).T
    vecs[:, V_FCB:V_FCB + NFC] = f(inputs["ffn_conv_b"][0]).reshape(NFC, 128).T
    vecs[:, V_CW:V_CW + 248] = f(inputs["conv_w"][0]).reshape(31, 8, 128).transpose(2, 1, 0).reshape(128, 248)
    vecs[:, V_FCW:V_FCW + NFC * 3] = f(inputs["ffn_conv_w"][0]).reshape(3, NFC, 128).transpose(2, 1, 0).reshape(128, NFC * 3)
    vecs[:, V_QG] = np.tile(f(inputs["q_norm_g"][0]), 2)
    vecs[:, V_KG] = np.tile(f(inputs["k_norm_g"][0]), 2)
    rows = np.zeros((128, 2 * D), np.float32)
    rows[:, 0:D] = np.broadcast_to(f(inputs["norm1_g"][0])[None, :], (128, D))
    rows[:, D:] = np.broadcast_to(f(inputs["norm2_g"][0])[None, :], (128, D))
    cst = np.zeros((128, NCST), np.float32)
    cst[:, C_ID:C_ID + 128] = np.eye(128, dtype=np.float32)
    cst[0:64, C_BLK:C_BLK + 64] = 1.0
    cst[64:128, C_BLK + 64:C_BLK + 128] = 1.0
    k_i = np.arange(128)[:, None]
    q_i = np.arange(128)[None, :]
    cst[:, C_ST:C_ST + 128] = q_i + 128 - k_i
    cst[:, C_ST + 128:C_ST + 256] = np.maximum(q_i - k_i, 0)
    cst[:, C_MK:C_MK + 128] = (k_i >= q_i)
    cst[:, C_MK + 128:C_MK + 256] = (q_i >= k_i)
    shared = {
        "w_in": f(inputs["w_in"][0]), "w_conv_out": f(inputs["w_conv_out"][0]), "w_attn_out": f(inputs["w_attn_out"][0]),
        "w_out": f(inputs["w_out"][0]), "w_up": f(inputs["w_up"][0]), "w_down": f(inputs["w_down"][0]),
        "vecs": vecs, "rows": rows, "consts": cst,
    }
    return shared


def kernel(**inputs):
    x = np.asarray(inputs["x"], dtype=np.float32)
    B = x.shape[0]
    shared = _host_layout(inputs)
    nc = build_nc(SEQ_PER_CORE)
    in_maps = []
    for c in range(NCORES):
        m = dict(shared)
        m["x"] = np.ascontiguousarray(x[c * SEQ_PER_CORE:(c + 1) * SEQ_PER_CORE].reshape(SEQ_PER_CORE * S, D))
        in_maps.append(m)
    res = run_bass_kernel_spmd(nc, in_maps, core_ids=list(range(NCORES)))
    out = np.concatenate([np.asarray(r["out"], dtype=np.float32).reshape(SEQ_PER_CORE, S, D) for r in res.results], axis=0)
    return out
```

```python
import numpy as np
import concourse.bass as bass
import concourse.mybir as mybir
from concourse.bass_utils import run_bass_kernel_spmd
from contextlib import ExitStack

F32 = mybir.dt.float32
BF16 = mybir.dt.bfloat16
ALU = mybir.AluOpType
AF = mybir.ActivationFunctionType

S = 2048
D = 1024
NH = 16
DFF = 2816
IN_COLS = 7168
EPS = 1e-6
NCORES = 8
SEQ_PER_CORE = 2
NFC = 2 * DFF // 128
ND = 10
KB = 1024

V_GATEB = 0
V_CONVB = 16
V_CNG = 24
V_FCB = 32
V_CW = 76
V_FCW = V_CW + 8 * 31
V_QG = V_FCW + NFC * 3
V_KG = V_QG + 1
NV = V_KG + 1

C_ID, C_BLK, C_ST, C_MK, NCST = 0, 128, 256, 512, 768


class Buf:
    __slots__ = ("ap", "w", "r")

    def __init__(self, ap):
        self.ap = ap
        self.w = None
        self.r = {}


class Sched:
    def __init__(self, nc, stack):
        self.nc = nc
        self.eng = dict(pe=nc.tensor, act=nc.scalar, dve=nc.vector, pool=nc.gpsimd, sp=nc.sync)
        self.msem = {e: stack.enter_context(nc.semaphore("ms_" + e)) for e in ("pe", "act", "dve", "pool")}
        self.mcnt = {e: 0 for e in self.msem}
        self.dsem = {q: [stack.enter_context(nc.semaphore("d_%s_%d" % (q, i))) for i in range(ND)] for q in ("sp", "pool")}
        self.dcnt = {q: [0] * ND for q in self.dsem}
        self.dnext = {q: 0 for q in self.dsem}
        self.seen = {e: {} for e in self.eng}
        self.pending = {e: False for e in self.msem}

    def _wait(self, e, ev):
        sem, val, key, prod = ev
        if self.seen[e].get(key, 0) >= val:
            return
        self.eng[e].wait_ge(sem, val)
        self.seen[e][key] = val

    def _deps(self, e, reads, writes):
        for b in reads:
            if b.w is not None:
                if b.w[3] == e and e == "pe":
                    continue
                self._wait(e, b.w)
        for b in writes:
            if b.w is not None and not (b.w[3] == e and e == "pe"):
                self._wait(e, b.w)
            for ev in b.r.values():
                if not (ev[3] == e and e == "pe"):
                    self._wait(e, ev)

    def _record(self, ev, reads, writes):
        for b in reads:
            b.r[ev[2]] = ev
        for b in writes:
            b.w = ev
            b.r = {}

    def op(self, e, fn, reads=(), writes=(), signal=True):
        self._deps(e, reads, writes)
        ins = fn(self.eng[e])
        if signal:
            self.mcnt[e] += 1
            ins.then_inc(self.msem[e], 1)
            ev = (self.msem[e], self.mcnt[e], e, e)
            self.pending[e] = False
        else:
            ev = (self.msem[e], self.mcnt[e] + 1, e, e)
            self.pending[e] = True
        self._record(ev, reads, writes)
        return ins

    def dma(self, q, out_ap, in_ap, reads=(), writes=()):
        i = self.dnext[q]
        self.dnext[q] = (i + 1) % ND
        sem = self.dsem[q][i]
        key = "d_%s_%d" % (q, i)
        if self.dcnt[q][i] > 0:
            self._wait(q, (sem, self.dcnt[q][i], key, "dma"))
        self._deps(q, reads, writes)
        self.eng[q].dma_start(out=out_ap, in_=in_ap).then_inc(sem, 16)
        self.dcnt[q][i] += 16
        ev = (sem, self.dcnt[q][i], key, "dma")
        self._record(ev, reads, writes)

    def all_events(self):
        evs = []
        for e in self.msem:
            assert not self.pending[e], e
            if self.mcnt[e] > 0:
                evs.append((self.msem[e], self.mcnt[e], e, e))
        for q in self.dsem:
            for i in range(ND):
                if self.dcnt[q][i] > 0:
                    evs.append((self.dsem[q][i], self.dcnt[q][i], "d_%s_%d" % (q, i), "dma"))
        return evs

    def barrier(self, engines=("pe", "act", "dve", "pool", "sp")):
        evs = self.all_events()
        for e in engines:
            for ev in evs:
                self._wait(e, ev)


def build_nc(nseq=SEQ_PER_CORE, dbg=False, stop_after=None):
    nc = bass.Bass("TRN2", target_bir_lowering=False)
    dt = nc.dram_tensor
    x_d = dt("x", [nseq * S, D], F32, kind="ExternalInput").ap()
    w_in = dt("w_in", [D, IN_COLS], F32, kind="ExternalInput").ap()
    w_co = dt("w_conv_out", [D, D], F32, kind="ExternalInput").ap()
    w_ao = dt("w_attn_out", [D, D], F32, kind="ExternalInput").ap()
    w_out = dt("w_out", [D, D], F32, kind="ExternalInput").ap()
    w_up = dt("w_up", [D, 2 * DFF], F32, kind="ExternalInput").ap()
    w_dn = dt("w_down", [DFF, D], F32, kind="ExternalInput").ap()
    vecs_d = dt("vecs", [128, NV], F32, kind="ExternalInput").ap()
    rows_d = dt("rows", [128, 2 * D], F32, kind="ExternalInput").ap()
    cst_d = dt("consts", [128, NCST], F32, kind="ExternalInput").ap()
    out_d = dt("out", [nseq * S, D], F32, kind="ExternalOutput").ap()
    vscr = dt("vscr", [S, 8, 256], BF16, kind="Internal").ap()
    x1scr = dt("x1scr", [S, D], F32, kind="Internal").ap()
    dbg_d = {}
    if dbg:
        for nm, shp, ty in (("d_h1T", [128, 8, S], BF16), ("d_qT", [128, S], BF16), ("d_kT", [128, S], BF16),
                            ("d_oT", [128, 8, S], BF16), ("d_saT", [128, 8, S], BF16), ("d_mixT", [128, 8, S], BF16),
                            ("d_h2T", [128, 8, S], BF16), ("d_gT", [128, 22, 1024], BF16)):
            dbg_d[nm] = dt(nm, shp, ty, kind="ExternalOutput").ap()

    with ExitStack() as top:
        sc = Sched(nc, top)
        op, dma = sc.op, sc.dma
        base0 = nc.sbuf_base
        base = ((base0 + 63) // 64) * 64
        ARENA = 204 * KB
        top.enter_context(nc.sbuf_tensor("arena", [128, (ARENA + base - base0 + 64) // 2], BF16))
        uid = [0]

        def at(off, shape, dtype):
            uid[0] += 1
            n = 1
            for d_ in shape[1:]:
                n *= d_
            nbytes = n * (4 if dtype == F32 else 2)
            assert off % 32 == 0 and off + nbytes <= ARENA, (off, nbytes)
            h = nc.alloc_sbuf_tensor_at("t%d" % uid[0], list(shape), dtype, offset=base + off)
            return Buf(h.ap()), off + ((nbytes + 31) // 32) * 32

        class Region:
            def __init__(self, off):
                self.off = off

            def get(self, shape, dtype):
                b, self.off = at(self.off, shape, dtype)
                return b

        G = Region(0)
        cst = G.get([128, NCST], F32)
        vecs = G.get([128, NV], F32)
        rows = G.get([128, 2 * D], F32)
        ident = G.get([128, 128], BF16)
        blk = G.get([128, 128], BF16)
        ones = G.get([128, 128], BF16)
        NRING = 3
        ring = [G.get([128, 8, 512], BF16) for i in range(NRING)]
        ring_i = [0]
        R0 = G.off
        psb = [Buf(top.enter_context(nc.psum_tensor("ps%d" % i, [128, 512], F32)).ap()) for i in range(7)]
        pst = Buf(top.enter_context(nc.psum_tensor("pst", [128, 1024], BF16)).ap())

        def V(col, n=1):
            return vecs.ap[:, col:col + n]

        dma("sp", cst.ap, cst_d, writes=[cst])
        dma("sp", vecs.ap, vecs_d, writes=[vecs])
        dma("sp", rows.ap, rows_d, writes=[rows])
        op("dve", lambda e: e.tensor_copy(out=ident.ap, in_=cst.ap[:, C_ID:C_ID + 128]), [cst], [ident])
        op("dve", lambda e: e.tensor_copy(out=blk.ap, in_=cst.ap[:, C_BLK:C_BLK + 128]), [cst], [blk])
        op("dve", lambda e: e.memset(ones.ap, 1.0), [], [ones])
        op("dve", lambda e: e.tensor_scalar(out=V(V_KG), in0=V(V_KG), scalar1=8.0, scalar2=None, op0=ALU.mult), [vecs], [vecs])

        def eidx(h, gi):
            return 16 - (h + 1) + 4 * gi

        def load_w(src_ap, ncols=512, kchunks=8, slot=None, col0=0):
            if slot is None:
                slot = ring[ring_i[0] % NRING]
                ring_i[0] += 1
            dma("pool", slot.ap[:, 0:kchunks, col0:col0 + ncols], src_ap.rearrange("(k p) c -> p k c", p=128), writes=[slot])
            return slot

        def mm_group(out_ap, out_buf, pairs, reads, start=True, sig=True):
            n = len(pairs)
            for i, (l, r) in enumerate(pairs):
                op("pe", lambda e: e.matmul(out_ap, lhsT=l, rhs=r, start=(start and i == 0), stop=(i == n - 1), skip_group_check=True),
                   reads, [out_buf], signal=(sig and i == n - 1))

        rrc = {"p4": 0, "st": 0, "pb": 0, "o": 0, "pq": 0, "b1": 0}

        def nxt(key, n):
            v = rrc[key] % n
            rrc[key] += 1
            return v

        def rms_stage1(src_dram_rows, xb, junk, sq):
            if src_dram_rows is not None:
                dma("sp", xb.ap, src_dram_rows, writes=[xb])
            op("act", lambda e: e.activation(out=junk.ap, in_=xb.ap, func=AF.Square, accum_out=sq.ap[:, 0:1]), [xb], [junk, sq])
            op("dve", lambda e: e.tensor_scalar(out=sq.ap[:, 1:2], in0=sq.ap[:, 0:1], scalar1=1.0 / D, scalar2=EPS, op0=ALU.mult, op1=ALU.add), [sq], [sq])
            op("act", lambda e: e.activation(out=sq.ap[:, 1:2], in_=sq.ap[:, 1:2], func=AF.Ln), [sq], [sq])
            op("act", lambda e: e.activation(out=sq.ap[:, 1:2], in_=sq.ap[:, 1:2], func=AF.Exp, scale=-0.5), [sq], [sq])

        def rms_stage2(xb, sq, xnb, grow_ap, dstT, t):
            op("dve", lambda e: e.scalar_tensor_tensor(out=xnb.ap, in0=xb.ap, scalar=sq.ap[:, 1:2], in1=grow_ap, op0=ALU.mult, op1=ALU.mult), [xb, sq, rows], [xnb])
            for k in range(8):
                op("pe", lambda e: e.transpose(out=pst.ap[:, 128 * k:128 * k + 128], in_=xnb.ap[:, 128 * k:128 * k + 128], identity=ident.ap),
                   [xnb, ident], [pst], signal=(k == 7))
            op("act", lambda e: e.activation(out=dstT.ap[:, :, 128 * t:128 * t + 128], in_=pst.ap.rearrange("p (k t) -> p k t", k=8), func=AF.Copy), [pst], [dstT])

        pre = {}
        for s in range(nseq):
            xs = x_d[s * S:(s + 1) * S, :]
            outs = out_d[s * S:(s + 1) * S, :]
            stop_now = False
            skip_rest = False
            for _once in ([] if stop_after == "conly" else [0]):
                L = Region(R0)
                h1T = L.get([128, 8, S], BF16)
                oT = L.get([128, 8, S], BF16)
                R1 = L.off
                T = Region(R1)
                xt = [T.get([128, D], F32) for i in range(3)]
                junk = T.get([128, D], BF16)
                xn = [T.get([128, D], BF16) for i in range(2)]
                ssq = [T.get([128, 8], F32) for i in range(3)]
                vst = [T.get([128, 8, 256], BF16) for i in range(2)]
                for i in range(2):
                    op("dve", lambda e: e.memset(vst[i].ap, 1.0), [], [vst[i]])
                wv = pre.pop("wv", None) or [load_w(w_in[:, 4096 + 512 * cb:4096 + 512 * cb + 512]) for cb in range(2)]
                for t in range(17):
                    if t < 16:
                        rms_stage1(xs[128 * t:128 * t + 128, :], xt[t % 3], junk, ssq[t % 3])
                    if t >= 1:
                        rms_stage2(xt[(t - 1) % 3], ssq[(t - 1) % 3], xn[(t - 1) % 2], rows.ap[:, 0:D], h1T, t - 1)
                for t in range(16):
                    vb = vst[t % 2]
                    for cb in range(2):
                        pb_ = psb[nxt("p4", 4)]
                        mm_group(pb_.ap, pb_, [(h1T.ap[:, k, 128 * t:128 * t + 128], wv[cb].ap[:, k, :]) for k in range(8)], [h1T, wv[cb]])
                        pv4 = pb_.ap.rearrange("p (h c) -> p h c", h=4)
                        op("act", lambda e: e.activation(out=vb.ap[:, 4 * cb:4 * cb + 4, 0:64], in_=pv4[:, :, 0:64], func=AF.Copy), [pb_], [vb])
                        op("dve", lambda e: e.tensor_copy(out=vb.ap[:, 4 * cb:4 * cb + 4, 192:256], in_=pv4[:, :, 64:128]), [pb_], [vb])
                    dma("sp", vscr[128 * t:128 * t + 128, :, :], vb.ap, reads=[vb])
                if dbg:
                    dma("sp", dbg_d["d_h1T"], h1T.ap, reads=[h1T])
                if stop_after not in ("v", "v+"):
                    pre["wq"] = load_w(w_in[:, 2048:2048 + 512])
                    pre["wk"] = load_w(w_in[:, 3072:3072 + 512])
                sc.barrier()
                if stop_after == "v":
                    stop_now = True
                    break
                if stop_after == "v+":
                    skip_rest = True
                    break
                T = Region(R1)
                etab = T.get([128, 24, 256], BF16)
                etmp = T.get([128, 256], F32)
                qT = [T.get([128, S], BF16) for i in range(2)]
                kT = [T.get([128, S], BF16) for i in range(2)]
                sqb = [T.get([128, 512], BF16) for i in range(2)]
                Rb = [T.get([128, 512], F32) for i in range(2)]
                varr = [[T.get([128, 16, 256], BF16) for g in range(3)] for i in range(2)]
                P3s = [T.get([128, S], BF16) for i in range(2)]
                Pb = [T.get([128, 512], BF16) for i in range(8)]
                rden = T.get([128, 512], F32)
                for ei in range(24):
                    c = 2.0 ** (ei / 2.0 - 8.0)
                    op("act", lambda e: e.activation(out=etmp.ap, in_=cst.ap[:, C_ST:C_ST + 256], func=AF.Exp, scale=-c), [cst], [etmp])
                    op("dve", lambda e: e.tensor_tensor(out=etab.ap[:, ei, :], in0=etmp.ap, in1=cst.ap[:, C_MK:C_MK + 256], op=ALU.mult), [etmp, cst], [etab])
                wq = wk = None
                pending = []

                def tick():
                    for p_ in pending:
                        p_[0] -= 1
                    while pending and pending[0][0] <= 0:
                        pending.pop(0)[1]()

                def flush():
                    while pending:
                        pending.pop(0)[1]()

                for hp in range(8):
                    if hp == 0:
                        wq, wk = pre.pop("wq"), pre.pop("wk")
                    elif hp % 4 == 0:
                        wq = load_w(w_in[:, 2048 + 128 * hp:2048 + 128 * hp + 512])
                        wk = load_w(w_in[:, 3072 + 128 * hp:3072 + 128 * hp + 512])
                    va = varr[hp % 2]
                    for g, r in enumerate((1, 4, 16)):
                        if r == 1:
                            dma("sp", va[g].ap, vscr.rearrange("(b i) h c -> i b h c", i=128)[:, :, hp, :], writes=[va[g]])
                        else:
                            for rho in range(r):
                                nb = 16 // r
                                src = vscr.rearrange("(b i r) h c -> r i b h c", r=r, i=128)[rho, :, :, hp, :]
                                dma("sp", va[g].ap[:, rho * nb:(rho + 1) * nb, :], src, writes=[va[g]])
                    qb, kb = qT[hp % 2], kT[hp % 2]
                    coff = 128 * (hp % 4)
                    for (wt_, dstb, gcol) in ((wq, qb, V_QG), (wk, kb, V_KG)):
                        for tt in range(4):
                            pa_ = psb[4 + nxt("pq", 2)]
                            sq_, R_ = sqb[tt % 2], Rb[tt % 2]
                            mm_group(pa_.ap, pa_, [(wt_.ap[:, k, coff:coff + 128], h1T.ap[:, k, 512 * tt:512 * tt + 512]) for k in range(8)], [wt_, h1T])
                            op("act", lambda e: e.activation(out=sq_.ap, in_=pa_.ap, func=AF.Square), [pa_], [sq_])
                            p6 = psb[6]
                            mm_group(p6.ap, p6, [(blk.ap, sq_.ap)], [blk, sq_])
                            op("dve", lambda e: e.tensor_scalar(out=R_.ap, in0=p6.ap, scalar1=64.0 * EPS, scalar2=None, op0=ALU.add), [p6], [R_])
                            op("act", lambda e: e.activation(out=R_.ap, in_=R_.ap, func=AF.Ln), [R_], [R_])
                            op("act", lambda e: e.activation(out=R_.ap, in_=R_.ap, func=AF.Exp, scale=-0.5), [R_], [R_])
                            op("dve", lambda e: e.scalar_tensor_tensor(out=dstb.ap[:, 512 * tt:512 * tt + 512], in0=pa_.ap, scalar=V(gcol), in1=R_.ap, op0=ALU.mult, op1=ALU.mult),
                               [pa_, vecs, R_], [dstb])
                    if dbg and hp == 0:
                        dma("sp", dbg_d["d_qT"], qb.ap, reads=[qb])
                        dma("sp", dbg_d["d_kT"], kb.ap, reads=[kb])
                    for e_ in range(2):
                        h = 2 * hp + e_
                        p0 = 64 * e_
                        qh = qb.ap[p0:p0 + 64, :]
                        kh = kb.ap[p0:p0 + 64, :]
                        P3 = P3s[h % 2]

                        def stile(dst_ap, dst_buf, kcols, qcols, sig, kh=kh, qh=qh, kb=kb, qb=qb):
                            op("pe", lambda e: e.matmul(dst_ap, lhsT=kh[:, kcols], rhs=qh[:, qcols], start=True, stop=True, skip_group_check=True),
                               [kb, qb], [dst_buf], signal=sig)

                        def make_pv(units, c, e_=e_, p0=p0, hp=hp, va=va):
                            def emit():
                                ob = psb[2 + nxt("o", 2)]
                                nu = len(units)
                                for ui, (pbuf, p_ap, gi, tile_i, ocols) in enumerate(units):
                                    op("pe", lambda e: e.matmul(ob.ap[:, ocols], lhsT=va[gi].ap[:, tile_i, 128 * e_:128 * e_ + 128], rhs=p_ap,
                                                                start=(ui == 0), stop=(ui == nu - 1), skip_group_check=True),
                                       [va[gi], pbuf], [ob], signal=(ui == nu - 1))
                                dq = 64 * (1 - e_)
                                op("act", lambda e: e.activation(out=rden.ap[p0:p0 + 64, :], in_=ob.ap[dq:dq + 64, :], func=AF.Ln), [ob], [rden])
                                op("act", lambda e: e.activation(out=rden.ap[p0:p0 + 64, :], in_=rden.ap[p0:p0 + 64, :], func=AF.Exp, scale=-1.0), [rden], [rden])
                                op("dve", lambda e: e.tensor_tensor(out=oT.ap[p0:p0 + 64, hp, 512 * c:512 * c + 512], in0=ob.ap[p0:p0 + 64, :], in1=rden.ap[p0:p0 + 64, :], op=ALU.mult),
                                   [ob, rden], [oT])
                            return emit

                        e3 = eidx(h, 2)
                        for ch in range(4):
                            stb = psb[(0, 1, 6, 4, 5)[nxt("st", 5)]]
                            for rr in range(4):
                                rho = 4 * ch + rr
                                stile(stb.ap[:, 128 * rr:128 * rr + 128], stb, slice(rho, S, 16), slice(rho, S, 16), rr == 3)
                            pv = P3.ap[:, 512 * ch:512 * ch + 512]
                            pv3 = pv.rearrange("p (a b) -> p a b", a=4)
                            e_ap = etab.ap[:, e3:e3 + 1, 128:256].to_broadcast([128, 4, 128])
                            op("act", lambda e: e.activation(out=pv, in_=stb.ap, func=AF.Exp), [stb], [P3])
                            op("dve", lambda e: e.tensor_tensor(out=pv3, in0=pv3, in1=e_ap, op=ALU.mult), [P3, etab], [P3])
                            tick()
                        for c in range(4):
                            units = []
                            for gi, r in ((0, 1), (1, 4)):
                                ei = eidx(h, gi)
                                for half in range(2):
                                    stb = psb[(0, 1, 6, 4, 5)[nxt("st", 5)]]
                                    pbuf = Pb[nxt("pb", 8)]
                                    for jj in range(2):
                                        if gi == 0:
                                            b = 4 * c + 2 * half + jj
                                            has_prev = b > 0
                                            kprev = slice(128 * (b - 1), 128 * b)
                                            kown = slice(128 * b, 128 * b + 128)
                                            ocols = slice(128 * (b - 4 * c), 128 * (b - 4 * c) + 128)
                                            tprev, town = b - 1, b
                                        else:
                                            rho = 2 * half + jj
                                            has_prev = c > 0
                                            kprev = slice(rho + 512 * (c - 1), 512 * c, 4)
                                            kown = slice(rho + 512 * c, 512 * c + 512, 4)
                                            ocols = slice(rho, 512, 4)
                                            tprev, town = rho * 4 + c - 1, rho * 4 + c
                                        o0 = 256 * jj
                                        if has_prev:
                                            stile(stb.ap[:, o0:o0 + 128], stb, kprev, kown, False)
                                            units.append((pbuf, pbuf.ap[:, o0:o0 + 128], gi, tprev, ocols))
                                        stile(stb.ap[:, o0 + 128:o0 + 256], stb, kown, kown, jj == 1)
                                        units.append((pbuf, pbuf.ap[:, o0 + 128:o0 + 256], gi, town, ocols))
                                    e_ap = etab.ap[:, ei:ei + 1, :].to_broadcast([128, 2, 256])
                                    pb3 = pbuf.ap.rearrange("p (a b) -> p a b", a=2)
                                    op("act", lambda e: e.activation(out=pbuf.ap, in_=stb.ap, func=AF.Exp), [stb], [pbuf])
                                    op("dve", lambda e: e.tensor_tensor(out=pb3, in0=pb3, in1=e_ap, op=ALU.mult), [pbuf, etab], [pbuf])
                                    tick()
                            for rho in range(16):
                                units.append((P3, P3.ap[:, 128 * rho + 32 * c:128 * rho + 32 * c + 32], 2, rho, slice(rho, 512, 16)))
                            pending.append([2, make_pv(units, c)])
                    flush()
                if dbg:
                    dma("sp", dbg_d["d_oT"], oT.ap, reads=[oT])
                if stop_after not in ("attn", "attn+"):
                    pre["wa"] = load_w(w_in[:, 0:512])
                    pre["wg"] = load_w(w_in[:, 1024:1024 + 512])
                sc.barrier()
                if stop_after == "attn":
                    stop_now = True
                    break
                if stop_after == "attn+":
                    skip_rest = True
                    break
                T = Region(R1)
                aT = T.get([128, 8, 32 + S], BF16)
                R2 = T.off
                cvT = T.get([128, 8, S], BF16)
                dgb = T.get([128, 31, 128], BF16)
                sg = [T.get([128, 512], F32) for i in range(2)]
                sgm = [T.get([128, 512], F32) for i in range(2)]
                sqv = [T.get([128, 512], BF16) for i in range(2)]
                rr_ = T.get([128, S], F32)
                gsb = [T.get([128, 512], F32) for i in range(2)]
                t1 = [T.get([128, 512], F32) for i in range(2)]
                AO = 32
                op("dve", lambda e: e.memset(aT.ap[:, :, 0:AO], 0.0), [], [aT])
                for cc in range(8):
                    if cc == 0:
                        wa, wg = pre.pop("wa"), pre.pop("wg")
                    elif cc % 4 == 0:
                        wa = load_w(w_in[:, 128 * cc:128 * cc + 512])
                        wg = load_w(w_in[:, 1024 + 128 * cc:1024 + 128 * cc + 512])
                    coff = 128 * (cc % 4)
                    for j in range(31):
                        op("dve", lambda e: e.tensor_scalar(out=dgb.ap[:, j, :], in0=ident.ap, scalar1=V(V_CW + cc * 31 + j), scalar2=None, op0=ALU.mult),
                           [ident, vecs], [dgb])
                    for tt in range(4):
                        pg_ = psb[4 + nxt("b1", 3)]
                        pv_ = psb[4 + nxt("b1", 3)]
                        mm_group(pg_.ap, pg_, [(wg.ap[:, k, coff:coff + 128], h1T.ap[:, k, 512 * tt:512 * tt + 512]) for k in range(8)], [wg, h1T])
                        mm_group(pv_.ap, pv_, [(wa.ap[:, k, coff:coff + 128], h1T.ap[:, k, 512 * tt:512 * tt + 512]) for k in range(8)], [wa, h1T])
                        sg_ = sg[tt % 2]
                        op("act", lambda e: e.activation(out=sg_.ap, in_=pg_.ap, func=AF.Sigmoid), [pg_], [sg_])
                        op("dve", lambda e: e.tensor_tensor(out=aT.ap[:, cc, AO + 512 * tt:AO + 512 * tt + 512], in0=pv_.ap, in1=sg_.ap, op=ALU.mult), [pv_, sg_], [aT])
                    for tt in range(4):
                        pc_ = psb[4 + nxt("b1", 3)]
                        mm_group(pc_.ap, pc_, [(dgb.ap[:, j, :], aT.ap[:, cc, AO - 30 + 512 * tt + j:AO - 30 + 512 * tt + j + 512]) for j in range(31)], [dgb, aT])
                        op("act", lambda e: e.activation(out=cvT.ap[:, cc, 512 * tt:512 * tt + 512], in_=pc_.ap, func=AF.Identity, bias=V(V_CONVB + cc)), [pc_, vecs], [cvT])
                        sv = sqv[tt % 2]
                        op("act", lambda e: e.activation(out=sv.ap, in_=pc_.ap, func=AF.Square, bias=V(V_CONVB + cc)), [pc_, vecs], [sv])
                        op("pe", lambda e: e.matmul(psb[tt].ap, lhsT=ones.ap, rhs=sv.ap, start=(cc == 0), stop=(cc == 7), skip_group_check=True),
                           [ones, sv], [psb[tt]], signal=True)
                for tt in range(4):
                    op("dve", lambda e: e.tensor_scalar(out=rr_.ap[:, 512 * tt:512 * tt + 512], in0=psb[tt].ap, scalar1=1.0 / D, scalar2=EPS, op0=ALU.mult, op1=ALU.add), [psb[tt]], [rr_])
                op("act", lambda e: e.activation(out=rr_.ap, in_=rr_.ap, func=AF.Ln), [rr_], [rr_])
                op("act", lambda e: e.activation(out=rr_.ap, in_=rr_.ap, func=AF.Exp, scale=-0.5), [rr_], [rr_])
                def silu_unit(cc, tt):
                    sg_, sm_ = sg[tt % 2], sgm[tt % 2]
                    cs = slice(512 * tt, 512 * tt + 512)
                    op("dve", lambda e: e.scalar_tensor_tensor(out=sg_.ap, in0=cvT.ap[:, cc, cs], scalar=V(V_CNG + cc), in1=rr_.ap[:, cs], op0=ALU.mult, op1=ALU.mult),
                       [cvT, vecs, rr_], [sg_])
                    op("act", lambda e: e.activation(out=sm_.ap, in_=sg_.ap, func=AF.Sigmoid), [sg_], [sm_])
                    op("dve", lambda e: e.tensor_tensor(out=cvT.ap[:, cc, cs], in0=sg_.ap, in1=sm_.ap, op=ALU.mult), [sg_, sm_], [cvT])

                silu_units = [(cc, tt) for cc in range(8) for tt in range(4)]
                mixT = aT
                for dc in range(8):
                    wsl = ring[ring_i[0] % NRING]
                    ring_i[0] += 1
                    load_w(w_in[:, 6144 + 128 * dc:6144 + 128 * dc + 128], ncols=128, slot=wsl, col0=0)
                    load_w(w_ao[:, 128 * dc:128 * dc + 128], ncols=128, slot=wsl, col0=128)
                    for tt in range(4):
                        cs = slice(512 * tt, 512 * tt + 512)
                        pg_ = psb[nxt("p4", 4)]
                        py_ = psb[nxt("p4", 4)]
                        mm_group(pg_.ap, pg_, [(wsl.ap[:, k, 0:128], h1T.ap[:, k, cs]) for k in range(8)], [wsl, h1T])
                        mm_group(py_.ap, py_, [(wsl.ap[:, k, 128:256], oT.ap[:, k, cs]) for k in range(8)], [wsl, oT])
                        g_ = gsb[0]
                        op("act", lambda e: e.activation(out=g_.ap, in_=pg_.ap, func=AF.Sigmoid, bias=V(V_GATEB + 8 + dc)), [pg_, vecs], [g_])
                        op("dve", lambda e: e.tensor_tensor(out=mixT.ap[:, dc, AO + 512 * tt:AO + 512 * tt + 512], in0=py_.ap, in1=g_.ap, op=ALU.mult), [py_, g_], [mixT])
                        if silu_units:
                            silu_unit(*silu_units.pop(0))
                while silu_units:
                    silu_unit(*silu_units.pop(0))
                if dbg:
                    dma("sp", dbg_d["d_saT"], cvT.ap, reads=[cvT])
                for dc in range(8):
                    wsl = ring[ring_i[0] % NRING]
                    ring_i[0] += 1
                    load_w(w_in[:, 5120 + 128 * dc:5120 + 128 * dc + 128], ncols=128, slot=wsl, col0=0)
                    load_w(w_co[:, 128 * dc:128 * dc + 128], ncols=128, slot=wsl, col0=128)
                    for tt in range(4):
                        cs = slice(512 * tt, 512 * tt + 512)
                        mcs = slice(AO + 512 * tt, AO + 512 * tt + 512)
                        pg_ = psb[nxt("p4", 4)]
                        py_ = psb[nxt("p4", 4)]
                        mm_group(pg_.ap, pg_, [(wsl.ap[:, k, 0:128], h1T.ap[:, k, cs]) for k in range(8)], [wsl, h1T])
                        mm_group(py_.ap, py_, [(wsl.ap[:, k, 128:256], cvT.ap[:, k, cs]) for k in range(8)], [wsl, cvT])
                        g_ = gsb[1]
                        t_ = t1[tt % 2]
                        op("act", lambda e: e.activation(out=g_.ap, in_=pg_.ap, func=AF.Sigmoid, bias=V(V_GATEB + dc)), [pg_, vecs], [g_])
                        op("dve", lambda e: e.tensor_tensor(out=t_.ap, in0=py_.ap, in1=g_.ap, op=ALU.mult), [py_, g_], [t_])
                        op("dve", lambda e: e.tensor_tensor(out=mixT.ap[:, dc, mcs], in0=mixT.ap[:, dc, mcs], in1=t_.ap, op=ALU.add), [mixT, t_], [mixT])
                if dbg:
                    dma("sp", dbg_d["d_mixT"], mixT.ap[:, :, AO:AO + S], reads=[mixT])
                if stop_after not in ("mix", "mix+"):
                    pre["wo"] = [load_w(w_out[:, 512 * cb:512 * cb + 512]) for cb in range(2)]
                sc.barrier()
                if stop_after == "mix":
                    stop_now = True
                    break
                if stop_after == "mix+":
                    skip_rest = True
                    break
                L = Region(R0)
                h2T = L.get([128, 8, S], BF16)
                T = Region(R2)
                xt = [T.get([128, D], F32) for i in range(3)]
                x1t = [T.get([128, D], F32) for i in range(3)]
                junk = T.get([128, D], BF16)
                xn = [T.get([128, D], BF16) for i in range(2)]
                ssq = [T.get([128, 8], F32) for i in range(3)]
                wo = pre.pop("wo")
                for t in range(17):
                    if t < 16:
                        xb, x1b = xt[t % 3], x1t[t % 3]
                        dma("sp", xb.ap, xs[128 * t:128 * t + 128, :], writes=[xb])
                        for cb in range(2):
                            p_ = psb[nxt("p4", 4)]
                            mm_group(p_.ap, p_, [(mixT.ap[:, k, AO + 128 * t:AO + 128 * t + 128], wo[cb].ap[:, k, :]) for k in range(8)], [mixT, wo[cb]])
                            op("dve", lambda e: e.tensor_tensor(out=x1b.ap[:, 512 * cb:512 * cb + 512], in0=p_.ap, in1=xb.ap[:, 512 * cb:512 * cb + 512], op=ALU.add), [p_, xb], [x1b])
                        dma("sp", x1scr[128 * t:128 * t + 128, :], x1b.ap, reads=[x1b])
                        rms_stage1(None, x1b, junk, ssq[t % 3])
                    if t >= 1:
                        rms_stage2(x1t[(t - 1) % 3], ssq[(t - 1) % 3], xn[(t - 1) % 2], rows.ap[:, D:2 * D], h2T, t - 1)
                if dbg:
                    dma("sp", dbg_d["d_h2T"], h2T.ap, reads=[h2T])
                sc.barrier()
                if stop_after == "x1":
                    stop_now = True
                    break
                if stop_after == "x1+":
                    skip_rest = True
                    break
            if stop_now:
                break
            if skip_rest:
                continue
            if stop_after == "conly":
                L = Region(R0)
                h2T = L.get([128, 8, S], BF16)
            T = Region(L.off)
            gT = T.get([128, 22, 1024], BF16)
            wdn = [T.get([128, D], BF16) for kk in range(22)]
            uT = [[T.get([128, 8 + 1024], BF16) for vg in range(2)] for i in range(2)]
            halo = T.get([128, NFC, 2], BF16)
            dgf = [T.get([128, 6, 128], BF16) for i in range(2)]
            sgf = [T.get([128, 512], F32) for i in range(2)]
            x1r = [T.get([128, D], F32) for i in range(2)]
            ot = [T.get([128, D], F32) for i in range(2)]
            UO = 8
            for i in range(2):
                for vg in range(2):
                    op("dve", lambda e: e.memset(uT[i][vg].ap[:, 0:UO], 0.0), [], [uT[i][vg]])
            for hf in range(2):
                for j in range(22):
                    wsl = ring[ring_i[0] % NRING]
                    ring_i[0] += 1
                    load_w(w_up[:, 128 * j:128 * j + 128], ncols=128, slot=wsl, col0=0)
                    load_w(w_up[:, DFF + 128 * j:DFF + 128 * j + 128], ncols=128, slot=wsl, col0=128)
                    if hf == 0:
                        dma("pool", wdn[j].ap, w_dn[128 * j:128 * j + 128, :], writes=[wdn[j]])
                    dg_ = dgf[j % 2]
                    for vg in range(2):
                        ch = j + 22 * vg
                        for jj in range(3):
                            op("dve", lambda e: e.tensor_scalar(out=dg_.ap[:, 3 * vg + jj, :], in0=ident.ap, scalar1=V(V_FCW + ch * 3 + jj), scalar2=None, op0=ALU.mult),
                               [ident, vecs], [dg_])
                    ub = uT[j % 2]
                    for vg in range(2):
                        ch = j + 22 * vg
                        if hf == 1:
                            op("dve", lambda e: e.tensor_copy(out=ub[vg].ap[:, UO - 2:UO], in_=halo.ap[:, ch, :]), [halo], [ub[vg]])
                        for tt in range(2):
                            tok = slice(1024 * hf + 512 * tt, 1024 * hf + 512 * tt + 512)
                            p_ = psb[nxt("p4", 4)]
                            mm_group(p_.ap, p_, [(wsl.ap[:, k, 128 * vg:128 * vg + 128], h2T.ap[:, k, tok]) for k in range(8)], [wsl, h2T])
                            op("act", lambda e: e.activation(out=ub[vg].ap[:, UO + 512 * tt:UO + 512 * tt + 512], in_=p_.ap, func=AF.Copy), [p_], [ub[vg]])
                        if hf == 0:
                            op("dve", lambda e: e.tensor_copy(out=halo.ap[:, ch, :], in_=ub[vg].ap[:, UO + 1022:UO + 1024]), [ub[vg]], [halo])
                    for tt in range(2):
                        pcv = psb[4 + nxt("b1", 3)]
                        pcg = psb[4 + nxt("b1", 3)]
                        for vg, pc_ in ((1, pcg), (0, pcv)):
                            mm_group(pc_.ap, pc_, [(dg_.ap[:, 3 * vg + jj, :], ub[vg].ap[:, UO - 2 + 512 * tt + jj:UO - 2 + 512 * tt + jj + 512]) for jj in range(3)], [dg_, ub[vg]])
                        s_ = sgf[tt % 2]
                        op("act", lambda e: e.activation(out=s_.ap, in_=pcg.ap, func=AF.Silu, bias=V(V_FCB + 22 + j)), [pcg, vecs], [s_])
                        op("dve", lambda e: e.scalar_tensor_tensor(out=gT.ap[:, j, 512 * tt:512 * tt + 512], in0=pcv.ap, scalar=V(V_FCB + j), in1=s_.ap, op0=ALU.add, op1=ALU.mult),
                           [pcv, vecs, s_], [gT])
                if dbg and hf == 0:
                    dma("sp", dbg_d["d_gT"], gT.ap, reads=[gT])
                for t8 in range(8):
                    t = 8 * hf + t8
                    xr, ob_ = x1r[t8 % 2], ot[t8 % 2]
                    dma("sp", xr.ap, x1scr[128 * t:128 * t + 128, :], writes=[xr])
                    for cb in range(2):
                        p_ = psb[nxt("p4", 4)]
                        mm_group(p_.ap, p_, [(gT.ap[:, j, 128 * t8:128 * t8 + 128], wdn[j].ap[:, 512 * cb:512 * cb + 512]) for j in range(22)], [gT] + wdn)
                        op("dve", lambda e: e.tensor_tensor(out=ob_.ap[:, 512 * cb:512 * cb + 512], in0=p_.ap, in1=xr.ap[:, 512 * cb:512 * cb + 512], op=ALU.add), [p_, xr], [ob_])
                    dma("sp", outs[128 * t:128 * t + 128, :], ob_.ap, reads=[ob_])
            if s + 1 < nseq:
                pre["wv"] = [load_w(w_in[:, 4096 + 512 * cb:4096 + 512 * cb + 512]) for cb in range(2)]
            sc.barrier()
        sc.barrier(engines=("sp",))
    return nc


def _host_layout(inputs):
    f = lambda a: np.ascontiguousarray(np.asarray(a, dtype=np.float32))
    vecs = np.zeros((128, NV), np.float32)
    vecs[:, V_GATEB:V_GATEB + 16] = f(inputs["gate_b"][0]).reshape(16, 128).T
    vecs[:, V_CONVB:V_CONVB + 8] = f(inputs["conv_b"][0]).reshape(8, 128).T
    vecs[:, V_CNG:V_CNG + 8] = f(inputs["conv_norm_g"][0]).reshape(8, 128).T
    vecs[:, V_FCB:V_FCB + NFC] = f(inputs["ffn_conv_b"][0]).reshape(NFC, 128).T
    vecs[:, V_CW:V_CW + 248] = f(inputs["conv_w"][0]).reshape(31, 8, 128).transpose(2, 1, 0).reshape(128, 248)
    vecs[:, V_FCW:V_FCW + NFC * 3] = f(inputs["ffn_conv_w"][0]).reshape(3, NFC, 128).transpose(2, 1, 0).reshape(128, NFC * 3)
    vecs[:, V_QG] = np.tile(f(inputs["q_norm_g"][0]), 2)
    vecs[:, V_KG] = np.tile(f(inputs["k_norm_g"][0]), 2)
    rows = np.zeros((128, 2 * D), np.float32)
    rows[:, 0:D] = np.broadcast_to(f(inputs["norm1_g"][0])[None, :], (128, D))
    rows[:, D:] = np.broadcast_to(f(inputs["norm2_g"][0])[None, :], (128, D))
    cst = np.zeros((128, NCST), np.float32)
    cst[:, C_ID:C_ID + 128] = np.eye(128, dtype=np.float32)
    cst[0:64, C_BLK:C_BLK + 64] = 1.0
    cst[64:128, C_BLK + 64:C_BLK + 128] = 1.0
    k_i = np.arange(128)[:, None]
    q_i = np.arange(128)[None, :]
    cst[:, C_ST:C_ST + 128] = q_i + 128 - k_i
    cst[:, C_ST + 128:C_ST + 256] = np.maximum(q_i - k_i, 0)
    cst[:, C_MK:C_MK + 128] = (k_i >= q_i)
    cst[:, C_MK + 128:C_MK + 256] = (q_i >= k_i)
    shared = {
        "w_in": f(inputs["w_in"][0]), "w_conv_out": f(inputs["w_conv_out"][0]), "w_attn_out": f(inputs["w_attn_out"][0]),
        "w_out": f(inputs["w_out"][0]), "w_up": f(inputs["w_up"][0]), "w_down": f(inputs["w_down"][0]),
        "vecs": vecs, "rows": rows, "consts": cst,
    }
    return shared


def kernel(**inputs):
    x = np.asarray(inputs["x"], dtype=np.float32)
    B = x.shape[0]
    shared = _host_layout(inputs)
    nc = build_nc(SEQ_PER_CORE)
    in_maps = []
    for c in range(NCORES):
        m = dict(shared)
        m["x"] = np.ascontiguousarray(x[c * SEQ_PER_CORE:(c + 1) * SEQ_PER_CORE].reshape(SEQ_PER_CORE * S, D))
        in_maps.append(m)
    res = run_bass_kernel_spmd(nc, in_maps, core_ids=list(range(NCORES)))
    out = np.concatenate([np.asarray(r["out"], dtype=np.float32).reshape(SEQ_PER_CORE, S, D) for r in res.results], axis=0)
    return out
```

```python
import numpy as np
import concourse.bass as bass
import concourse.mybir as mybir
from concourse.bass_utils import run_bass_kernel_spmd
from contextlib import ExitStack

F32 = mybir.dt.float32
BF16 = mybir.dt.bfloat16
ALU = mybir.AluOpType
AF = mybir.ActivationFunctionType

S = 2048
D = 1024
NH = 16
DFF = 2816
IN_COLS = 7168
EPS = 1e-6
NCORES = 8
SEQ_PER_CORE = 2
NFC = 2 * DFF // 128
ND = 10
KB = 1024

V_GATEB = 0
V_CONVB = 16
V_CNG = 24
V_FCB = 32
V_CW = 76
V_FCW = V_CW + 8 * 31
V_QG = V_FCW + NFC * 3
V_KG = V_QG + 1
NV = V_KG + 1

C_ID, C_BLK, C_ST, C_MK, NCST = 0, 128, 256, 512, 768


class Buf:
    __slots__ = ("ap", "w", "r")

    def __init__(self, ap):
        self.ap = ap
        self.w = None
        self.r = {}


class Sched:
    def __init__(self, nc, stack):
        self.nc = nc
        self.eng = dict(pe=nc.tensor, act=nc.scalar, dve=nc.vector, pool=nc.gpsimd, sp=nc.sync)
        self.msem = {e: stack.enter_context(nc.semaphore("ms_" + e)) for e in ("pe", "act", "dve", "pool")}
        self.mcnt = {e: 0 for e in self.msem}
        self.dsem = {q: [stack.enter_context(nc.semaphore("d_%s_%d" % (q, i))) for i in range(ND)] for q in ("sp", "pool")}
        self.dcnt = {q: [0] * ND for q in self.dsem}
        self.dnext = {q: 0 for q in self.dsem}
        self.seen = {e: {} for e in self.eng}
        self.pending = {e: False for e in self.msem}

    def _wait(self, e, ev):
        sem, val, key, prod = ev
        if self.seen[e].get(key, 0) >= val:
            return
        self.eng[e].wait_ge(sem, val)
        self.seen[e][key] = val

    def _deps(self, e, reads, writes):
        for b in reads:
            if b.w is not None:
                if b.w[3] == e and e == "pe":
                    continue
                self._wait(e, b.w)
        for b in writes:
            if b.w is not None and not (b.w[3] == e and e == "pe"):
                self._wait(e, b.w)
            for ev in b.r.values():
                if not (ev[3] == e and e == "pe"):
                    self._wait(e, ev)

    def _record(self, ev, reads, writes):
        for b in reads:
            b.r[ev[2]] = ev
        for b in writes:
            b.w = ev
            b.r = {}

    def op(self, e, fn, reads=(), writes=(), signal=True):
        self._deps(e, reads, writes)
        ins = fn(self.eng[e])
        if signal:
            self.mcnt[e] += 1
            ins.then_inc(self.msem[e], 1)
            ev = (self.msem[e], self.mcnt[e], e, e)
            self.pending[e] = False
        else:
            ev = (self.msem[e], self.mcnt[e] + 1, e, e)
            self.pending[e] = True
        self._record(ev, reads, writes)
        return ins

    def dma(self, q, out_ap, in_ap, reads=(), writes=()):
        i = self.dnext[q]
        self.dnext[q] = (i + 1) % ND
        sem = self.dsem[q][i]
        key = "d_%s_%d" % (q, i)
        if self.dcnt[q][i] > 0:
            self._wait(q, (sem, self.dcnt[q][i], key, "dma"))
        self._deps(q, reads, writes)
        self.eng[q].dma_start(out=out_ap, in_=in_ap).then_inc(sem, 16)
        self.dcnt[q][i] += 16
        ev = (sem, self.dcnt[q][i], key, "dma")
        self._record(ev, reads, writes)

    def all_events(self):
        evs = []
        for e in self.msem:
            assert not self.pending[e], e
            if self.mcnt[e] > 0:
                evs.append((self.msem[e], self.mcnt[e], e, e))
        for q in self.dsem:
            for i in range(ND):
                if self.dcnt[q][i] > 0:
                    evs.append((self.dsem[q][i], self.dcnt[q][i], "d_%s_%d" % (q, i), "dma"))
        return evs

    def barrier(self, engines=("pe", "act", "dve", "pool", "sp")):
        evs = self.all_events()
        for e in engines:
            for ev in evs:
                self._wait(e, ev)


def build_nc(nseq=SEQ_PER_CORE, dbg=False, stop_after=None):
    nc = bass.Bass("TRN2", target_bir_lowering=False)
    dt = nc.dram_tensor
    x_d = dt("x", [nseq * S, D], F32, kind="ExternalInput").ap()
    w_in = dt("w_in", [D, IN_COLS], F32, kind="ExternalInput").ap()
    w_co = dt("w_conv_out", [D, D], F32, kind="ExternalInput").ap()
    w_ao = dt("w_attn_out", [D, D], F32, kind="ExternalInput").ap()
    w_out = dt("w_out", [D, D], F32, kind="ExternalInput").ap()
    w_up = dt("w_up", [D, 2 * DFF], F32, kind="ExternalInput").ap()
    w_dn = dt("w_down", [DFF, D], F32, kind="ExternalInput").ap()
    vecs_d = dt("vecs", [128, NV], F32, kind="ExternalInput").ap()
    rows_d = dt("rows", [128, 2 * D], F32, kind="ExternalInput").ap()
    cst_d = dt("consts", [128, NCST], F32, kind="ExternalInput").ap()
    out_d = dt("out", [nseq * S, D], F32, kind="ExternalOutput").ap()
    vscr = dt("vscr", [S, 8, 256], BF16, kind="Internal").ap()
    x1scr = dt("x1scr", [S, D], F32, kind="Internal").ap()
    dbg_d = {}
    if dbg:
        for nm, shp, ty in (("d_h1T", [128, 8, S], BF16), ("d_qT", [128, S], BF16), ("d_kT", [128, S], BF16),
                            ("d_oT", [128, 8, S], BF16), ("d_saT", [128, 8, S], BF16), ("d_mixT", [128, 8, S], BF16),
                            ("d_h2T", [128, 8, S], BF16), ("d_gT", [128, 22, 1024], BF16)):
            dbg_d[nm] = dt(nm, shp, ty, kind="ExternalOutput").ap()

    with ExitStack() as top:
        sc = Sched(nc, top)
        op, dma = sc.op, sc.dma
        base0 = nc.sbuf_base
        base = ((base0 + 63) // 64) * 64
        ARENA = 204 * KB
        top.enter_context(nc.sbuf_tensor("arena", [128, (ARENA + base - base0 + 64) // 2], BF16))
        uid = [0]

        def at(off, shape, dtype):
            uid[0] += 1
            n = 1
            for d_ in shape[1:]:
                n *= d_
            nbytes = n * (4 if dtype == F32 else 2)
            assert off % 32 == 0 and off + nbytes <= ARENA, (off, nbytes)
            h = nc.alloc_sbuf_tensor_at("t%d" % uid[0], list(shape), dtype, offset=base + off)
            return Buf(h.ap()), off + ((nbytes + 31) // 32) * 32

        class Region:
            def __init__(self, off):
                self.off = off

            def get(self, shape, dtype):
                b, self.off = at(self.off, shape, dtype)
                return b

        G = Region(0)
        cst = G.get([128, NCST], F32)
        vecs = G.get([128, NV], F32)
        rows = G.get([128, 2 * D], F32)
        ident = G.get([128, 128], BF16)
        blk = G.get([128, 128], BF16)
        ones = G.get([128, 128], BF16)
        NRING = 3
        ring = [G.get([128, 8, 512], BF16) for i in range(NRING)]
        ring_i = [0]
        R0 = G.off
        psb = [Buf(top.enter_context(nc.psum_tensor("ps%d" % i, [128, 512], F32)).ap()) for i in range(7)]
        pst = Buf(top.enter_context(nc.psum_tensor("pst", [128, 1024], BF16)).ap())

        def V(col, n=1):
            return vecs.ap[:, col:col + n]

        dma("sp", cst.ap, cst_d, writes=[cst])
        dma("sp", vecs.ap, vecs_d, writes=[vecs])
        dma("sp", rows.ap, rows_d, writes=[rows])
        op("dve", lambda e: e.tensor_copy(out=ident.ap, in_=cst.ap[:, C_ID:C_ID + 128]), [cst], [ident])
        op("dve", lambda e: e.tensor_copy(out=blk.ap, in_=cst.ap[:, C_BLK:C_BLK + 128]), [cst], [blk])
        op("dve", lambda e: e.memset(ones.ap, 1.0), [], [ones])
        op("dve", lambda e: e.tensor_scalar(out=V(V_KG), in0=V(V_KG), scalar1=8.0, scalar2=None, op0=ALU.mult), [vecs], [vecs])

        def eidx(h, gi):
            return 16 - (h + 1) + 4 * gi

        def load_w(src_ap, ncols=512, kchunks=8, slot=None, col0=0):
            if slot is None:
                slot = ring[ring_i[0] % NRING]
                ring_i[0] += 1
            dma("pool", slot.ap[:, 0:kchunks, col0:col0 + ncols], src_ap.rearrange("(k p) c -> p k c", p=128), writes=[slot])
            return slot

        def mm_group(out_ap, out_buf, pairs, reads, start=True, sig=True):
            n = len(pairs)
            for i, (l, r) in enumerate(pairs):
                op("pe", lambda e: e.matmul(out_ap, lhsT=l, rhs=r, start=(start and i == 0), stop=(i == n - 1), skip_group_check=True),
                   reads, [out_buf], signal=(sig and i == n - 1))

        rrc = {"p4": 0, "st": 0, "pb": 0, "o": 0, "pq": 0, "b1": 0}

        def nxt(key, n):
            v = rrc[key] % n
            rrc[key] += 1
            return v

        def rms_stage1(src_dram_rows, xb, junk, sq):
            if src_dram_rows is not None:
                dma("sp", xb.ap, src_dram_rows, writes=[xb])
            op("act", lambda e: e.activation(out=junk.ap, in_=xb.ap, func=AF.Square, accum_out=sq.ap[:, 0:1]), [xb], [junk, sq])
            op("dve", lambda e: e.tensor_scalar(out=sq.ap[:, 1:2], in0=sq.ap[:, 0:1], scalar1=1.0 / D, scalar2=EPS, op0=ALU.mult, op1=ALU.add), [sq], [sq])
            op("act", lambda e: e.activation(out=sq.ap[:, 1:2], in_=sq.ap[:, 1:2], func=AF.Ln), [sq], [sq])
            op("act", lambda e: e.activation(out=sq.ap[:, 1:2], in_=sq.ap[:, 1:2], func=AF.Exp, scale=-0.5), [sq], [sq])

        def rms_stage2(xb, sq, xnb, grow_ap, dstT, t):
            op("dve", lambda e: e.scalar_tensor_tensor(out=xnb.ap, in0=xb.ap, scalar=sq.ap[:, 1:2], in1=grow_ap, op0=ALU.mult, op1=ALU.mult), [xb, sq, rows], [xnb])
            for k in range(8):
                op("pe", lambda e: e.transpose(out=pst.ap[:, 128 * k:128 * k + 128], in_=xnb.ap[:, 128 * k:128 * k + 128], identity=ident.ap),
                   [xnb, ident], [pst], signal=(k == 7))
            op("act", lambda e: e.activation(out=dstT.ap[:, :, 128 * t:128 * t + 128], in_=pst.ap.rearrange("p (k t) -> p k t", k=8), func=AF.Copy), [pst], [dstT])

        pre = {}
        for s in range(nseq):
            xs = x_d[s * S:(s + 1) * S, :]
            outs = out_d[s * S:(s + 1) * S, :]
            stop_now = False
            skip_rest = False
            for _once in ([] if stop_after == "conly" else [0]):
                L = Region(R0)
                h1T = L.get([128, 8, S], BF16)
                oT = L.get([128, 8, S], BF16)
                R1 = L.off
                T = Region(R1)
                xt = [T.get([128, D], F32) for i in range(3)]
                junk = T.get([128, D], BF16)
                xn = [T.get([128, D], BF16) for i in range(2)]
                ssq = [T.get([128, 8], F32) for i in range(3)]
                vst = [T.get([128, 8, 256], BF16) for i in range(2)]
                for i in range(2):
                    op("dve", lambda e: e.memset(vst[i].ap, 1.0), [], [vst[i]])
                wv = pre.pop("wv", None) or [load_w(w_in[:, 4096 + 512 * cb:4096 + 512 * cb + 512]) for cb in range(2)]
                for t in range(17):
                    if t < 16:
                        rms_stage1(xs[128 * t:128 * t + 128, :], xt[t % 3], junk, ssq[t % 3])
                    if t >= 1:
                        rms_stage2(xt[(t - 1) % 3], ssq[(t - 1) % 3], xn[(t - 1) % 2], rows.ap[:, 0:D], h1T, t - 1)
                for t in range(16):
                    vb = vst[t % 2]
                    for cb in range(2):
                        pb_ = psb[nxt("p4", 4)]
                        mm_group(pb_.ap, pb_, [(h1T.ap[:, k, 128 * t:128 * t + 128], wv[cb].ap[:, k, :]) for k in range(8)], [h1T, wv[cb]])
                        pv4 = pb_.ap.rearrange("p (h c) -> p h c", h=4)
                        op("act", lambda e: e.activation(out=vb.ap[:, 4 * cb:4 * cb + 4, 0:64], in_=pv4[:, :, 0:64], func=AF.Copy), [pb_], [vb])
                        op("dve", lambda e: e.tensor_copy(out=vb.ap[:, 4 * cb:4 * cb + 4, 192:256], in_=pv4[:, :, 64:128]), [pb_], [vb])
                    dma("sp", vscr[128 * t:128 * t + 128, :, :], vb.ap, reads=[vb])
                if dbg:
                    dma("sp", dbg_d["d_h1T"], h1T.ap, reads=[h1T])
                if stop_after not in ("v", "v+"):
                    pre["wq"] = load_w(w_in[:, 2048:2048 + 512])
                    pre["wk"] = load_w(w_in[:, 3072:3072 + 512])
                sc.barrier()
                if stop_after == "v":
                    stop_now = True
                    break
                if stop_after == "v+":
                    skip_rest = True
                    break
                T = Region(R1)
                etab = T.get([128, 24, 256], BF16)
                etmp = T.get([128, 256], F32)
                qT = [T.get([128, S], BF16) for i in range(2)]
                kT = [T.get([128, S], BF16) for i in range(2)]
                sqb = [T.get([128, 512], BF16) for i in range(2)]
                Rb = [T.get([128, 512], F32) for i in range(2)]
                varr = [[T.get([128, 16, 256], BF16) for g in range(3)] for i in range(2)]
                P3s = [T.get([128, S], BF16) for i in range(2)]
                Pb = [T.get([128, 512], BF16) for i in range(8)]
                rden = T.get([128, 512], F32)
                for ei in range(24):
                    c = 2.0 ** (ei / 2.0 - 8.0)
                    op("act", lambda e: e.activation(out=etmp.ap, in_=cst.ap[:, C_ST:C_ST + 256], func=AF.Exp, scale=-c), [cst], [etmp])
                    op("dve", lambda e: e.tensor_tensor(out=etab.ap[:, ei, :], in0=etmp.ap, in1=cst.ap[:, C_MK:C_MK + 256], op=ALU.mult), [etmp, cst], [etab])
                wq = wk = None
                pending = []

                def tick():
                    for p_ in pending:
                        p_[0] -= 1
                    while pending and pending[0][0] <= 0:
                        pending.pop(0)[1]()

                def flush():
                    while pending:
                        pending.pop(0)[1]()

                for hp in range(8):
                    if hp == 0:
                        wq, wk = pre.pop("wq"), pre.pop("wk")
                    elif hp % 4 == 0:
                        wq = load_w(w_in[:, 2048 + 128 * hp:2048 + 128 * hp + 512])
                        wk = load_w(w_in[:, 3072 + 128 * hp:3072 + 128 * hp + 512])
                    va = varr[hp % 2]
                    for g, r in enumerate((1, 4, 16)):
                        if r == 1:
                            dma("sp", va[g].ap, vscr.rearrange("(b i) h c -> i b h c", i=128)[:, :, hp, :], writes=[va[g]])
                        else:
                            for rho in range(r):
                                nb = 16 // r
                                src = vscr.rearrange("(b i r) h c -> r i b h c", r=r, i=128)[rho, :, :, hp, :]
                                dma("sp", va[g].ap[:, rho * nb:(rho + 1) * nb, :], src, writes=[va[g]])
                    qb, kb = qT[hp % 2], kT[hp % 2]
                    coff = 128 * (hp % 4)
                    for (wt_, dstb, gcol) in ((wq, qb, V_QG), (wk, kb, V_KG)):
                        for tt in range(4):
                            pa_ = psb[4 + nxt("pq", 2)]
                            sq_, R_ = sqb[tt % 2], Rb[tt % 2]
                            mm_group(pa_.ap, pa_, [(wt_.ap[:, k, coff:coff + 128], h1T.ap[:, k, 512 * tt:512 * tt + 512]) for k in range(8)], [wt_, h1T])
                            op("act", lambda e: e.activation(out=sq_.ap, in_=pa_.ap, func=AF.Square), [pa_], [sq_])
                            p6 = psb[6]
                            mm_group(p6.ap, p6, [(blk.ap, sq_.ap)], [blk, sq_])
                            op("dve", lambda e: e.tensor_scalar(out=R_.ap, in0=p6.ap, scalar1=64.0 * EPS, scalar2=None, op0=ALU.add), [p6], [R_])
                            op("act", lambda e: e.activation(out=R_.ap, in_=R_.ap, func=AF.Ln), [R_], [R_])
                            op("act", lambda e: e.activation(out=R_.ap, in_=R_.ap, func=AF.Exp, scale=-0.5), [R_], [R_])
                            op("dve", lambda e: e.scalar_tensor_tensor(out=dstb.ap[:, 512 * tt:512 * tt + 512], in0=pa_.ap, scalar=V(gcol), in1=R_.ap, op0=ALU.mult, op1=ALU.mult),
                               [pa_, vecs, R_], [dstb])
                    if dbg and hp == 0:
                        dma("sp", dbg_d["d_qT"], qb.ap, reads=[qb])
                        dma("sp", dbg_d["d_kT"], kb.ap, reads=[kb])
                    for e_ in range(2):
                        h = 2 * hp + e_
                        p0 = 64 * e_
                        qh = qb.ap[p0:p0 + 64, :]
                        kh = kb.ap[p0:p0 + 64, :]
                        P3 = P3s[h % 2]

                        def stile(dst_ap, dst_buf, kcols, qcols, sig, kh=kh, qh=qh, kb=kb, qb=qb):
                            op("pe", lambda e: e.matmul(dst_ap, lhsT=kh[:, kcols], rhs=qh[:, qcols], start=True, stop=True, skip_group_check=True),
                               [kb, qb], [dst_buf], signal=sig)

                        def make_pv(units, c, e_=e_, p0=p0, hp=hp, va=va):
                            def emit():
                                ob = psb[2 + nxt("o", 2)]
                                nu = len(units)
                                for ui, (pbuf, p_ap, gi, tile_i, ocols) in enumerate(units):
                                    op("pe", lambda e: e.matmul(ob.ap[:, ocols], lhsT=va[gi].ap[:, tile_i, 128 * e_:128 * e_ + 128], rhs=p_ap,
                                                                start=(ui == 0), stop=(ui == nu - 1), skip_group_check=True),
                                       [va[gi], pbuf], [ob], signal=(ui == nu - 1))
                                dq = 64 * (1 - e_)
                                op("act", lambda e: e.activation(out=rden.ap[p0:p0 + 64, :], in_=ob.ap[dq:dq + 64, :], func=AF.Ln), [ob], [rden])
                                op("act", lambda e: e.activation(out=rden.ap[p0:p0 + 64, :], in_=rden.ap[p0:p0 + 64, :], func=AF.Exp, scale=-1.0), [rden], [rden])
                                op("dve", lambda e: e.tensor_tensor(out=oT.ap[p0:p0 + 64, hp, 512 * c:512 * c + 512], in0=ob.ap[p0:p0 + 64, :], in1=rden.ap[p0:p0 + 64, :], op=ALU.mult),
                                   [ob, rden], [oT])
                            return emit

                        e3 = eidx(h, 2)
                        for ch in range(4):
                            stb = psb[(0, 1, 6, 4, 5)[nxt("st", 5)]]
                            for rr in range(4):
                                rho = 4 * ch + rr
                                stile(stb.ap[:, 128 * rr:128 * rr + 128], stb, slice(rho, S, 16), slice(rho, S, 16), rr == 3)
                            pv = P3.ap[:, 512 * ch:512 * ch + 512]
                            pv3 = pv.rearrange("p (a b) -> p a b", a=4)
                            e_ap = etab.ap[:, e3:e3 + 1, 128:256].to_broadcast([128, 4, 128])
                            op("act", lambda e: e.activation(out=pv, in_=stb.ap, func=AF.Exp), [stb], [P3])
                            op("dve", lambda e: e.tensor_tensor(out=pv3, in0=pv3, in1=e_ap, op=ALU.mult), [P3, etab], [P3])
                            tick()
                        for c in range(4):
                            units = []
                            for gi, r in ((0, 1), (1, 4)):
                                ei = eidx(h, gi)
                                for half in range(2):
                                    stb = psb[(0, 1, 6, 4, 5)[nxt("st", 5)]]
                                    pbuf = Pb[nxt("pb", 8)]
                                    for jj in range(2):
                                        if gi == 0:
                                            b = 4 * c + 2 * half + jj
                                            has_prev = b > 0
                                            kprev = slice(128 * (b - 1), 128 * b)
                                            kown = slice(128 * b, 128 * b + 128)
                                            ocols = slice(128 * (b - 4 * c), 128 * (b - 4 * c) + 128)
                                            tprev, town = b - 1, b
                                        else:
                                            rho = 2 * half + jj
                                            has_prev = c > 0
                                            kprev = slice(rho + 512 * (c - 1), 512 * c, 4)
                                            kown = slice(rho + 512 * c, 512 * c + 512, 4)
                                            ocols = slice(rho, 512, 4)
                                            tprev, town = rho * 4 + c - 1, rho * 4 + c
                                        o0 = 256 * jj
                                        if has_prev:
                                            stile(stb.ap[:, o0:o0 + 128], stb, kprev, kown, False)
                                            units.append((pbuf, pbuf.ap[:, o0:o0 + 128], gi, tprev, ocols))
                                        stile(stb.ap[:, o0 + 128:o0 + 256], stb, kown, kown, jj == 1)
                                        units.append((pbuf, pbuf.ap[:, o0 + 128:o0 + 256], gi, town, ocols))
                                    e_ap = etab.ap[:, ei:ei + 1, :].to_broadcast([128, 2, 256])
                                    pb3 = pbuf.ap.rearrange("p (a b) -> p a b", a=2)
                                    op("act", lambda e: e.activation(out=pbuf.ap, in_=stb.ap, func=AF.Exp), [stb], [pbuf])
                                    op("dve", lambda e: e.tensor_tensor(out=pb3, in0=pb3, in1=e_ap, op=ALU.mult), [pbuf, etab], [pbuf])
                                    tick()
                            for rho in range(16):
                                units.append((P3, P3.ap[:, 128 * rho + 32 * c:128 * rho + 32 * c + 32], 2, rho, slice(rho, 512, 16)))
                            pending.append([2, make_pv(units, c)])
                    flush()
                if dbg:
                    dma("sp", dbg_d["d_oT"], oT.ap, reads=[oT])
                if stop_after not in ("attn", "attn+"):
                    pre["wa"] = load_w(w_in[:, 0:512])
                    pre["wg"] = load_w(w_in[:, 1024:1024 + 512])
                sc.barrier()
                if stop_after == "attn":
                    stop_now = True
                    break
                if stop_after == "attn+":
                    skip_rest = True
                    break
                T = Region(R1)
                aT = T.get([128, 8, 32 + S], BF16)
                R2 = T.off
                cvT = T.get([128, 8, S], BF16)
                dgb = T.get([128, 31, 128], BF16)
                sg = [T.get([128, 512], F32) for i in range(2)]
                sgm = [T.get([128, 512], F32) for i in range(2)]
                sqv = [T.get([128, 512], BF16) for i in range(2)]
                rr_ = T.get([128, S], F32)
                gsb = [T.get([128, 512], F32) for i in range(2)]
                t1 = [T.get([128, 512], F32) for i in range(2)]
                AO = 32
                op("dve", lambda e: e.memset(aT.ap[:, :, 0:AO], 0.0), [], [aT])
                pend_ss = []
                for cc in range(8):
                    if cc == 0:
                        wa, wg = pre.pop("wa"), pre.pop("wg")
                    elif cc % 4 == 0:
                        wa = load_w(w_in[:, 128 * cc:128 * cc + 512])
                        wg = load_w(w_in[:, 1024 + 128 * cc:1024 + 128 * cc + 512])
                    coff = 128 * (cc % 4)
                    for j in range(31):
                        op("dve", lambda e: e.tensor_scalar(out=dgb.ap[:, j, :], in0=ident.ap, scalar1=V(V_CW + cc * 31 + j), scalar2=None, op0=ALU.mult),
                           [ident, vecs], [dgb])
                    for tt in range(4):
                        pg_ = psb[4 + nxt("b1", 3)]
                        pv_ = psb[4 + nxt("b1", 3)]
                        mm_group(pg_.ap, pg_, [(wg.ap[:, k, coff:coff + 128], h1T.ap[:, k, 512 * tt:512 * tt + 512]) for k in range(8)], [wg, h1T])
                        mm_group(pv_.ap, pv_, [(wa.ap[:, k, coff:coff + 128], h1T.ap[:, k, 512 * tt:512 * tt + 512]) for k in range(8)], [wa, h1T])
                        sg_ = sg[tt % 2]
                        op("act", lambda e: e.activation(out=sg_.ap, in_=pg_.ap, func=AF.Sigmoid), [pg_], [sg_])
                        op("dve", lambda e: e.tensor_tensor(out=aT.ap[:, cc, AO + 512 * tt:AO + 512 * tt + 512], in0=pv_.ap, in1=sg_.ap, op=ALU.mult), [pv_, sg_], [aT])
                    for tt in range(4):
                        pc_ = psb[4 + nxt("b1", 3)]
                        mm_group(pc_.ap, pc_, [(dgb.ap[:, j, :], aT.ap[:, cc, AO - 30 + 512 * tt + j:AO - 30 + 512 * tt + j + 512]) for j in range(31)], [dgb, aT])
                        while pend_ss:
                            pend_ss.pop(0)()
                        sv = sqv[tt % 2]
                        op("act", lambda e: e.activation(out=sv.ap, in_=pc_.ap, func=AF.Square, bias=V(V_CONVB + cc)), [pc_, vecs], [sv])
                        op("act", lambda e: e.activation(out=cvT.ap[:, cc, 512 * tt:512 * tt + 512], in_=pc_.ap, func=AF.Identity, bias=V(V_CONVB + cc)), [pc_, vecs], [cvT])

                        def ss_mm(tt=tt, cc=cc, sv=sv):
                            op("pe", lambda e: e.matmul(psb[tt].ap, lhsT=ones.ap, rhs=sv.ap, start=(cc == 0), stop=(cc == 7), skip_group_check=True),
                               [ones, sv], [psb[tt]], signal=True)
                        pend_ss.append(ss_mm)
                while pend_ss:
                    pend_ss.pop(0)()
                for tt in range(4):
                    op("dve", lambda e: e.tensor_scalar(out=rr_.ap[:, 512 * tt:512 * tt + 512], in0=psb[tt].ap, scalar1=1.0 / D, scalar2=EPS, op0=ALU.mult, op1=ALU.add), [psb[tt]], [rr_])
                op("act", lambda e: e.activation(out=rr_.ap, in_=rr_.ap, func=AF.Ln), [rr_], [rr_])
                op("act", lambda e: e.activation(out=rr_.ap, in_=rr_.ap, func=AF.Exp, scale=-0.5), [rr_], [rr_])
                def silu_unit(cc, tt):
                    sg_, sm_ = sg[tt % 2], sgm[tt % 2]
                    cs = slice(512 * tt, 512 * tt + 512)
                    op("dve", lambda e: e.scalar_tensor_tensor(out=sg_.ap, in0=cvT.ap[:, cc, cs], scalar=V(V_CNG + cc), in1=rr_.ap[:, cs], op0=ALU.mult, op1=ALU.mult),
                       [cvT, vecs, rr_], [sg_])
                    op("act", lambda e: e.activation(out=sm_.ap, in_=sg_.ap, func=AF.Sigmoid), [sg_], [sm_])
                    op("dve", lambda e: e.tensor_tensor(out=cvT.ap[:, cc, cs], in0=sg_.ap, in1=sm_.ap, op=ALU.mult), [sg_, sm_], [cvT])

                silu_units = [(cc, tt) for cc in range(8) for tt in range(4)]
                mixT = aT
                for dc in range(8):
                    wsl = ring[ring_i[0] % NRING]
                    ring_i[0] += 1
                    load_w(w_in[:, 6144 + 128 * dc:6144 + 128 * dc + 128], ncols=128, slot=wsl, col0=0)
                    load_w(w_ao[:, 128 * dc:128 * dc + 128], ncols=128, slot=wsl, col0=128)
                    for tt in range(4):
                        cs = slice(512 * tt, 512 * tt + 512)
                        pg_ = psb[nxt("p4", 4)]
                        py_ = psb[nxt("p4", 4)]
                        mm_group(pg_.ap, pg_, [(wsl.ap[:, k, 0:128], h1T.ap[:, k, cs]) for k in range(8)], [wsl, h1T])
                        mm_group(py_.ap, py_, [(wsl.ap[:, k, 128:256], oT.ap[:, k, cs]) for k in range(8)], [wsl, oT])
                        g_ = gsb[0]
                        op("act", lambda e: e.activation(out=g_.ap, in_=pg_.ap, func=AF.Sigmoid, bias=V(V_GATEB + 8 + dc)), [pg_, vecs], [g_])
                        op("dve", lambda e: e.tensor_tensor(out=mixT.ap[:, dc, AO + 512 * tt:AO + 512 * tt + 512], in0=py_.ap, in1=g_.ap, op=ALU.mult), [py_, g_], [mixT])
                        if silu_units:
                            silu_unit(*silu_units.pop(0))
                while silu_units:
                    silu_unit(*silu_units.pop(0))
                if dbg:
                    dma("sp", dbg_d["d_saT"], cvT.ap, reads=[cvT])
                for dc in range(8):
                    wsl = ring[ring_i[0] % NRING]
                    ring_i[0] += 1
                    load_w(w_in[:, 5120 + 128 * dc:5120 + 128 * dc + 128], ncols=128, slot=wsl, col0=0)
                    load_w(w_co[:, 128 * dc:128 * dc + 128], ncols=128, slot=wsl, col0=128)
                    for tt in range(4):
                        cs = slice(512 * tt, 512 * tt + 512)
                        mcs = slice(AO + 512 * tt, AO + 512 * tt + 512)
                        pg_ = psb[nxt("p4", 4)]
                        py_ = psb[nxt("p4", 4)]
                        mm_group(pg_.ap, pg_, [(wsl.ap[:, k, 0:128], h1T.ap[:, k, cs]) for k in range(8)], [wsl, h1T])
                        mm_group(py_.ap, py_, [(wsl.ap[:, k, 128:256], cvT.ap[:, k, cs]) for k in range(8)], [wsl, cvT])
                        g_ = gsb[1]
                        t_ = t1[tt % 2]
                        op("act", lambda e: e.activation(out=g_.ap, in_=pg_.ap, func=AF.Sigmoid, bias=V(V_GATEB + dc)), [pg_, vecs], [g_])
                        op("dve", lambda e: e.tensor_tensor(out=t_.ap, in0=py_.ap, in1=g_.ap, op=ALU.mult), [py_, g_], [t_])
                        op("dve", lambda e: e.tensor_tensor(out=mixT.ap[:, dc, mcs], in0=mixT.ap[:, dc, mcs], in1=t_.ap, op=ALU.add), [mixT, t_], [mixT])
                if dbg:
                    dma("sp", dbg_d["d_mixT"], mixT.ap[:, :, AO:AO + S], reads=[mixT])
                if stop_after not in ("mix", "mix+"):
                    pre["wo"] = [load_w(w_out[:, 512 * cb:512 * cb + 512]) for cb in range(2)]
                sc.barrier()
                if stop_after == "mix":
                    stop_now = True
                    break
                if stop_after == "mix+":
                    skip_rest = True
                    break
                L = Region(R0)
                h2T = L.get([128, 8, S], BF16)
                T = Region(R2)
                xt = [T.get([128, D], F32) for i in range(3)]
                x1t = [T.get([128, D], F32) for i in range(3)]
                junk = T.get([128, D], BF16)
                xn = [T.get([128, D], BF16) for i in range(2)]
                ssq = [T.get([128, 8], F32) for i in range(3)]
                wo = pre.pop("wo")
                for t in range(17):
                    if t < 16:
                        xb, x1b = xt[t % 3], x1t[t % 3]
                        dma("sp", xb.ap, xs[128 * t:128 * t + 128, :], writes=[xb])
                        for cb in range(2):
                            p_ = psb[nxt("p4", 4)]
                            mm_group(p_.ap, p_, [(mixT.ap[:, k, AO + 128 * t:AO + 128 * t + 128], wo[cb].ap[:, k, :]) for k in range(8)], [mixT, wo[cb]])
                            op("dve", lambda e: e.tensor_tensor(out=x1b.ap[:, 512 * cb:512 * cb + 512], in0=p_.ap, in1=xb.ap[:, 512 * cb:512 * cb + 512], op=ALU.add), [p_, xb], [x1b])
                        dma("sp", x1scr[128 * t:128 * t + 128, :], x1b.ap, reads=[x1b])
                        rms_stage1(None, x1b, junk, ssq[t % 3])
                    if t >= 1:
                        rms_stage2(x1t[(t - 1) % 3], ssq[(t - 1) % 3], xn[(t - 1) % 2], rows.ap[:, D:2 * D], h2T, t - 1)
                if dbg:
                    dma("sp", dbg_d["d_h2T"], h2T.ap, reads=[h2T])
                sc.barrier()
                if stop_after == "x1":
                    stop_now = True
                    break
                if stop_after == "x1+":
                    skip_rest = True
                    break
            if stop_now:
                break
            if skip_rest:
                continue
            if stop_after == "conly":
                L = Region(R0)
                h2T = L.get([128, 8, S], BF16)
            T = Region(L.off)
            gT = T.get([128, 22, 1024], BF16)
            wdn = [T.get([128, D], BF16) for kk in range(22)]
            uT = [[T.get([128, 8 + 1024], BF16) for vg in range(2)] for i in range(2)]
            halo = T.get([128, NFC, 2], BF16)
            dgf = [T.get([128, 6, 128], BF16) for i in range(2)]
            sgf = [T.get([128, 512], F32) for i in range(2)]
            x1r = [T.get([128, D], F32) for i in range(2)]
            ot = [T.get([128, D], F32) for i in range(2)]
            UO = 8
            for i in range(2):
                for vg in range(2):
                    op("dve", lambda e: e.memset(uT[i][vg].ap[:, 0:UO], 0.0), [], [uT[i][vg]])
            for hf in range(2):
                for j in range(22):
                    wsl = ring[ring_i[0] % NRING]
                    ring_i[0] += 1
                    load_w(w_up[:, 128 * j:128 * j + 128], ncols=128, slot=wsl, col0=0)
                    load_w(w_up[:, DFF + 128 * j:DFF + 128 * j + 128], ncols=128, slot=wsl, col0=128)
                    if hf == 0:
                        dma("pool", wdn[j].ap, w_dn[128 * j:128 * j + 128, :], writes=[wdn[j]])
                    dg_ = dgf[j % 2]
                    for vg in range(2):
                        ch = j + 22 * vg
                        for jj in range(3):
                            op("dve", lambda e: e.tensor_scalar(out=dg_.ap[:, 3 * vg + jj, :], in0=ident.ap, scalar1=V(V_FCW + ch * 3 + jj), scalar2=None, op0=ALU.mult),
                               [ident, vecs], [dg_])
                    ub = uT[j % 2]
                    for vg in range(2):
                        ch = j + 22 * vg
                        if hf == 1:
                            op("dve", lambda e: e.tensor_copy(out=ub[vg].ap[:, UO - 2:UO], in_=halo.ap[:, ch, :]), [halo], [ub[vg]])
                        for tt in range(2):
                            tok = slice(1024 * hf + 512 * tt, 1024 * hf + 512 * tt + 512)
                            p_ = psb[nxt("p4", 4)]
                            mm_group(p_.ap, p_, [(wsl.ap[:, k, 128 * vg:128 * vg + 128], h2T.ap[:, k, tok]) for k in range(8)], [wsl, h2T])
                            op("act", lambda e: e.activation(out=ub[vg].ap[:, UO + 512 * tt:UO + 512 * tt + 512], in_=p_.ap, func=AF.Copy), [p_], [ub[vg]])
                        if hf == 0:
                            op("dve", lambda e: e.tensor_copy(out=halo.ap[:, ch, :], in_=ub[vg].ap[:, UO + 1022:UO + 1024]), [ub[vg]], [halo])
                    for tt in range(2):
                        pcv = psb[4 + nxt("b1", 3)]
                        pcg = psb[4 + nxt("b1", 3)]
                        for vg, pc_ in ((1, pcg), (0, pcv)):
                            mm_group(pc_.ap, pc_, [(dg_.ap[:, 3 * vg + jj, :], ub[vg].ap[:, UO - 2 + 512 * tt + jj:UO - 2 + 512 * tt + jj + 512]) for jj in range(3)], [dg_, ub[vg]])
                        s_ = sgf[tt % 2]
                        op("act", lambda e: e.activation(out=s_.ap, in_=pcg.ap, func=AF.Silu, bias=V(V_FCB + 22 + j)), [pcg, vecs], [s_])
                        op("dve", lambda e: e.scalar_tensor_tensor(out=gT.ap[:, j, 512 * tt:512 * tt + 512], in0=pcv.ap, scalar=V(V_FCB + j), in1=s_.ap, op0=ALU.add, op1=ALU.mult),
                           [pcv, vecs, s_], [gT])
                if dbg and hf == 0:
                    dma("sp", dbg_d["d_gT"], gT.ap, reads=[gT])
                for t8 in range(8):
                    t = 8 * hf + t8
                    xr, ob_ = x1r[t8 % 2], ot[t8 % 2]
                    dma("sp", xr.ap, x1scr[128 * t:128 * t + 128, :], writes=[xr])
                    for cb in range(2):
                        p_ = psb[nxt("p4", 4)]
                        mm_group(p_.ap, p_, [(gT.ap[:, j, 128 * t8:128 * t8 + 128], wdn[j].ap[:, 512 * cb:512 * cb + 512]) for j in range(22)], [gT] + wdn)
                        op("dve", lambda e: e.tensor_tensor(out=ob_.ap[:, 512 * cb:512 * cb + 512], in0=p_.ap, in1=xr.ap[:, 512 * cb:512 * cb + 512], op=ALU.add), [p_, xr], [ob_])
                    dma("sp", outs[128 * t:128 * t + 128, :], ob_.ap, reads=[ob_])
            if s + 1 < nseq:
                pre["wv"] = [load_w(w_in[:, 4096 + 512 * cb:4096 + 512 * cb + 512]) for cb in range(2)]
            sc.barrier()
        sc.barrier(engines=("sp",))
    return nc


def _host_layout(inputs):
    f = lambda a: np.ascontiguousarray(np.asarray(a, dtype=np.float32))
    vecs = np.zeros((128, NV), np.float32)
    vecs[:, V_GATEB:V_GATEB + 16] = f(inputs["gate_b"][0]).reshape(16, 128).T
    vecs[:, V_CONVB:V_CONVB + 8] = f(inputs["conv_b"][0]).reshape(8, 128).T
    vecs[:, V_CNG:V_CNG + 8] = f(inputs["conv_norm_g"][0]).reshape(8, 128).T
    vecs[:, V_FCB:V_FCB + NFC] = f(inputs["ffn_conv_b"][0]).reshape(NFC, 128).T
    vecs[:, V_CW:V_CW + 248] = f(inputs["conv_w"][0]).reshape(31, 8, 128).transpose(2, 1, 0).reshape(128, 248)
    vecs[:, V_FCW:V_FCW + NFC * 3] = f(inputs["ffn_conv_w"][0]).reshape(3, NFC, 128).transpose(2, 1, 0).reshape(128, NFC * 3)
    vecs[:, V_QG] = np.tile(f(inputs["q_norm_g"][0]), 2)
    vecs[:, V_KG] = np.tile(f(inputs["k_norm_g"][0]), 2)
    rows = np.zeros((128, 2 * D), np.float32)
    rows[:, 0:D] = np.broadcast_to(f(inputs["norm1_g"][0])[None, :], (128, D))
    rows[:, D:] = np.broadcast_to(f(inputs["norm2_g"][0])[None, :], (128, D))
    cst = np.zeros((128, NCST), np.float32)
    cst[:, C_ID:C_ID + 128] = np.eye(128, dtype=np.float32)
    cst[0:64, C_BLK:C_BLK + 64] = 1.0
    cst[64:128, C_BLK + 64:C_BLK + 128] = 1.0
    k_i = np.arange(128)[:, None]
    q_i = np.arange(128)[None, :]
    cst[:, C_ST:C_ST + 128] = q_i + 128 - k_i
    cst[:, C_ST + 128:C_ST + 256] = np.maximum(q_i - k_i, 0)
    cst[:, C_MK:C_MK + 128] = (k_i >= q_i)
    cst[:, C_MK + 128:C_MK + 256] = (q_i >= k_i)
    shared = {
        "w_in": f(inputs["w_in"][0]), "w_conv_out": f(inputs["w_conv_out"][0]), "w_attn_out": f(inputs["w_attn_out"][0]),
        "w_out": f(inputs["w_out"][0]), "w_up": f(inputs["w_up"][0]), "w_down": f(inputs["w_down"][0]),
        "vecs": vecs, "rows": rows, "consts": cst,
    }
    return shared


def kernel(**inputs):
    x = np.asarray(inputs["x"], dtype=np.float32)
    B = x.shape[0]
    shared = _host_layout(inputs)
    nc = build_nc(SEQ_PER_CORE)
    in_maps = []
    for c in range(NCORES):
        m = dict(shared)
        m["x"] = np.ascontiguousarray(x[c * SEQ_PER_CORE:(c + 1) * SEQ_PER_CORE].reshape(SEQ_PER_CORE * S, D))
        in_maps.append(m)
    res = run_bass_kernel_spmd(nc, in_maps, core_ids=list(range(NCORES)))
    out = np.concatenate([np.asarray(r["out"], dtype=np.float32).reshape(SEQ_PER_CORE, S, D) for r in res.results], axis=0)
    return out
```

```python
import numpy as np
import concourse.bass as bass
import concourse.mybir as mybir
from concourse.bass_utils import run_bass_kernel_spmd
from contextlib import ExitStack

F32 = mybir.dt.float32
BF16 = mybir.dt.bfloat16
ALU = mybir.AluOpType
AF = mybir.ActivationFunctionType

S = 2048
D = 1024
NH = 16
DFF = 2816
IN_COLS = 7168
EPS = 1e-6
NCORES = 8
SEQ_PER_CORE = 2
NFC = 2 * DFF // 128
ND = 10
KB = 1024

V_GATEB = 0
V_CONVB = 16
V_CNG = 24
V_FCB = 32
V_CW = 76
V_FCW = V_CW + 8 * 31
V_QG = V_FCW + NFC * 3
V_KG = V_QG + 1
NV = V_KG + 1

C_ID, C_BLK, C_ST, C_MK, NCST = 0, 128, 256, 512, 768


class Buf:
    __slots__ = ("ap", "w", "r")

    def __init__(self, ap):
        self.ap = ap
        self.w = None
        self.r = {}


class Sched:
    def __init__(self, nc, stack):
        self.nc = nc
        self.eng = dict(pe=nc.tensor, act=nc.scalar, dve=nc.vector, pool=nc.gpsimd, sp=nc.sync)
        self.msem = {e: stack.enter_context(nc.semaphore("ms_" + e)) for e in ("pe", "act", "dve", "pool")}
        self.mcnt = {e: 0 for e in self.msem}
        self.dsem = {q: [stack.enter_context(nc.semaphore("d_%s_%d" % (q, i))) for i in range(ND)] for q in ("sp", "pool")}
        self.dcnt = {q: [0] * ND for q in self.dsem}
        self.dnext = {q: 0 for q in self.dsem}
        self.seen = {e: {} for e in self.eng}
        self.pending = {e: False for e in self.msem}

    def _wait(self, e, ev):
        sem, val, key, prod = ev
        if self.seen[e].get(key, 0) >= val:
            return
        self.eng[e].wait_ge(sem, val)
        self.seen[e][key] = val

    def _deps(self, e, reads, writes):
        for b in reads:
            if b.w is not None:
                if b.w[3] == e and e == "pe":
                    continue
                self._wait(e, b.w)
        for b in writes:
            if b.w is not None and not (b.w[3] == e and e == "pe"):
                self._wait(e, b.w)
            for ev in b.r.values():
                if not (ev[3] == e and e == "pe"):
                    self._wait(e, ev)

    def _record(self, ev, reads, writes):
        for b in reads:
            b.r[ev[2]] = ev
        for b in writes:
            b.w = ev
            b.r = {}

    def op(self, e, fn, reads=(), writes=(), signal=True):
        self._deps(e, reads, writes)
        ins = fn(self.eng[e])
        if signal:
            self.mcnt[e] += 1
            ins.then_inc(self.msem[e], 1)
            ev = (self.msem[e], self.mcnt[e], e, e)
            self.pending[e] = False
        else:
            ev = (self.msem[e], self.mcnt[e] + 1, e, e)
            self.pending[e] = True
        self._record(ev, reads, writes)
        return ins

    def dma(self, q, out_ap, in_ap, reads=(), writes=()):
        i = self.dnext[q]
        self.dnext[q] = (i + 1) % ND
        sem = self.dsem[q][i]
        key = "d_%s_%d" % (q, i)
        if self.dcnt[q][i] > 0:
            self._wait(q, (sem, self.dcnt[q][i], key, "dma"))
        self._deps(q, reads, writes)
        self.eng[q].dma_start(out=out_ap, in_=in_ap).then_inc(sem, 16)
        self.dcnt[q][i] += 16
        ev = (sem, self.dcnt[q][i], key, "dma")
        self._record(ev, reads, writes)

    def all_events(self):
        evs = []
        for e in self.msem:
            assert not self.pending[e], e
            if self.mcnt[e] > 0:
                evs.append((self.msem[e], self.mcnt[e], e, e))
        for q in self.dsem:
            for i in range(ND):
                if self.dcnt[q][i] > 0:
                    evs.append((self.dsem[q][i], self.dcnt[q][i], "d_%s_%d" % (q, i), "dma"))
        return evs

    def barrier(self, engines=("pe", "act", "dve", "pool", "sp")):
        evs = self.all_events()
        for e in engines:
            for ev in evs:
                self._wait(e, ev)


def build_nc(nseq=SEQ_PER_CORE, dbg=False, stop_after=None):
    nc = bass.Bass("TRN2", target_bir_lowering=False)
    dt = nc.dram_tensor
    x_d = dt("x", [nseq * S, D], F32, kind="ExternalInput").ap()
    w_in = dt("w_in", [D, IN_COLS], F32, kind="ExternalInput").ap()
    w_co = dt("w_conv_out", [D, D], F32, kind="ExternalInput").ap()
    w_ao = dt("w_attn_out", [D, D], F32, kind="ExternalInput").ap()
    w_out = dt("w_out", [D, D], F32, kind="ExternalInput").ap()
    w_up = dt("w_up", [D, 2 * DFF], F32, kind="ExternalInput").ap()
    w_dn = dt("w_down", [DFF, D], F32, kind="ExternalInput").ap()
    vecs_d = dt("vecs", [128, NV], F32, kind="ExternalInput").ap()
    rows_d = dt("rows", [128, 2 * D], F32, kind="ExternalInput").ap()
    cst_d = dt("consts", [128, NCST], F32, kind="ExternalInput").ap()
    out_d = dt("out", [nseq * S, D], F32, kind="ExternalOutput").ap()
    vscr = dt("vscr", [S, 8, 256], BF16, kind="Internal").ap()
    x1scr = dt("x1scr", [S, D], F32, kind="Internal").ap()
    dbg_d = {}
    if dbg:
        for nm, shp, ty in (("d_h1T", [128, 8, S], BF16), ("d_qT", [128, S], BF16), ("d_kT", [128, S], BF16),
                            ("d_oT", [128, 8, S], BF16), ("d_saT", [128, 8, S], BF16), ("d_mixT", [128, 8, S], BF16),
                            ("d_h2T", [128, 8, S], BF16), ("d_gT", [128, 22, 1024], BF16)):
            dbg_d[nm] = dt(nm, shp, ty, kind="ExternalOutput").ap()

    with ExitStack() as top:
        sc = Sched(nc, top)
        op, dma = sc.op, sc.dma
        base0 = nc.sbuf_base
        base = ((base0 + 63) // 64) * 64
        ARENA = 204 * KB
        top.enter_context(nc.sbuf_tensor("arena", [128, (ARENA + base - base0 + 64) // 2], BF16))
        uid = [0]

        def at(off, shape, dtype):
            uid[0] += 1
            n = 1
            for d_ in shape[1:]:
                n *= d_
            nbytes = n * (4 if dtype == F32 else 2)
            assert off % 32 == 0 and off + nbytes <= ARENA, (off, nbytes)
            h = nc.alloc_sbuf_tensor_at("t%d" % uid[0], list(shape), dtype, offset=base + off)
            return Buf(h.ap()), off + ((nbytes + 31) // 32) * 32

        class Region:
            def __init__(self, off):
                self.off = off

            def get(self, shape, dtype):
                b, self.off = at(self.off, shape, dtype)
                return b

        G = Region(0)
        cst = G.get([128, NCST], F32)
        vecs = G.get([128, NV], F32)
        rows = G.get([128, 2 * D], F32)
        ident = G.get([128, 128], BF16)
        blk = G.get([128, 128], BF16)
        ones = G.get([128, 128], BF16)
        NRING = 3
        ring = [G.get([128, 8, 512], BF16) for i in range(NRING)]
        ring_i = [0]
        R0 = G.off
        psb = [Buf(top.enter_context(nc.psum_tensor("ps%d" % i, [128, 512], F32)).ap()) for i in range(7)]
        pst = Buf(top.enter_context(nc.psum_tensor("pst", [128, 1024], BF16)).ap())

        def V(col, n=1):
            return vecs.ap[:, col:col + n]

        dma("sp", cst.ap, cst_d, writes=[cst])
        dma("sp", vecs.ap, vecs_d, writes=[vecs])
        dma("sp", rows.ap, rows_d, writes=[rows])
        op("dve", lambda e: e.tensor_copy(out=ident.ap, in_=cst.ap[:, C_ID:C_ID + 128]), [cst], [ident])
        op("dve", lambda e: e.tensor_copy(out=blk.ap, in_=cst.ap[:, C_BLK:C_BLK + 128]), [cst], [blk])
        op("dve", lambda e: e.memset(ones.ap, 1.0), [], [ones])
        op("dve", lambda e: e.tensor_scalar(out=V(V_KG), in0=V(V_KG), scalar1=8.0, scalar2=None, op0=ALU.mult), [vecs], [vecs])

        def eidx(h, gi):
            return 16 - (h + 1) + 4 * gi

        def load_w(src_ap, ncols=512, kchunks=8, slot=None, col0=0):
            if slot is None:
                slot = ring[ring_i[0] % NRING]
                ring_i[0] += 1
            dma("pool", slot.ap[:, 0:kchunks, col0:col0 + ncols], src_ap.rearrange("(k p) c -> p k c", p=128), writes=[slot])
            return slot

        def mm_group(out_ap, out_buf, pairs, reads, start=True, sig=True):
            n = len(pairs)
            for i, (l, r) in enumerate(pairs):
                op("pe", lambda e: e.matmul(out_ap, lhsT=l, rhs=r, start=(start and i == 0), stop=(i == n - 1), skip_group_check=True),
                   reads, [out_buf], signal=(sig and i == n - 1))

        rrc = {"p4": 0, "st": 0, "pb": 0, "o": 0, "pq": 0, "b1": 0}

        def nxt(key, n):
            v = rrc[key] % n
            rrc[key] += 1
            return v

        def rms_stage1(src_dram_rows, xb, junk, sq):
            if src_dram_rows is not None:
                dma("sp", xb.ap, src_dram_rows, writes=[xb])
            op("act", lambda e: e.activation(out=junk.ap, in_=xb.ap, func=AF.Square, accum_out=sq.ap[:, 0:1]), [xb], [junk, sq])
            op("dve", lambda e: e.tensor_scalar(out=sq.ap[:, 1:2], in0=sq.ap[:, 0:1], scalar1=1.0 / D, scalar2=EPS, op0=ALU.mult, op1=ALU.add), [sq], [sq])
            op("act", lambda e: e.activation(out=sq.ap[:, 1:2], in_=sq.ap[:, 1:2], func=AF.Ln), [sq], [sq])
            op("act", lambda e: e.activation(out=sq.ap[:, 1:2], in_=sq.ap[:, 1:2], func=AF.Exp, scale=-0.5), [sq], [sq])

        def rms_stage2(xb, sq, xnb, grow_ap, dstT, t):
            op("dve", lambda e: e.scalar_tensor_tensor(out=xnb.ap, in0=xb.ap, scalar=sq.ap[:, 1:2], in1=grow_ap, op0=ALU.mult, op1=ALU.mult), [xb, sq, rows], [xnb])
            for k in range(8):
                op("pe", lambda e: e.transpose(out=pst.ap[:, 128 * k:128 * k + 128], in_=xnb.ap[:, 128 * k:128 * k + 128], identity=ident.ap),
                   [xnb, ident], [pst], signal=(k == 7))
            op("act", lambda e: e.activation(out=dstT.ap[:, :, 128 * t:128 * t + 128], in_=pst.ap.rearrange("p (k t) -> p k t", k=8), func=AF.Copy), [pst], [dstT])

        pre = {}
        for s in range(nseq):
            xs = x_d[s * S:(s + 1) * S, :]
            outs = out_d[s * S:(s + 1) * S, :]
            stop_now = False
            skip_rest = False
            for _once in ([] if stop_after == "conly" else [0]):
                L = Region(R0)
                h1T = L.get([128, 8, S], BF16)
                oT = L.get([128, 8, S], BF16)
                R1 = L.off
                T = Region(R1)
                xt = [T.get([128, D], F32) for i in range(3)]
                junk = T.get([128, D], BF16)
                xn = [T.get([128, D], BF16) for i in range(2)]
                ssq = [T.get([128, 8], F32) for i in range(3)]
                vst = [T.get([128, 8, 256], BF16) for i in range(2)]
                for i in range(2):
                    op("dve", lambda e: e.memset(vst[i].ap, 1.0), [], [vst[i]])
                wv = pre.pop("wv", None) or [load_w(w_in[:, 4096 + 512 * cb:4096 + 512 * cb + 512]) for cb in range(2)]
                for t in range(17):
                    if t < 16:
                        rms_stage1(xs[128 * t:128 * t + 128, :], xt[t % 3], junk, ssq[t % 3])
                    if t >= 1:
                        rms_stage2(xt[(t - 1) % 3], ssq[(t - 1) % 3], xn[(t - 1) % 2], rows.ap[:, 0:D], h1T, t - 1)
                for t in range(16):
                    vb = vst[t % 2]
                    for cb in range(2):
                        pb_ = psb[nxt("p4", 4)]
                        mm_group(pb_.ap, pb_, [(h1T.ap[:, k, 128 * t:128 * t + 128], wv[cb].ap[:, k, :]) for k in range(8)], [h1T, wv[cb]])
                        pv4 = pb_.ap.rearrange("p (h c) -> p h c", h=4)
                        op("act", lambda e: e.activation(out=vb.ap[:, 4 * cb:4 * cb + 4, 0:64], in_=pv4[:, :, 0:64], func=AF.Copy), [pb_], [vb])
                        op("dve", lambda e: e.tensor_copy(out=vb.ap[:, 4 * cb:4 * cb + 4, 192:256], in_=pv4[:, :, 64:128]), [pb_], [vb])
                    dma("sp", vscr[128 * t:128 * t + 128, :, :], vb.ap, reads=[vb])
                if dbg:
                    dma("sp", dbg_d["d_h1T"], h1T.ap, reads=[h1T])
                if stop_after not in ("v", "v+"):
                    pre["wq"] = load_w(w_in[:, 2048:2048 + 512])
                    pre["wk"] = load_w(w_in[:, 3072:3072 + 512])
                sc.barrier()
                if stop_after == "v":
                    stop_now = True
                    break
                if stop_after == "v+":
                    skip_rest = True
                    break
                T = Region(R1)
                etab = T.get([128, 24, 256], BF16)
                etmp = T.get([128, 256], F32)
                qT = [T.get([128, S], BF16) for i in range(2)]
                kT = [T.get([128, S], BF16) for i in range(2)]
                sqb = [T.get([128, 512], BF16) for i in range(2)]
                Rb = [T.get([128, 512], F32) for i in range(2)]
                varr = [[T.get([128, 16, 256], BF16) for g in range(3)] for i in range(2)]
                P3s = [T.get([128, S], BF16) for i in range(2)]
                Pb = [T.get([128, 512], BF16) for i in range(8)]
                rden = T.get([128, 512], F32)
                for ei in range(24):
                    c = 2.0 ** (ei / 2.0 - 8.0)
                    op("act", lambda e: e.activation(out=etmp.ap, in_=cst.ap[:, C_ST:C_ST + 256], func=AF.Exp, scale=-c), [cst], [etmp])
                    op("dve", lambda e: e.tensor_tensor(out=etab.ap[:, ei, :], in0=etmp.ap, in1=cst.ap[:, C_MK:C_MK + 256], op=ALU.mult), [etmp, cst], [etab])
                wq = wk = None
                pending = []

                def tick():
                    for p_ in pending:
                        p_[0] -= 1
                    while pending and pending[0][0] <= 0:
                        pending.pop(0)[1]()

                def flush():
                    while pending:
                        pending.pop(0)[1]()

                wst = {}

                def project(hp):
                    if hp == 0:
                        wst["q"], wst["k"] = pre.pop("wq"), pre.pop("wk")
                    elif hp % 4 == 0:
                        wst["q"] = load_w(w_in[:, 2048 + 128 * hp:2048 + 128 * hp + 512])
                        wst["k"] = load_w(w_in[:, 3072 + 128 * hp:3072 + 128 * hp + 512])
                    wq, wk = wst["q"], wst["k"]
                    va = varr[hp % 2]
                    for g, r in enumerate((1, 4, 16)):
                        if r == 1:
                            dma("sp", va[g].ap, vscr.rearrange("(b i) h c -> i b h c", i=128)[:, :, hp, :], writes=[va[g]])
                        else:
                            for rho in range(r):
                                nb = 16 // r
                                src = vscr.rearrange("(b i r) h c -> r i b h c", r=r, i=128)[rho, :, :, hp, :]
                                dma("sp", va[g].ap[:, rho * nb:(rho + 1) * nb, :], src, writes=[va[g]])
                    qb, kb = qT[hp % 2], kT[hp % 2]
                    coff = 128 * (hp % 4)
                    for (wt_, dstb, gcol) in ((wq, qb, V_QG), (wk, kb, V_KG)):
                        for tt in range(4):
                            pa_ = psb[4 + nxt("pq", 2)]
                            sq_, R_ = sqb[tt % 2], Rb[tt % 2]
                            mm_group(pa_.ap, pa_, [(wt_.ap[:, k, coff:coff + 128], h1T.ap[:, k, 512 * tt:512 * tt + 512]) for k in range(8)], [wt_, h1T])
                            op("act", lambda e: e.activation(out=sq_.ap, in_=pa_.ap, func=AF.Square), [pa_], [sq_])
                            p6 = psb[6]
                            mm_group(p6.ap, p6, [(blk.ap, sq_.ap)], [blk, sq_])
                            op("dve", lambda e: e.tensor_scalar(out=R_.ap, in0=p6.ap, scalar1=64.0 * EPS, scalar2=None, op0=ALU.add), [p6], [R_])
                            op("act", lambda e: e.activation(out=R_.ap, in_=R_.ap, func=AF.Ln), [R_], [R_])
                            op("act", lambda e: e.activation(out=R_.ap, in_=R_.ap, func=AF.Exp, scale=-0.5), [R_], [R_])
                            op("dve", lambda e: e.scalar_tensor_tensor(out=dstb.ap[:, 512 * tt:512 * tt + 512], in0=pa_.ap, scalar=V(gcol), in1=R_.ap, op0=ALU.mult, op1=ALU.mult),
                               [pa_, vecs, R_], [dstb])
                    if dbg and hp == 0:
                        dma("sp", dbg_d["d_qT"], qb.ap, reads=[qb])
                        dma("sp", dbg_d["d_kT"], kb.ap, reads=[kb])

                project(0)
                for hp in range(8):
                    va = varr[hp % 2]
                    qb, kb = qT[hp % 2], kT[hp % 2]
                    for e_ in range(2):
                        if e_ == 1 and hp + 1 < 8:
                            project(hp + 1)
                        h = 2 * hp + e_
                        p0 = 64 * e_
                        qh = qb.ap[p0:p0 + 64, :]
                        kh = kb.ap[p0:p0 + 64, :]
                        P3 = P3s[h % 2]

                        def stile(dst_ap, dst_buf, kcols, qcols, sig, kh=kh, qh=qh, kb=kb, qb=qb):
                            op("pe", lambda e: e.matmul(dst_ap, lhsT=kh[:, kcols], rhs=qh[:, qcols], start=True, stop=True, skip_group_check=True),
                               [kb, qb], [dst_buf], signal=sig)

                        def make_pv(units, c, e_=e_, p0=p0, hp=hp, va=va):
                            def emit():
                                ob = psb[2 + nxt("o", 2)]
                                nu = len(units)
                                for ui, (pbuf, p_ap, gi, tile_i, ocols) in enumerate(units):
                                    op("pe", lambda e: e.matmul(ob.ap[:, ocols], lhsT=va[gi].ap[:, tile_i, 128 * e_:128 * e_ + 128], rhs=p_ap,
                                                                start=(ui == 0), stop=(ui == nu - 1), skip_group_check=True),
                                       [va[gi], pbuf], [ob], signal=(ui == nu - 1))
                                dq = 64 * (1 - e_)
                                op("act", lambda e: e.activation(out=rden.ap[p0:p0 + 64, :], in_=ob.ap[dq:dq + 64, :], func=AF.Ln), [ob], [rden])
                                op("act", lambda e: e.activation(out=rden.ap[p0:p0 + 64, :], in_=rden.ap[p0:p0 + 64, :], func=AF.Exp, scale=-1.0), [rden], [rden])
                                op("dve", lambda e: e.tensor_tensor(out=oT.ap[p0:p0 + 64, hp, 512 * c:512 * c + 512], in0=ob.ap[p0:p0 + 64, :], in1=rden.ap[p0:p0 + 64, :], op=ALU.mult),
                                   [ob, rden], [oT])
                            return emit

                        e3 = eidx(h, 2)
                        for ch in range(4):
                            stb = psb[(0, 1, 6, 4, 5)[nxt("st", 5)]]
                            for rr in range(4):
                                rho = 4 * ch + rr
                                stile(stb.ap[:, 128 * rr:128 * rr + 128], stb, slice(rho, S, 16), slice(rho, S, 16), rr == 3)
                            pv = P3.ap[:, 512 * ch:512 * ch + 512]
                            pv3 = pv.rearrange("p (a b) -> p a b", a=4)
                            e_ap = etab.ap[:, e3:e3 + 1, 128:256].to_broadcast([128, 4, 128])
                            op("act", lambda e: e.activation(out=pv, in_=stb.ap, func=AF.Exp), [stb], [P3])
                            op("dve", lambda e: e.tensor_tensor(out=pv3, in0=pv3, in1=e_ap, op=ALU.mult), [P3, etab], [P3])
                            tick()
                        for c in range(4):
                            units = []
                            for gi, r in ((0, 1), (1, 4)):
                                ei = eidx(h, gi)
                                for half in range(2):
                                    stb = psb[(0, 1, 6, 4, 5)[nxt("st", 5)]]
                                    pbuf = Pb[nxt("pb", 8)]
                                    for jj in range(2):
                                        if gi == 0:
                                            b = 4 * c + 2 * half + jj
                                            has_prev = b > 0
                                            kprev = slice(128 * (b - 1), 128 * b)
                                            kown = slice(128 * b, 128 * b + 128)
                                            ocols = slice(128 * (b - 4 * c), 128 * (b - 4 * c) + 128)
                                            tprev, town = b - 1, b
                                        else:
                                            rho = 2 * half + jj
                                            has_prev = c > 0
                                            kprev = slice(rho + 512 * (c - 1), 512 * c, 4)
                                            kown = slice(rho + 512 * c, 512 * c + 512, 4)
                                            ocols = slice(rho, 512, 4)
                                            tprev, town = rho * 4 + c - 1, rho * 4 + c
                                        o0 = 256 * jj
                                        if has_prev:
                                            stile(stb.ap[:, o0:o0 + 128], stb, kprev, kown, False)
                                            units.append((pbuf, pbuf.ap[:, o0:o0 + 128], gi, tprev, ocols))
                                        stile(stb.ap[:, o0 + 128:o0 + 256], stb, kown, kown, jj == 1)
                                        units.append((pbuf, pbuf.ap[:, o0 + 128:o0 + 256], gi, town, ocols))
                                    e_ap = etab.ap[:, ei:ei + 1, :].to_broadcast([128, 2, 256])
                                    pb3 = pbuf.ap.rearrange("p (a b) -> p a b", a=2)
                                    op("act", lambda e: e.activation(out=pbuf.ap, in_=stb.ap, func=AF.Exp), [stb], [pbuf])
                                    op("dve", lambda e: e.tensor_tensor(out=pb3, in0=pb3, in1=e_ap, op=ALU.mult), [pbuf, etab], [pbuf])
                                    tick()
                            for rho in range(16):
                                units.append((P3, P3.ap[:, 128 * rho + 32 * c:128 * rho + 32 * c + 32], 2, rho, slice(rho, 512, 16)))
                            pending.append([2, make_pv(units, c)])
                    flush()
                if dbg:
                    dma("sp", dbg_d["d_oT"], oT.ap, reads=[oT])
                if stop_after not in ("attn", "attn+"):
                    pre["wa"] = load_w(w_in[:, 0:512])
                    pre["wg"] = load_w(w_in[:, 1024:1024 + 512])
                sc.barrier()
                if stop_after == "attn":
                    stop_now = True
                    break
                if stop_after == "attn+":
                    skip_rest = True
                    break
                T = Region(R1)
                aT = T.get([128, 8, 32 + S], BF16)
                R2 = T.off
                cvT = T.get([128, 8, S], BF16)
                dgb = T.get([128, 31, 128], BF16)
                sg = [T.get([128, 512], F32) for i in range(2)]
                sgm = [T.get([128, 512], F32) for i in range(2)]
                sqv = [T.get([128, 512], BF16) for i in range(2)]
                rr_ = T.get([128, S], F32)
                gsb = [T.get([128, 512], F32) for i in range(2)]
                t1 = [T.get([128, 512], F32) for i in range(2)]
                AO = 32
                op("dve", lambda e: e.memset(aT.ap[:, :, 0:AO], 0.0), [], [aT])
                pend_ss = []
                for cc in range(8):
                    if cc == 0:
                        wa, wg = pre.pop("wa"), pre.pop("wg")
                    elif cc % 4 == 0:
                        wa = load_w(w_in[:, 128 * cc:128 * cc + 512])
                        wg = load_w(w_in[:, 1024 + 128 * cc:1024 + 128 * cc + 512])
                    coff = 128 * (cc % 4)
                    for j in range(31):
                        op("dve", lambda e: e.tensor_scalar(out=dgb.ap[:, j, :], in0=ident.ap, scalar1=V(V_CW + cc * 31 + j), scalar2=None, op0=ALU.mult),
                           [ident, vecs], [dgb])
                    for tt in range(4):
                        pg_ = psb[4 + nxt("b1", 3)]
                        pv_ = psb[4 + nxt("b1", 3)]
                        mm_group(pg_.ap, pg_, [(wg.ap[:, k, coff:coff + 128], h1T.ap[:, k, 512 * tt:512 * tt + 512]) for k in range(8)], [wg, h1T])
                        mm_group(pv_.ap, pv_, [(wa.ap[:, k, coff:coff + 128], h1T.ap[:, k, 512 * tt:512 * tt + 512]) for k in range(8)], [wa, h1T])
                        sg_ = sg[tt % 2]
                        op("act", lambda e: e.activation(out=sg_.ap, in_=pg_.ap, func=AF.Sigmoid), [pg_], [sg_])
                        op("dve", lambda e: e.tensor_tensor(out=aT.ap[:, cc, AO + 512 * tt:AO + 512 * tt + 512], in0=pv_.ap, in1=sg_.ap, op=ALU.mult), [pv_, sg_], [aT])
                    for tt in range(4):
                        pc_ = psb[4 + nxt("b1", 3)]
                        mm_group(pc_.ap, pc_, [(dgb.ap[:, j, :], aT.ap[:, cc, AO - 30 + 512 * tt + j:AO - 30 + 512 * tt + j + 512]) for j in range(31)], [dgb, aT])
                        while pend_ss:
                            pend_ss.pop(0)()
                        sv = sqv[tt % 2]
                        op("act", lambda e: e.activation(out=sv.ap, in_=pc_.ap, func=AF.Square, bias=V(V_CONVB + cc)), [pc_, vecs], [sv])
                        op("act", lambda e: e.activation(out=cvT.ap[:, cc, 512 * tt:512 * tt + 512], in_=pc_.ap, func=AF.Identity, bias=V(V_CONVB + cc)), [pc_, vecs], [cvT])

                        def ss_mm(tt=tt, cc=cc, sv=sv):
                            op("pe", lambda e: e.matmul(psb[tt].ap, lhsT=ones.ap, rhs=sv.ap, start=(cc == 0), stop=(cc == 7), skip_group_check=True),
                               [ones, sv], [psb[tt]], signal=True)
                        pend_ss.append(ss_mm)
                while pend_ss:
                    pend_ss.pop(0)()
                for tt in range(4):
                    op("dve", lambda e: e.tensor_scalar(out=rr_.ap[:, 512 * tt:512 * tt + 512], in0=psb[tt].ap, scalar1=1.0 / D, scalar2=EPS, op0=ALU.mult, op1=ALU.add), [psb[tt]], [rr_])
                op("act", lambda e: e.activation(out=rr_.ap, in_=rr_.ap, func=AF.Ln), [rr_], [rr_])
                op("act", lambda e: e.activation(out=rr_.ap, in_=rr_.ap, func=AF.Exp, scale=-0.5), [rr_], [rr_])
                def silu_unit(cc, tt):
                    sg_, sm_ = sg[tt % 2], sgm[tt % 2]
                    cs = slice(512 * tt, 512 * tt + 512)
                    op("dve", lambda e: e.scalar_tensor_tensor(out=sg_.ap, in0=cvT.ap[:, cc, cs], scalar=V(V_CNG + cc), in1=rr_.ap[:, cs], op0=ALU.mult, op1=ALU.mult),
                       [cvT, vecs, rr_], [sg_])
                    op("act", lambda e: e.activation(out=sm_.ap, in_=sg_.ap, func=AF.Sigmoid), [sg_], [sm_])
                    op("dve", lambda e: e.tensor_tensor(out=cvT.ap[:, cc, cs], in0=sg_.ap, in1=sm_.ap, op=ALU.mult), [sg_, sm_], [cvT])

                silu_units = [(cc, tt) for cc in range(8) for tt in range(4)]
                mixT = aT
                for dc in range(8):
                    wsl = ring[ring_i[0] % NRING]
                    ring_i[0] += 1
                    load_w(w_in[:, 6144 + 128 * dc:6144 + 128 * dc + 128], ncols=128, slot=wsl, col0=0)
                    load_w(w_ao[:, 128 * dc:128 * dc + 128], ncols=128, slot=wsl, col0=128)
                    for tt in range(4):
                        cs = slice(512 * tt, 512 * tt + 512)
                        pg_ = psb[nxt("p4", 4)]
                        py_ = psb[nxt("p4", 4)]
                        mm_group(pg_.ap, pg_, [(wsl.ap[:, k, 0:128], h1T.ap[:, k, cs]) for k in range(8)], [wsl, h1T])
                        mm_group(py_.ap, py_, [(wsl.ap[:, k, 128:256], oT.ap[:, k, cs]) for k in range(8)], [wsl, oT])
                        g_ = gsb[0]
                        op("act", lambda e: e.activation(out=g_.ap, in_=pg_.ap, func=AF.Sigmoid, bias=V(V_GATEB + 8 + dc)), [pg_, vecs], [g_])
                        op("dve", lambda e: e.tensor_tensor(out=mixT.ap[:, dc, AO + 512 * tt:AO + 512 * tt + 512], in0=py_.ap, in1=g_.ap, op=ALU.mult), [py_, g_], [mixT])
                        if silu_units:
                            silu_unit(*silu_units.pop(0))
                while silu_units:
                    silu_unit(*silu_units.pop(0))
                if dbg:
                    dma("sp", dbg_d["d_saT"], cvT.ap, reads=[cvT])
                for dc in range(8):
                    wsl = ring[ring_i[0] % NRING]
                    ring_i[0] += 1
                    load_w(w_in[:, 5120 + 128 * dc:5120 + 128 * dc + 128], ncols=128, slot=wsl, col0=0)
                    load_w(w_co[:, 128 * dc:128 * dc + 128], ncols=128, slot=wsl, col0=128)
                    for tt in range(4):
                        cs = slice(512 * tt, 512 * tt + 512)
                        mcs = slice(AO + 512 * tt, AO + 512 * tt + 512)
                        pg_ = psb[nxt("p4", 4)]
                        py_ = psb[nxt("p4", 4)]
                        mm_group(pg_.ap, pg_, [(wsl.ap[:, k, 0:128], h1T.ap[:, k, cs]) for k in range(8)], [wsl, h1T])
                        mm_group(py_.ap, py_, [(wsl.ap[:, k, 128:256], cvT.ap[:, k, cs]) for k in range(8)], [wsl, cvT])
                        g_ = gsb[1]
                        t_ = t1[tt % 2]
                        op("act", lambda e: e.activation(out=g_.ap, in_=pg_.ap, func=AF.Sigmoid, bias=V(V_GATEB + dc)), [pg_, vecs], [g_])
                        op("dve", lambda e: e.tensor_tensor(out=t_.ap, in0=py_.ap, in1=g_.ap, op=ALU.mult), [py_, g_], [t_])
                        op("dve", lambda e: e.tensor_tensor(out=mixT.ap[:, dc, mcs], in0=mixT.ap[:, dc, mcs], in1=t_.ap, op=ALU.add), [mixT, t_], [mixT])
                if dbg:
                    dma("sp", dbg_d["d_mixT"], mixT.ap[:, :, AO:AO + S], reads=[mixT])
                if stop_after not in ("mix", "mix+"):
                    pre["wo"] = [load_w(w_out[:, 512 * cb:512 * cb + 512]) for cb in range(2)]
                sc.barrier()
                if stop_after == "mix":
                    stop_now = True
                    break
                if stop_after == "mix+":
                    skip_rest = True
                    break
                L = Region(R0)
                h2T = L.get([128, 8, S], BF16)
                T = Region(R2)
                xt = [T.get([128, D], F32) for i in range(3)]
                x1t = [T.get([128, D], F32) for i in range(3)]
                junk = T.get([128, D], BF16)
                xn = [T.get([128, D], BF16) for i in range(2)]
                ssq = [T.get([128, 8], F32) for i in range(3)]
                wo = pre.pop("wo")
                for t in range(17):
                    if t < 16:
                        xb, x1b = xt[t % 3], x1t[t % 3]
                        dma("sp", xb.ap, xs[128 * t:128 * t + 128, :], writes=[xb])
                        for cb in range(2):
                            p_ = psb[nxt("p4", 4)]
                            mm_group(p_.ap, p_, [(mixT.ap[:, k, AO + 128 * t:AO + 128 * t + 128], wo[cb].ap[:, k, :]) for k in range(8)], [mixT, wo[cb]])
                            op("dve", lambda e: e.tensor_tensor(out=x1b.ap[:, 512 * cb:512 * cb + 512], in0=p_.ap, in1=xb.ap[:, 512 * cb:512 * cb + 512], op=ALU.add), [p_, xb], [x1b])
                        dma("sp", x1scr[128 * t:128 * t + 128, :], x1b.ap, reads=[x1b])
                        rms_stage1(None, x1b, junk, ssq[t % 3])
                    if t >= 1:
                        rms_stage2(x1t[(t - 1) % 3], ssq[(t - 1) % 3], xn[(t - 1) % 2], rows.ap[:, D:2 * D], h2T, t - 1)
                if dbg:
                    dma("sp", dbg_d["d_h2T"], h2T.ap, reads=[h2T])
                sc.barrier()
                if stop_after == "x1":
                    stop_now = True
                    break
                if stop_after == "x1+":
                    skip_rest = True
                    break
            if stop_now:
                break
            if skip_rest:
                continue
            if stop_after == "conly":
                L = Region(R0)
                h2T = L.get([128, 8, S], BF16)
            T = Region(L.off)
            gT = T.get([128, 22, 1024], BF16)
            wdn = [T.get([128, D], BF16) for kk in range(22)]
            uT = [[T.get([128, 8 + 1024], BF16) for vg in range(2)] for i in range(2)]
            halo = T.get([128, NFC, 2], BF16)
            dgf = [T.get([128, 6, 128], BF16) for i in range(2)]
            sgf = [T.get([128, 512], F32) for i in range(2)]
            x1r = [T.get([128, D], F32) for i in range(2)]
            ot = [T.get([128, D], F32) for i in range(2)]
            UO = 8
            for i in range(2):
                for vg in range(2):
                    op("dve", lambda e: e.memset(uT[i][vg].ap[:, 0:UO], 0.0), [], [uT[i][vg]])
            for hf in range(2):
                for j in range(22):
                    wsl = ring[ring_i[0] % NRING]
                    ring_i[0] += 1
                    load_w(w_up[:, 128 * j:128 * j + 128], ncols=128, slot=wsl, col0=0)
                    load_w(w_up[:, DFF + 128 * j:DFF + 128 * j + 128], ncols=128, slot=wsl, col0=128)
                    if hf == 0:
                        dma("pool", wdn[j].ap, w_dn[128 * j:128 * j + 128, :], writes=[wdn[j]])
                    dg_ = dgf[j % 2]
                    for vg in range(2):
                        ch = j + 22 * vg
                        for jj in range(3):
                            op("dve", lambda e: e.tensor_scalar(out=dg_.ap[:, 3 * vg + jj, :], in0=ident.ap, scalar1=V(V_FCW + ch * 3 + jj), scalar2=None, op0=ALU.mult),
                               [ident, vecs], [dg_])
                    ub = uT[j % 2]
                    for vg in range(2):
                        ch = j + 22 * vg
                        if hf == 1:
                            op("dve", lambda e: e.tensor_copy(out=ub[vg].ap[:, UO - 2:UO], in_=halo.ap[:, ch, :]), [halo], [ub[vg]])
                        for tt in range(2):
                            tok = slice(1024 * hf + 512 * tt, 1024 * hf + 512 * tt + 512)
                            p_ = psb[nxt("p4", 4)]
                            mm_group(p_.ap, p_, [(wsl.ap[:, k, 128 * vg:128 * vg + 128], h2T.ap[:, k, tok]) for k in range(8)], [wsl, h2T])
                            op("act", lambda e: e.activation(out=ub[vg].ap[:, UO + 512 * tt:UO + 512 * tt + 512], in_=p_.ap, func=AF.Copy), [p_], [ub[vg]])
                        if hf == 0:
                            op("dve", lambda e: e.tensor_copy(out=halo.ap[:, ch, :], in_=ub[vg].ap[:, UO + 1022:UO + 1024]), [ub[vg]], [halo])
                    for tt in range(2):
                        pcv = psb[4 + nxt("b1", 3)]
                        pcg = psb[4 + nxt("b1", 3)]
                        for vg, pc_ in ((1, pcg), (0, pcv)):
                            mm_group(pc_.ap, pc_, [(dg_.ap[:, 3 * vg + jj, :], ub[vg].ap[:, UO - 2 + 512 * tt + jj:UO - 2 + 512 * tt + jj + 512]) for jj in range(3)], [dg_, ub[vg]])
                        s_ = sgf[tt % 2]
                        op("act", lambda e: e.activation(out=s_.ap, in_=pcg.ap, func=AF.Silu, bias=V(V_FCB + 22 + j)), [pcg, vecs], [s_])
                        op("dve", lambda e: e.scalar_tensor_tensor(out=gT.ap[:, j, 512 * tt:512 * tt + 512], in0=pcv.ap, scalar=V(V_FCB + j), in1=s_.ap, op0=ALU.add, op1=ALU.mult),
                           [pcv, vecs, s_], [gT])
                if dbg and hf == 0:
                    dma("sp", dbg_d["d_gT"], gT.ap, reads=[gT])
                for t8 in range(8):
                    t = 8 * hf + t8
                    xr, ob_ = x1r[t8 % 2], ot[t8 % 2]
                    dma("sp", xr.ap, x1scr[128 * t:128 * t + 128, :], writes=[xr])
                    for cb in range(2):
                        p_ = psb[nxt("p4", 4)]
                        mm_group(p_.ap, p_, [(gT.ap[:, j, 128 * t8:128 * t8 + 128], wdn[j].ap[:, 512 * cb:512 * cb + 512]) for j in range(22)], [gT] + wdn)
                        op("dve", lambda e: e.tensor_tensor(out=ob_.ap[:, 512 * cb:512 * cb + 512], in0=p_.ap, in1=xr.ap[:, 512 * cb:512 * cb + 512], op=ALU.add), [p_, xr], [ob_])
                    dma("sp", outs[128 * t:128 * t + 128, :], ob_.ap, reads=[ob_])
            if s + 1 < nseq:
                pre["wv"] = [load_w(w_in[:, 4096 + 512 * cb:4096 + 512 * cb + 512]) for cb in range(2)]
            sc.barrier()
        sc.barrier(engines=("sp",))
    return nc


def _host_layout(inputs):
    f = lambda a: np.ascontiguousarray(np.asarray(a, dtype=np.float32))
    vecs = np.zeros((128, NV), np.float32)
    vecs[:, V_GATEB:V_GATEB + 16] = f(inputs["gate_b"][0]).reshape(16, 128).T
    vecs[:, V_CONVB:V_CONVB + 8] = f(inputs["conv_b"][0]).reshape(8, 128).T
    vecs[:, V_CNG:V_CNG + 8] = f(inputs["conv_norm_g"][0]).reshape(8, 128).T
    vecs[:, V_FCB:V_FCB + NFC] = f(inputs["ffn_conv_b"][0]).reshape(NFC, 128).T
    vecs[:, V_CW:V_CW + 248] = f(inputs["conv_w"][0]).reshape(31, 8, 128).transpose(2, 1, 0).reshape(128, 248)
    vecs[:, V_FCW:V_FCW + NFC * 3] = f(inputs["ffn_conv_w"][0]).reshape(3, NFC, 128).transpose(2, 1, 0).reshape(128, NFC * 3)
    vecs[:, V_QG] = np.tile(f(inputs["q_norm_g"][0]), 2)
    vecs[:, V_KG] = np.tile(f(inputs["k_norm_g"][0]), 2)
    rows = np.zeros((128, 2 * D), np.float32)
    rows[:, 0:D] = np.broadcast_to(f(inputs["norm1_g"][0])[None, :], (128, D))
    rows[:, D:] = np.broadcast_to(f(inputs["norm2_g"][0])[None, :], (128, D))
    cst = np.zeros((128, NCST), np.float32)
    cst[:, C_ID:C_ID + 128] = np.eye(128, dtype=np.float32)
    cst[0:64, C_BLK:C_BLK + 64] = 1.0
    cst[64:128, C_BLK + 64:C_BLK + 128] = 1.0
    k_i = np.arange(128)[:, None]
    q_i = np.arange(128)[None, :]
    cst[:, C_ST:C_ST + 128] = q_i + 128 - k_i
    cst[:, C_ST + 128:C_ST + 256] = np.maximum(q_i - k_i, 0)
    cst[:, C_MK:C_MK + 128] = (k_i >= q_i)
    cst[:, C_MK + 128:C_MK + 256] = (q_i >= k_i)
    shared = {
        "w_in": f(inputs["w_in"][0]), "w_conv_out": f(inputs["w_conv_out"][0]), "w_attn_out": f(inputs["w_attn_out"][0]),
        "w_out": f(inputs["w_out"][0]), "w_up": f(inputs["w_up"][0]), "w_down": f(inputs["w_down"][0]),
        "vecs": vecs, "rows": rows, "consts": cst,
    }
    return shared


def kernel(**inputs):
    x = np.asarray(inputs["x"], dtype=np.float32)
    B = x.shape[0]
    shared = _host_layout(inputs)
    nc = build_nc(SEQ_PER_CORE)
    in_maps = []
    for c in range(NCORES):
        m = dict(shared)
        m["x"] = np.ascontiguousarray(x[c * SEQ_PER_CORE:(c + 1) * SEQ_PER_CORE].reshape(SEQ_PER_CORE * S, D))
        in_maps.append(m)
    res = run_bass_kernel_spmd(nc, in_maps, core_ids=list(range(NCORES)))
    out = np.concatenate([np.asarray(r["out"], dtype=np.float32).reshape(SEQ_PER_CORE, S, D) for r in res.results], axis=0)
    return out
```

```python
import numpy as np
import concourse.bass as bass
import concourse.mybir as mybir
from concourse.bass_utils import run_bass_kernel_spmd
from contextlib import ExitStack

F32 = mybir.dt.float32
BF16 = mybir.dt.bfloat16
ALU = mybir.AluOpType
AF = mybir.ActivationFunctionType

S = 2048
D = 1024
NH = 16
DFF = 2816
IN_COLS = 7168
EPS = 1e-6
NCORES = 8
SEQ_PER_CORE = 2
NFC = 2 * DFF // 128
ND = 10
KB = 1024

V_GATEB = 0
V_CONVB = 16
V_CNG = 24
V_FCB = 32
V_CW = 76
V_FCW = V_CW + 8 * 31
V_QG = V_FCW + NFC * 3
V_KG = V_QG + 1
NV = V_KG + 1

C_ID, C_BLK, C_ST, C_MK, NCST = 0, 128, 256, 512, 768


class Buf:
    __slots__ = ("ap", "w", "r")

    def __init__(self, ap):
        self.ap = ap
        self.w = None
        self.r = {}


class Sched:
    def __init__(self, nc, stack):
        self.nc = nc
        self.eng = dict(pe=nc.tensor, act=nc.scalar, dve=nc.vector, pool=nc.gpsimd, sp=nc.sync)
        self.msem = {e: stack.enter_context(nc.semaphore("ms_" + e)) for e in ("pe", "act", "dve", "pool")}
        self.mcnt = {e: 0 for e in self.msem}
        self.dsem = {q: [stack.enter_context(nc.semaphore("d_%s_%d" % (q, i))) for i in range(ND)] for q in ("sp", "pool")}
        self.dcnt = {q: [0] * ND for q in self.dsem}
        self.dnext = {q: 0 for q in self.dsem}
        self.seen = {e: {} for e in self.eng}
        self.pending = {e: False for e in self.msem}

    def _wait(self, e, ev):
        sem, val, key, prod = ev
        if self.seen[e].get(key, 0) >= val:
            return
        self.eng[e].wait_ge(sem, val)
        self.seen[e][key] = val

    def _deps(self, e, reads, writes):
        for b in reads:
            if b.w is not None:
                if b.w[3] == e and e == "pe":
                    continue
                self._wait(e, b.w)
        for b in writes:
            if b.w is not None and not (b.w[3] == e and e == "pe"):
                self._wait(e, b.w)
            for ev in b.r.values():
                if not (ev[3] == e and e == "pe"):
                    self._wait(e, ev)

    def _record(self, ev, reads, writes):
        for b in reads:
            b.r[ev[2]] = ev
        for b in writes:
            b.w = ev
            b.r = {}

    def op(self, e, fn, reads=(), writes=(), signal=True):
        self._deps(e, reads, writes)
        ins = fn(self.eng[e])
        if signal:
            self.mcnt[e] += 1
            ins.then_inc(self.msem[e], 1)
            ev = (self.msem[e], self.mcnt[e], e, e)
            self.pending[e] = False
        else:
            ev = (self.msem[e], self.mcnt[e] + 1, e, e)
            self.pending[e] = True
        self._record(ev, reads, writes)
        return ins

    def dma(self, q, out_ap, in_ap, reads=(), writes=()):
        i = self.dnext[q]
        self.dnext[q] = (i + 1) % ND
        sem = self.dsem[q][i]
        key = "d_%s_%d" % (q, i)
        if self.dcnt[q][i] > 0:
            self._wait(q, (sem, self.dcnt[q][i], key, "dma"))
        self._deps(q, reads, writes)
        self.eng[q].dma_start(out=out_ap, in_=in_ap).then_inc(sem, 16)
        self.dcnt[q][i] += 16
        ev = (sem, self.dcnt[q][i], key, "dma")
        self._record(ev, reads, writes)

    def all_events(self):
        evs = []
        for e in self.msem:
            assert not self.pending[e], e
            if self.mcnt[e] > 0:
                evs.append((self.msem[e], self.mcnt[e], e, e))
        for q in self.dsem:
            for i in range(ND):
                if self.dcnt[q][i] > 0:
                    evs.append((self.dsem[q][i], self.dcnt[q][i], "d_%s_%d" % (q, i), "dma"))
        return evs

    def barrier(self, engines=("pe", "act", "dve", "pool", "sp")):
        evs = self.all_events()
        for e in engines:
            for ev in evs:
                self._wait(e, ev)


def build_nc(nseq=SEQ_PER_CORE, dbg=False, stop_after=None):
    nc = bass.Bass("TRN2", target_bir_lowering=False)
    dt = nc.dram_tensor
    x_d = dt("x", [nseq * S, D], F32, kind="ExternalInput").ap()
    w_in = dt("w_in", [D, IN_COLS], F32, kind="ExternalInput").ap()
    w_co = dt("w_conv_out", [D, D], F32, kind="ExternalInput").ap()
    w_ao = dt("w_attn_out", [D, D], F32, kind="ExternalInput").ap()
    w_out = dt("w_out", [D, D], F32, kind="ExternalInput").ap()
    w_up = dt("w_up", [D, 2 * DFF], F32, kind="ExternalInput").ap()
    w_dn = dt("w_down", [DFF, D], F32, kind="ExternalInput").ap()
    vecs_d = dt("vecs", [128, NV], F32, kind="ExternalInput").ap()
    rows_d = dt("rows", [128, 2 * D], F32, kind="ExternalInput").ap()
    cst_d = dt("consts", [128, NCST], F32, kind="ExternalInput").ap()
    out_d = dt("out", [nseq * S, D], F32, kind="ExternalOutput").ap()
    vscr = dt("vscr", [S, 8, 256], BF16, kind="Internal").ap()
    x1scr = dt("x1scr", [S, D], F32, kind="Internal").ap()
    dbg_d = {}
    if dbg:
        for nm, shp, ty in (("d_h1T", [128, 8, S], BF16), ("d_qT", [128, S], BF16), ("d_kT", [128, S], BF16),
                            ("d_oT", [128, 8, S], BF16), ("d_saT", [128, 8, S], BF16), ("d_mixT", [128, 8, S], BF16),
                            ("d_h2T", [128, 8, S], BF16), ("d_gT", [128, 22, 1024], BF16)):
            dbg_d[nm] = dt(nm, shp, ty, kind="ExternalOutput").ap()

    with ExitStack() as top:
        sc = Sched(nc, top)
        op, dma = sc.op, sc.dma
        base0 = nc.sbuf_base
        base = ((base0 + 63) // 64) * 64
        ARENA = 204 * KB
        top.enter_context(nc.sbuf_tensor("arena", [128, (ARENA + base - base0 + 64) // 2], BF16))
        uid = [0]

        def at(off, shape, dtype):
            uid[0] += 1
            n = 1
            for d_ in shape[1:]:
                n *= d_
            nbytes = n * (4 if dtype == F32 else 2)
            assert off % 32 == 0 and off + nbytes <= ARENA, (off, nbytes)
            h = nc.alloc_sbuf_tensor_at("t%d" % uid[0], list(shape), dtype, offset=base + off)
            return Buf(h.ap()), off + ((nbytes + 31) // 32) * 32

        class Region:
            def __init__(self, off):
                self.off = off

            def get(self, shape, dtype):
                b, self.off = at(self.off, shape, dtype)
                return b

        G = Region(0)
        cst = G.get([128, NCST], F32)
        vecs = G.get([128, NV], F32)
        rows = G.get([128, 2 * D], F32)
        ident = G.get([128, 128], BF16)
        blk = G.get([128, 128], BF16)
        ones = G.get([128, 128], BF16)
        NRING = 3
        ring = [G.get([128, 8, 512], BF16) for i in range(NRING)]
        ring_i = [0]
        R0 = G.off
        psb = [Buf(top.enter_context(nc.psum_tensor("ps%d" % i, [128, 512], F32)).ap()) for i in range(7)]
        pst = Buf(top.enter_context(nc.psum_tensor("pst", [128, 1024], BF16)).ap())

        def V(col, n=1):
            return vecs.ap[:, col:col + n]

        dma("sp", cst.ap, cst_d, writes=[cst])
        dma("sp", vecs.ap, vecs_d, writes=[vecs])
        dma("sp", rows.ap, rows_d, writes=[rows])
        op("dve", lambda e: e.tensor_copy(out=ident.ap, in_=cst.ap[:, C_ID:C_ID + 128]), [cst], [ident])
        op("dve", lambda e: e.tensor_copy(out=blk.ap, in_=cst.ap[:, C_BLK:C_BLK + 128]), [cst], [blk])
        op("dve", lambda e: e.memset(ones.ap, 1.0), [], [ones])
        op("dve", lambda e: e.tensor_scalar(out=V(V_KG), in0=V(V_KG), scalar1=8.0, scalar2=None, op0=ALU.mult), [vecs], [vecs])

        def eidx(h, gi):
            return 16 - (h + 1) + 4 * gi

        def load_w(src_ap, ncols=512, kchunks=8, slot=None, col0=0):
            if slot is None:
                slot = ring[ring_i[0] % NRING]
                ring_i[0] += 1
            dma("pool", slot.ap[:, 0:kchunks, col0:col0 + ncols], src_ap.rearrange("(k p) c -> p k c", p=128), writes=[slot])
            return slot

        def mm_group(out_ap, out_buf, pairs, reads, start=True, sig=True):
            n = len(pairs)
            for i, (l, r) in enumerate(pairs):
                op("pe", lambda e: e.matmul(out_ap, lhsT=l, rhs=r, start=(start and i == 0), stop=(i == n - 1), skip_group_check=True),
                   reads, [out_buf], signal=(sig and i == n - 1))

        rrc = {"p4": 0, "st": 0, "pb": 0, "o": 0, "pq": 0, "b1": 0}

        def nxt(key, n):
            v = rrc[key] % n
            rrc[key] += 1
            return v

        def rms_stage1(src_dram_rows, xb, junk, sq):
            if src_dram_rows is not None:
                dma("sp", xb.ap, src_dram_rows, writes=[xb])
            op("act", lambda e: e.activation(out=junk.ap, in_=xb.ap, func=AF.Square, accum_out=sq.ap[:, 0:1]), [xb], [junk, sq])
            op("dve", lambda e: e.tensor_scalar(out=sq.ap[:, 1:2], in0=sq.ap[:, 0:1], scalar1=1.0 / D, scalar2=EPS, op0=ALU.mult, op1=ALU.add), [sq], [sq])
            op("act", lambda e: e.activation(out=sq.ap[:, 1:2], in_=sq.ap[:, 1:2], func=AF.Ln), [sq], [sq])
            op("act", lambda e: e.activation(out=sq.ap[:, 1:2], in_=sq.ap[:, 1:2], func=AF.Exp, scale=-0.5), [sq], [sq])

        def rms_stage2(xb, sq, xnb, grow_ap, dstT, t):
            op("dve", lambda e: e.scalar_tensor_tensor(out=xnb.ap, in0=xb.ap, scalar=sq.ap[:, 1:2], in1=grow_ap, op0=ALU.mult, op1=ALU.mult), [xb, sq, rows], [xnb])
            for k in range(8):
                op("pe", lambda e: e.transpose(out=pst.ap[:, 128 * k:128 * k + 128], in_=xnb.ap[:, 128 * k:128 * k + 128], identity=ident.ap),
                   [xnb, ident], [pst], signal=(k == 7))
            op("act", lambda e: e.activation(out=dstT.ap[:, :, 128 * t:128 * t + 128], in_=pst.ap.rearrange("p (k t) -> p k t", k=8), func=AF.Copy), [pst], [dstT])

        pre = {}
        for s in range(nseq):
            xs = x_d[s * S:(s + 1) * S, :]
            outs = out_d[s * S:(s + 1) * S, :]
            stop_now = False
            skip_rest = False
            for _once in ([] if stop_after == "conly" else [0]):
                L = Region(R0)
                h1T = L.get([128, 8, S], BF16)
                oT = L.get([128, 8, S], BF16)
                R1 = L.off
                T = Region(R1)
                xt = [T.get([128, D], F32) for i in range(3)]
                junk = T.get([128, D], BF16)
                xn = [T.get([128, D], BF16) for i in range(2)]
                ssq = [T.get([128, 8], F32) for i in range(3)]
                vst = [T.get([128, 8, 256], BF16) for i in range(2)]
                for i in range(2):
                    op("dve", lambda e: e.memset(vst[i].ap, 1.0), [], [vst[i]])
                wv = pre.pop("wv", None) or [load_w(w_in[:, 4096 + 512 * cb:4096 + 512 * cb + 512]) for cb in range(2)]
                for t in range(17):
                    if t < 16:
                        rms_stage1(xs[128 * t:128 * t + 128, :], xt[t % 3], junk, ssq[t % 3])
                    if t >= 1:
                        rms_stage2(xt[(t - 1) % 3], ssq[(t - 1) % 3], xn[(t - 1) % 2], rows.ap[:, 0:D], h1T, t - 1)
                for t in range(16):
                    vb = vst[t % 2]
                    for cb in range(2):
                        pb_ = psb[nxt("p4", 4)]
                        mm_group(pb_.ap, pb_, [(h1T.ap[:, k, 128 * t:128 * t + 128], wv[cb].ap[:, k, :]) for k in range(8)], [h1T, wv[cb]])
                        pv4 = pb_.ap.rearrange("p (h c) -> p h c", h=4)
                        op("act", lambda e: e.activation(out=vb.ap[:, 4 * cb:4 * cb + 4, 0:64], in_=pv4[:, :, 0:64], func=AF.Copy), [pb_], [vb])
                        op("dve", lambda e: e.tensor_copy(out=vb.ap[:, 4 * cb:4 * cb + 4, 192:256], in_=pv4[:, :, 64:128]), [pb_], [vb])
                    dma("sp", vscr[128 * t:128 * t + 128, :, :], vb.ap, reads=[vb])
                if dbg:
                    dma("sp", dbg_d["d_h1T"], h1T.ap, reads=[h1T])
                if stop_after not in ("v", "v+"):
                    pre["wq"] = load_w(w_in[:, 2048:2048 + 512])
                    pre["wk"] = load_w(w_in[:, 3072:3072 + 512])
                sc.barrier()
                if stop_after == "v":
                    stop_now = True
                    break
                if stop_after == "v+":
                    skip_rest = True
                    break
                T = Region(R1)
                etab = T.get([128, 24, 256], BF16)
                etmp = T.get([128, 256], F32)
                qT = [T.get([128, S], BF16) for i in range(2)]
                kT = [T.get([128, S], BF16) for i in range(2)]
                sqb = [T.get([128, 512], BF16) for i in range(2)]
                Rb = [T.get([128, 512], F32) for i in range(2)]
                varr = [[T.get([128, 16, 256], BF16) for g in range(3)] for i in range(2)]
                P3s = [T.get([128, S], BF16) for i in range(2)]
                Pb = [T.get([128, 512], BF16) for i in range(8)]
                rden = T.get([128, 512], F32)
                for ei in range(24):
                    c = 2.0 ** (ei / 2.0 - 8.0)
                    op("act", lambda e: e.activation(out=etmp.ap, in_=cst.ap[:, C_ST:C_ST + 256], func=AF.Exp, scale=-c), [cst], [etmp])
                    op("dve", lambda e: e.tensor_tensor(out=etab.ap[:, ei, :], in0=etmp.ap, in1=cst.ap[:, C_MK:C_MK + 256], op=ALU.mult), [etmp, cst], [etab])
                wq = wk = None
                pending = []

                def tick():
                    for p_ in pending:
                        p_[0] -= 1
                    while pending and pending[0][0] <= 0:
                        pending.pop(0)[1]()

                def flush():
                    while pending:
                        pending.pop(0)[1]()

                wst = {}

                def project(hp):
                    if hp == 0:
                        wst["q"], wst["k"] = pre.pop("wq"), pre.pop("wk")
                    elif hp % 4 == 0:
                        wst["q"] = load_w(w_in[:, 2048 + 128 * hp:2048 + 128 * hp + 512])
                        wst["k"] = load_w(w_in[:, 3072 + 128 * hp:3072 + 128 * hp + 512])
                    wq, wk = wst["q"], wst["k"]
                    va = varr[hp % 2]
                    for g, r in enumerate((1, 4, 16)):
                        if r == 1:
                            dma("sp", va[g].ap, vscr.rearrange("(b i) h c -> i b h c", i=128)[:, :, hp, :], writes=[va[g]])
                        else:
                            for rho in range(r):
                                nb = 16 // r
                                src = vscr.rearrange("(b i r) h c -> r i b h c", r=r, i=128)[rho, :, :, hp, :]
                                dma("sp", va[g].ap[:, rho * nb:(rho + 1) * nb, :], src, writes=[va[g]])
                    qb, kb = qT[hp % 2], kT[hp % 2]
                    coff = 128 * (hp % 4)
                    for (wt_, dstb, gcol) in ((wq, qb, V_QG), (wk, kb, V_KG)):
                        for tt in range(4):
                            pa_ = psb[4 + nxt("pq", 2)]
                            sq_, R_ = sqb[tt % 2], Rb[tt % 2]
                            mm_group(pa_.ap, pa_, [(wt_.ap[:, k, coff:coff + 128], h1T.ap[:, k, 512 * tt:512 * tt + 512]) for k in range(8)], [wt_, h1T])
                            op("act", lambda e: e.activation(out=sq_.ap, in_=pa_.ap, func=AF.Square), [pa_], [sq_])
                            p6 = psb[6]
                            mm_group(p6.ap, p6, [(blk.ap, sq_.ap)], [blk, sq_])
                            op("dve", lambda e: e.tensor_scalar(out=R_.ap, in0=p6.ap, scalar1=64.0 * EPS, scalar2=None, op0=ALU.add), [p6], [R_])
                            op("act", lambda e: e.activation(out=R_.ap, in_=R_.ap, func=AF.Ln), [R_], [R_])
                            op("act", lambda e: e.activation(out=R_.ap, in_=R_.ap, func=AF.Exp, scale=-0.5), [R_], [R_])
                            op("dve", lambda e: e.scalar_tensor_tensor(out=dstb.ap[:, 512 * tt:512 * tt + 512], in0=pa_.ap, scalar=V(gcol), in1=R_.ap, op0=ALU.mult, op1=ALU.mult),
                               [pa_, vecs, R_], [dstb])
                    if dbg and hp == 0:
                        dma("sp", dbg_d["d_qT"], qb.ap, reads=[qb])
                        dma("sp", dbg_d["d_kT"], kb.ap, reads=[kb])

                project(0)
                for hp in range(8):
                    va = varr[hp % 2]
                    qb, kb = qT[hp % 2], kT[hp % 2]
                    for e_ in range(2):
                        if e_ == 1 and hp + 1 < 8:
                            project(hp + 1)
                        h = 2 * hp + e_
                        p0 = 64 * e_
                        qh = qb.ap[p0:p0 + 64, :]
                        kh = kb.ap[p0:p0 + 64, :]
                        P3 = P3s[h % 2]

                        def stile(dst_ap, dst_buf, kcols, qcols, sig, kh=kh, qh=qh, kb=kb, qb=qb):
                            op("pe", lambda e: e.matmul(dst_ap, lhsT=kh[:, kcols], rhs=qh[:, qcols], start=True, stop=True, skip_group_check=True),
                               [kb, qb], [dst_buf], signal=sig)

                        def make_pv(units, c, e_=e_, p0=p0, hp=hp, va=va):
                            def emit():
                                ob = psb[2 + nxt("o", 2)]
                                nu = len(units)
                                for ui, (pbuf, p_ap, gi, tile_i, ocols) in enumerate(units):
                                    op("pe", lambda e: e.matmul(ob.ap[:, ocols], lhsT=va[gi].ap[:, tile_i, 128 * e_:128 * e_ + 128], rhs=p_ap,
                                                                start=(ui == 0), stop=(ui == nu - 1), skip_group_check=True),
                                       [va[gi], pbuf], [ob], signal=(ui == nu - 1))
                                dq = 64 * (1 - e_)
                                op("act", lambda e: e.activation(out=rden.ap[p0:p0 + 64, :], in_=ob.ap[dq:dq + 64, :], func=AF.Ln), [ob], [rden])
                                op("act", lambda e: e.activation(out=rden.ap[p0:p0 + 64, :], in_=rden.ap[p0:p0 + 64, :], func=AF.Exp, scale=-1.0), [rden], [rden])
                                op("dve", lambda e: e.tensor_tensor(out=oT.ap[p0:p0 + 64, hp, 512 * c:512 * c + 512], in0=ob.ap[p0:p0 + 64, :], in1=rden.ap[p0:p0 + 64, :], op=ALU.mult),
                                   [ob, rden], [oT])
                            return emit

                        e3 = eidx(h, 2)
                        for ch in range(4):
                            stb = psb[(0, 1, 6, 4, 5)[nxt("st", 5)]]
                            for rr in range(4):
                                rho = 4 * ch + rr
                                stile(stb.ap[:, 128 * rr:128 * rr + 128], stb, slice(rho, S, 16), slice(rho, S, 16), rr == 3)
                            pv = P3.ap[:, 512 * ch:512 * ch + 512]
                            pv3 = pv.rearrange("p (a b) -> p a b", a=4)
                            e_ap = etab.ap[:, e3:e3 + 1, 128:256].to_broadcast([128, 4, 128])
                            op("act", lambda e: e.activation(out=pv, in_=stb.ap, func=AF.Exp), [stb], [P3])
                            op("dve", lambda e: e.tensor_tensor(out=pv3, in0=pv3, in1=e_ap, op=ALU.mult), [P3, etab], [P3])
                            tick()
                        for c in range(4):
                            units = []
                            for gi, r in ((0, 1), (1, 4)):
                                ei = eidx(h, gi)
                                for half in range(2):
                                    stb = psb[(0, 1, 6, 4, 5)[nxt("st", 5)]]
                                    pbuf = Pb[nxt("pb", 8)]
                                    for jj in range(2):
                                        if gi == 0:
                                            b = 4 * c + 2 * half + jj
                                            has_prev = b > 0
                                            kprev = slice(128 * (b - 1), 128 * b)
                                            kown = slice(128 * b, 128 * b + 128)
                                            ocols = slice(128 * (b - 4 * c), 128 * (b - 4 * c) + 128)
                                            tprev, town = b - 1, b
                                        else:
                                            rho = 2 * half + jj
                                            has_prev = c > 0
                                            kprev = slice(rho + 512 * (c - 1), 512 * c, 4)
                                            kown = slice(rho + 512 * c, 512 * c + 512, 4)
                                            ocols = slice(rho, 512, 4)
                                            tprev, town = rho * 4 + c - 1, rho * 4 + c
                                        o0 = 256 * jj
                                        if has_prev:
                                            stile(stb.ap[:, o0:o0 + 128], stb, kprev, kown, False)
                                            units.append((pbuf, pbuf.ap[:, o0:o0 + 128], gi, tprev, ocols))
                                        stile(stb.ap[:, o0 + 128:o0 + 256], stb, kown, kown, jj == 1)
                                        units.append((pbuf, pbuf.ap[:, o0 + 128:o0 + 256], gi, town, ocols))
                                    e_ap = etab.ap[:, ei:ei + 1, :].to_broadcast([128, 2, 256])
                                    pb3 = pbuf.ap.rearrange("p (a b) -> p a b", a=2)
                                    op("act", lambda e: e.activation(out=pbuf.ap, in_=stb.ap, func=AF.Exp), [stb], [pbuf])
                                    op("dve", lambda e: e.tensor_tensor(out=pb3, in0=pb3, in1=e_ap, op=ALU.mult), [pbuf, etab], [pbuf])
                                    tick()
                            for rho in range(16):
                                units.append((P3, P3.ap[:, 128 * rho + 32 * c:128 * rho + 32 * c + 32], 2, rho, slice(rho, 512, 16)))
                            pending.append([2, make_pv(units, c)])
                flush()
                if dbg:
                    dma("sp", dbg_d["d_oT"], oT.ap, reads=[oT])
                if stop_after not in ("attn", "attn+"):
                    pre["wa"] = load_w(w_in[:, 0:512])
                    pre["wg"] = load_w(w_in[:, 1024:1024 + 512])
                sc.barrier()
                if stop_after == "attn":
                    stop_now = True
                    break
                if stop_after == "attn+":
                    skip_rest = True
                    break
                T = Region(R1)
                aT = T.get([128, 8, 32 + S], BF16)
                R2 = T.off
                cvT = T.get([128, 8, S], BF16)
                dgb = T.get([128, 31, 128], BF16)
                sg = [T.get([128, 512], F32) for i in range(2)]
                sgm = [T.get([128, 512], F32) for i in range(2)]
                sqv = [T.get([128, 512], BF16) for i in range(2)]
                rr_ = T.get([128, S], F32)
                gsb = [T.get([128, 512], F32) for i in range(2)]
                t1 = [T.get([128, 512], F32) for i in range(2)]
                AO = 32
                op("dve", lambda e: e.memset(aT.ap[:, :, 0:AO], 0.0), [], [aT])
                pend_ss = []
                for cc in range(8):
                    if cc == 0:
                        wa, wg = pre.pop("wa"), pre.pop("wg")
                    elif cc % 4 == 0:
                        wa = load_w(w_in[:, 128 * cc:128 * cc + 512])
                        wg = load_w(w_in[:, 1024 + 128 * cc:1024 + 128 * cc + 512])
                    coff = 128 * (cc % 4)
                    for j in range(31):
                        op("dve", lambda e: e.tensor_scalar(out=dgb.ap[:, j, :], in0=ident.ap, scalar1=V(V_CW + cc * 31 + j), scalar2=None, op0=ALU.mult),
                           [ident, vecs], [dgb])
                    for tt in range(4):
                        pg_ = psb[4 + nxt("b1", 3)]
                        pv_ = psb[4 + nxt("b1", 3)]
                        mm_group(pg_.ap, pg_, [(wg.ap[:, k, coff:coff + 128], h1T.ap[:, k, 512 * tt:512 * tt + 512]) for k in range(8)], [wg, h1T])
                        mm_group(pv_.ap, pv_, [(wa.ap[:, k, coff:coff + 128], h1T.ap[:, k, 512 * tt:512 * tt + 512]) for k in range(8)], [wa, h1T])
                        sg_ = sg[tt % 2]
                        op("act", lambda e: e.activation(out=sg_.ap, in_=pg_.ap, func=AF.Sigmoid), [pg_], [sg_])
                        op("dve", lambda e: e.tensor_tensor(out=aT.ap[:, cc, AO + 512 * tt:AO + 512 * tt + 512], in0=pv_.ap, in1=sg_.ap, op=ALU.mult), [pv_, sg_], [aT])
                    for tt in range(4):
                        pc_ = psb[4 + nxt("b1", 3)]
                        mm_group(pc_.ap, pc_, [(dgb.ap[:, j, :], aT.ap[:, cc, AO - 30 + 512 * tt + j:AO - 30 + 512 * tt + j + 512]) for j in range(31)], [dgb, aT])
                        while pend_ss:
                            pend_ss.pop(0)()
                        sv = sqv[tt % 2]
                        op("act", lambda e: e.activation(out=sv.ap, in_=pc_.ap, func=AF.Square, bias=V(V_CONVB + cc)), [pc_, vecs], [sv])
                        op("act", lambda e: e.activation(out=cvT.ap[:, cc, 512 * tt:512 * tt + 512], in_=pc_.ap, func=AF.Identity, bias=V(V_CONVB + cc)), [pc_, vecs], [cvT])

                        def ss_mm(tt=tt, cc=cc, sv=sv):
                            op("pe", lambda e: e.matmul(psb[tt].ap, lhsT=ones.ap, rhs=sv.ap, start=(cc == 0), stop=(cc == 7), skip_group_check=True),
                               [ones, sv], [psb[tt]], signal=True)
                        pend_ss.append(ss_mm)
                while pend_ss:
                    pend_ss.pop(0)()
                for tt in range(4):
                    op("dve", lambda e: e.tensor_scalar(out=rr_.ap[:, 512 * tt:512 * tt + 512], in0=psb[tt].ap, scalar1=1.0 / D, scalar2=EPS, op0=ALU.mult, op1=ALU.add), [psb[tt]], [rr_])
                op("act", lambda e: e.activation(out=rr_.ap, in_=rr_.ap, func=AF.Ln), [rr_], [rr_])
                op("act", lambda e: e.activation(out=rr_.ap, in_=rr_.ap, func=AF.Exp, scale=-0.5), [rr_], [rr_])
                def silu_unit(cc, tt):
                    sg_, sm_ = sg[tt % 2], sgm[tt % 2]
                    cs = slice(512 * tt, 512 * tt + 512)
                    op("dve", lambda e: e.scalar_tensor_tensor(out=sg_.ap, in0=cvT.ap[:, cc, cs], scalar=V(V_CNG + cc), in1=rr_.ap[:, cs], op0=ALU.mult, op1=ALU.mult),
                       [cvT, vecs, rr_], [sg_])
                    op("act", lambda e: e.activation(out=sm_.ap, in_=sg_.ap, func=AF.Sigmoid), [sg_], [sm_])
                    op("dve", lambda e: e.tensor_tensor(out=cvT.ap[:, cc, cs], in0=sg_.ap, in1=sm_.ap, op=ALU.mult), [sg_, sm_], [cvT])

                silu_units = [(cc, tt) for cc in range(8) for tt in range(4)]
                mixT = aT
                for dc in range(8):
                    wsl = ring[ring_i[0] % NRING]
                    ring_i[0] += 1
                    load_w(w_in[:, 6144 + 128 * dc:6144 + 128 * dc + 128], ncols=128, slot=wsl, col0=0)
                    load_w(w_ao[:, 128 * dc:128 * dc + 128], ncols=128, slot=wsl, col0=128)
                    for tt in range(4):
                        cs = slice(512 * tt, 512 * tt + 512)
                        pg_ = psb[nxt("p4", 4)]
                        py_ = psb[nxt("p4", 4)]
                        mm_group(pg_.ap, pg_, [(wsl.ap[:, k, 0:128], h1T.ap[:, k, cs]) for k in range(8)], [wsl, h1T])
                        mm_group(py_.ap, py_, [(wsl.ap[:, k, 128:256], oT.ap[:, k, cs]) for k in range(8)], [wsl, oT])
                        g_ = gsb[0]
                        op("act", lambda e: e.activation(out=g_.ap, in_=pg_.ap, func=AF.Sigmoid, bias=V(V_GATEB + 8 + dc)), [pg_, vecs], [g_])
                        op("dve", lambda e: e.tensor_tensor(out=mixT.ap[:, dc, AO + 512 * tt:AO + 512 * tt + 512], in0=py_.ap, in1=g_.ap, op=ALU.mult), [py_, g_], [mixT])
                        if silu_units:
                            silu_unit(*silu_units.pop(0))
                while silu_units:
                    silu_unit(*silu_units.pop(0))
                if dbg:
                    dma("sp", dbg_d["d_saT"], cvT.ap, reads=[cvT])
                for dc in range(8):
                    wsl = ring[ring_i[0] % NRING]
                    ring_i[0] += 1
                    load_w(w_in[:, 5120 + 128 * dc:5120 + 128 * dc + 128], ncols=128, slot=wsl, col0=0)
                    load_w(w_co[:, 128 * dc:128 * dc + 128], ncols=128, slot=wsl, col0=128)
                    for tt in range(4):
                        cs = slice(512 * tt, 512 * tt + 512)
                        mcs = slice(AO + 512 * tt, AO + 512 * tt + 512)
                        pg_ = psb[nxt("p4", 4)]
                        py_ = psb[nxt("p4", 4)]
                        mm_group(pg_.ap, pg_, [(wsl.ap[:, k, 0:128], h1T.ap[:, k, cs]) for k in range(8)], [wsl, h1T])
                        mm_group(py_.ap, py_, [(wsl.ap[:, k, 128:256], cvT.ap[:, k, cs]) for k in range(8)], [wsl, cvT])
                        g_ = gsb[1]
                        t_ = t1[tt % 2]
                        op("act", lambda e: e.activation(out=g_.ap, in_=pg_.ap, func=AF.Sigmoid, bias=V(V_GATEB + dc)), [pg_, vecs], [g_])
                        op("dve", lambda e: e.tensor_tensor(out=t_.ap, in0=py_.ap, in1=g_.ap, op=ALU.mult), [py_, g_], [t_])
                        op("dve", lambda e: e.tensor_tensor(out=mixT.ap[:, dc, mcs], in0=mixT.ap[:, dc, mcs], in1=t_.ap, op=ALU.add), [mixT, t_], [mixT])
                if dbg:
                    dma("sp", dbg_d["d_mixT"], mixT.ap[:, :, AO:AO + S], reads=[mixT])
                if stop_after not in ("mix", "mix+"):
                    pre["wo"] = [load_w(w_out[:, 512 * cb:512 * cb + 512]) for cb in range(2)]
                sc.barrier()
                if stop_after == "mix":
                    stop_now = True
                    break
                if stop_after == "mix+":
                    skip_rest = True
                    break
                L = Region(R0)
                h2T = L.get([128, 8, S], BF16)
                T = Region(R2)
                xt = [T.get([128, D], F32) for i in range(3)]
                x1t = [T.get([128, D], F32) for i in range(3)]
                junk = T.get([128, D], BF16)
                xn = [T.get([128, D], BF16) for i in range(2)]
                ssq = [T.get([128, 8], F32) for i in range(3)]
                wo = pre.pop("wo")
                for t in range(17):
                    if t < 16:
                        xb, x1b = xt[t % 3], x1t[t % 3]
                        dma("sp", xb.ap, xs[128 * t:128 * t + 128, :], writes=[xb])
                        for cb in range(2):
                            p_ = psb[nxt("p4", 4)]
                            mm_group(p_.ap, p_, [(mixT.ap[:, k, AO + 128 * t:AO + 128 * t + 128], wo[cb].ap[:, k, :]) for k in range(8)], [mixT, wo[cb]])
                            op("dve", lambda e: e.tensor_tensor(out=x1b.ap[:, 512 * cb:512 * cb + 512], in0=p_.ap, in1=xb.ap[:, 512 * cb:512 * cb + 512], op=ALU.add), [p_, xb], [x1b])
                        dma("sp", x1scr[128 * t:128 * t + 128, :], x1b.ap, reads=[x1b])
                        rms_stage1(None, x1b, junk, ssq[t % 3])
                    if t >= 1:
                        rms_stage2(x1t[(t - 1) % 3], ssq[(t - 1) % 3], xn[(t - 1) % 2], rows.ap[:, D:2 * D], h2T, t - 1)
                if dbg:
                    dma("sp", dbg_d["d_h2T"], h2T.ap, reads=[h2T])
                sc.barrier()
                if stop_after == "x1":
                    stop_now = True
                    break
                if stop_after == "x1+":
                    skip_rest = True
                    break
            if stop_now:
                break
            if skip_rest:
                continue
            if stop_after == "conly":
                L = Region(R0)
                h2T = L.get([128, 8, S], BF16)
            T = Region(L.off)
            gT = T.get([128, 22, 1024], BF16)
            wdn = [T.get([128, D], BF16) for kk in range(22)]
            uT = [[T.get([128, 8 + 1024], BF16) for vg in range(2)] for i in range(2)]
            halo = T.get([128, NFC, 2], BF16)
            dgf = [T.get([128, 6, 128], BF16) for i in range(2)]
            sgf = [T.get([128, 512], F32) for i in range(2)]
            x1r = [T.get([128, D], F32) for i in range(2)]
            ot = [T.get([128, D], F32) for i in range(2)]
            UO = 8
            for i in range(2):
                for vg in range(2):
                    op("dve", lambda e: e.memset(uT[i][vg].ap[:, 0:UO], 0.0), [], [uT[i][vg]])
            for hf in range(2):
                for j in range(22):
                    wsl = ring[ring_i[0] % NRING]
                    ring_i[0] += 1
                    load_w(w_up[:, 128 * j:128 * j + 128], ncols=128, slot=wsl, col0=0)
                    load_w(w_up[:, DFF + 128 * j:DFF + 128 * j + 128], ncols=128, slot=wsl, col0=128)
                    if hf == 0:
                        dma("pool", wdn[j].ap, w_dn[128 * j:128 * j + 128, :], writes=[wdn[j]])
                    dg_ = dgf[j % 2]
                    for vg in range(2):
                        ch = j + 22 * vg
                        for jj in range(3):
                            op("dve", lambda e: e.tensor_scalar(out=dg_.ap[:, 3 * vg + jj, :], in0=ident.ap, scalar1=V(V_FCW + ch * 3 + jj), scalar2=None, op0=ALU.mult),
                               [ident, vecs], [dg_])
                    ub = uT[j % 2]
                    for vg in range(2):
                        ch = j + 22 * vg
                        if hf == 1:
                            op("dve", lambda e: e.tensor_copy(out=ub[vg].ap[:, UO - 2:UO], in_=halo.ap[:, ch, :]), [halo], [ub[vg]])
                        for tt in range(2):
                            tok = slice(1024 * hf + 512 * tt, 1024 * hf + 512 * tt + 512)
                            p_ = psb[nxt("p4", 4)]
                            mm_group(p_.ap, p_, [(wsl.ap[:, k, 128 * vg:128 * vg + 128], h2T.ap[:, k, tok]) for k in range(8)], [wsl, h2T])
                            op("act", lambda e: e.activation(out=ub[vg].ap[:, UO + 512 * tt:UO + 512 * tt + 512], in_=p_.ap, func=AF.Copy), [p_], [ub[vg]])
                        if hf == 0:
                            op("dve", lambda e: e.tensor_copy(out=halo.ap[:, ch, :], in_=ub[vg].ap[:, UO + 1022:UO + 1024]), [ub[vg]], [halo])
                    for tt in range(2):
                        pcv = psb[4 + nxt("b1", 3)]
                        pcg = psb[4 + nxt("b1", 3)]
                        for vg, pc_ in ((1, pcg), (0, pcv)):
                            mm_group(pc_.ap, pc_, [(dg_.ap[:, 3 * vg + jj, :], ub[vg].ap[:, UO - 2 + 512 * tt + jj:UO - 2 + 512 * tt + jj + 512]) for jj in range(3)], [dg_, ub[vg]])
                        s_ = sgf[tt % 2]
                        op("act", lambda e: e.activation(out=s_.ap, in_=pcg.ap, func=AF.Silu, bias=V(V_FCB + 22 + j)), [pcg, vecs], [s_])
                        op("dve", lambda e: e.scalar_tensor_tensor(out=gT.ap[:, j, 512 * tt:512 * tt + 512], in0=pcv.ap, scalar=V(V_FCB + j), in1=s_.ap, op0=ALU.add, op1=ALU.mult),
                           [pcv, vecs, s_], [gT])
                if dbg and hf == 0:
                    dma("sp", dbg_d["d_gT"], gT.ap, reads=[gT])
                for t8 in range(8):
                    t = 8 * hf + t8
                    xr, ob_ = x1r[t8 % 2], ot[t8 % 2]
                    dma("sp", xr.ap, x1scr[128 * t:128 * t + 128, :], writes=[xr])
                    for cb in range(2):
                        p_ = psb[nxt("p4", 4)]
                        mm_group(p_.ap, p_, [(gT.ap[:, j, 128 * t8:128 * t8 + 128], wdn[j].ap[:, 512 * cb:512 * cb + 512]) for j in range(22)], [gT] + wdn)
                        op("dve", lambda e: e.tensor_tensor(out=ob_.ap[:, 512 * cb:512 * cb + 512], in0=p_.ap, in1=xr.ap[:, 512 * cb:512 * cb + 512], op=ALU.add), [p_, xr], [ob_])
                    dma("sp", outs[128 * t:128 * t + 128, :], ob_.ap, reads=[ob_])
            if s + 1 < nseq:
                pre["wv"] = [load_w(w_in[:, 4096 + 512 * cb:4096 + 512 * cb + 512]) for cb in range(2)]
            sc.barrier()
        sc.barrier(engines=("sp",))
    return nc


def _host_layout(inputs):
    f = lambda a: np.ascontiguousarray(np.asarray(a, dtype=np.float32))
    vecs = np.zeros((128, NV), np.float32)
    vecs[:, V_GATEB:V_GATEB + 16] = f(inputs["gate_b"][0]).reshape(16, 128).T
    vecs[:, V_CONVB:V_CONVB + 8] = f(inputs["conv_b"][0]).reshape(8, 128).T
    vecs[:, V_CNG:V_CNG + 8] = f(inputs["conv_norm_g"][0]).reshape(8, 128).T
    vecs[:, V_FCB:V_FCB + NFC] = f(inputs["ffn_conv_b"][0]).reshape(NFC, 128).T
    vecs[:, V_CW:V_CW + 248] = f(inputs["conv_w"][0]).reshape(31, 8, 128).transpose(2, 1, 0).reshape(128, 248)
    vecs[:, V_FCW:V_FCW + NFC * 3] = f(inputs["ffn_conv_w"][0]).reshape(3, NFC, 128).transpose(2, 1, 0).reshape(128, NFC * 3)
    vecs[:, V_QG] = np.tile(f(inputs["q_norm_g"][0]), 2)
    vecs[:, V_KG] = np.tile(f(inputs["k_norm_g"][0]), 2)
    rows = np.zeros((128, 2 * D), np.float32)
    rows[:, 0:D] = np.broadcast_to(f(inputs["norm1_g"][0])[None, :], (128, D))
    rows[:, D:] = np.broadcast_to(f(inputs["norm2_g"][0])[None, :], (128, D))
    cst = np.zeros((128, NCST), np.float32)
    cst[:, C_ID:C_ID + 128] = np.eye(128, dtype=np.float32)
    cst[0:64, C_BLK:C_BLK + 64] = 1.0
    cst[64:128, C_BLK + 64:C_BLK + 128] = 1.0
    k_i = np.arange(128)[:, None]
    q_i = np.arange(128)[None, :]
    cst[:, C_ST:C_ST + 128] = q_i + 128 - k_i
    cst[:, C_ST + 128:C_ST + 256] = np.maximum(q_i - k_i, 0)
    cst[:, C_MK:C_MK + 128] = (k_i >= q_i)
    cst[:, C_MK + 128:C_MK + 256] = (q_i >= k_i)
    shared = {
        "w_in": f(inputs["w_in"][0]), "w_conv_out": f(inputs["w_conv_out"][0]), "w_attn_out": f(inputs["w_attn_out"][0]),
        "w_out": f(inputs["w_out"][0]), "w_up": f(inputs["w_up"][0]), "w_down": f(inputs["w_down"][0]),
        "vecs": vecs, "rows": rows, "consts": cst,
    }
    return shared


def kernel(**inputs):
    x = np.asarray(inputs["x"], dtype=np.float32)
    B = x.shape[0]
    shared = _host_layout(inputs)
    nc = build_nc(SEQ_PER_CORE)
    in_maps = []
    for c in range(NCORES):
        m = dict(shared)
        m["x"] = np.ascontiguousarray(x[c * SEQ_PER_CORE:(c + 1) * SEQ_PER_CORE].reshape(SEQ_PER_CORE * S, D))
        in_maps.append(m)
    res = run_bass_kernel_spmd(nc, in_maps, core_ids=list(range(NCORES)))
    out = np.concatenate([np.asarray(r["out"], dtype=np.float32).reshape(SEQ_PER_CORE, S, D) for r in res.results], axis=0)
    return out
```
